# Optimizing a Trainium2 kernel written in Bass

```python
import jax, jax.numpy as jnp
from jax import lax
import numpy as np

D_MODEL = 1024
BATCH = 8
SEQ = 2048
DEPTH = 1
DEC_BATCH = 128
DEC_SEQ = 1
PAST_LEN = 16384
PAGE_SIZE = 128

MIX_W = D_MODEL
CONV_CH = MIX_W // 2
RW = MIX_W - CONV_CH
HEAD_DIM = 64
N_HEADS = RW // HEAD_DIM
CONV_K = 3
LORA_W = 64
LORA_A = 64
D_PLE = 256
SHIFT_W = 3 * RW + LORA_W + LORA_A
IN_W = 4 * CONV_CH + SHIFT_W + RW
RMS_EPS = 1e-6
GN_EPS = 64e-5

kernel_name = "hymba_conv_rwkv7_step"


def _rmsnorm(x, g):
    xf = x.astype(jnp.float32)
    y = xf * lax.rsqrt(jnp.mean(xf * xf, axis=-1, keepdims=True) + RMS_EPS)
    return (y * g.astype(jnp.float32)).astype(x.dtype)


def _wkv7(r, decay, k, v, kk, a, s0):
    seq = tuple(jnp.moveaxis(t, 1, 0) for t in (r, decay, k, v, kk, a))

    def step(s, inp):
        r_t, w_t, k_t, v_t, kk_t, a_t = inp
        sa = jnp.einsum('bhvk,bhk->bhv', s, -kk_t)
        s = (s * w_t[:, :, None, :]
             + sa[..., None] * (kk_t * a_t)[:, :, None, :]
             + v_t[..., None] * k_t[:, :, None, :])
        y = jnp.einsum('bhvk,bhk->bhv', s, r_t)
        return s, y

    s_T, ys = lax.scan(step, s0, seq)
    return jnp.moveaxis(ys, 0, 1), s_T


def _layer(h, p, conv_buf, shift_buf, wkv_state, norm_g, w_in, conv_w, mu_shift, w0, w_up,
           a0, a_up, k_k, k_a, r_k, ln_w, ln_b, w_out, w_pg, w_pp):
    bsz, t_len, _ = h.shape
    f32 = jnp.float32
    xn = _rmsnorm(h, norm_g)
    z = xn @ w_in
    zB, zC, zh, gA, zs, gR = jnp.split(
        z, [CONV_CH, 2 * CONV_CH, 3 * CONV_CH, 4 * CONV_CH, 4 * CONV_CH + SHIFT_W], axis=-1)

    u = zC * zh
    u_ext = jnp.concatenate([conv_buf.astype(u.dtype), u], axis=1)
    conv = (conv_w[0] * u_ext[:, :t_len] + conv_w[1] * u_ext[:, 1:t_len + 1]
            + conv_w[2] * u_ext[:, 2:t_len + 2])
    y_a = zB * conv * jax.nn.silu(gA)
    new_conv = u_ext[:, -(CONV_K - 1):]

    zs_prev = jnp.concatenate([shift_buf.astype(zs.dtype)[:, None], zs[:, :-1]], axis=1)
    zm = zs + (zs_prev - zs) * mu_shift
    new_shift = zs[:, -1]
    r, k, v, wd, ad = jnp.split(zm, [RW, 2 * RW, 3 * RW, 3 * RW + LORA_W], axis=-1)
    r, k, v, wd, ad = (t.astype(f32) for t in (r, k, v, wd, ad))
    wlog = -jax.nn.softplus(-(w0.astype(f32) + jnp.tanh(wd) @ w_up.astype(f32))) - 0.5
    decay = jnp.exp(-jnp.exp(wlog))
    a = jax.nn.sigmoid(a0.astype(f32) + ad @ a_up.astype(f32))
    kk = k * k_k.astype(f32)
    k = k * (1.0 + (a - 1.0) * k_a.astype(f32))

    def hd(t):
        return t.reshape(bsz, t_len, N_HEADS, HEAD_DIM)

    r, decay, k, v, kk, a = (hd(t) for t in (r, decay, k, v, kk, a))
    kk = kk / jnp.maximum(jnp.sqrt(jnp.sum(kk * kk, axis=-1, keepdims=True)), 1e-12)
    o, s_T = _wkv7(r, decay, k, v, kk, a, wkv_state.astype(f32))
    mean = jnp.mean(o, axis=-1, keepdims=True)
    var = jnp.mean(jnp.square(o - mean), axis=-1, keepdims=True)
    o = (o - mean) * lax.rsqrt(var + GN_EPS)
    o = o.reshape(bsz, t_len, RW) * ln_w.astype(f32) + ln_b.astype(f32)
    bonus = jnp.sum(r * k * r_k.astype(f32), axis=-1, keepdims=True) * v
    o = o + bonus.reshape(bsz, t_len, RW)
    y_r = o.astype(h.dtype) * jax.nn.silu(gR)

    h = h + jnp.concatenate([y_a, y_r], axis=-1) @ w_out
    h = h + jax.nn.sigmoid(h @ w_pg) * (p @ w_pp)
    return h, new_conv, new_shift, s_T


def setup_inputs(seed: int = 0) -> dict:
    key = jax.random.key(seed)
    ks = jax.random.split(key, 24)

    def nrm(k, shape, scale):
        return jax.random.normal(k, shape, jnp.float32) * scale

    return {
        "x_prompt": nrm(ks[0], (BATCH, SEQ, D_MODEL), 1.0),
        "x_sample": nrm(ks[1], (DEC_BATCH, DEC_SEQ, D_MODEL), 1.0),
        "p_prompt": nrm(ks[2], (DEPTH, BATCH, SEQ, D_PLE), 1.0),
        "p_sample": nrm(ks[3], (DEPTH, DEC_BATCH, DEC_SEQ, D_PLE), 1.0),
        "state_conv": nrm(ks[4], (DEPTH, DEC_BATCH, CONV_K - 1, CONV_CH), 1.0),
        "state_shift": nrm(ks[5], (DEPTH, DEC_BATCH, SHIFT_W), 1.0),
        "state_wkv": nrm(ks[6], (DEPTH, DEC_BATCH, N_HEADS, HEAD_DIM, HEAD_DIM), 0.3),
        "norm_g": 1.0 + nrm(ks[7], (DEPTH, D_MODEL), 0.02),
        "w_in": nrm(ks[8], (DEPTH, D_MODEL, IN_W), D_MODEL ** -0.5),
        "conv_w": nrm(ks[9], (DEPTH, CONV_K, CONV_CH), CONV_K ** -0.5),
        "mu_shift": jax.random.uniform(ks[10], (DEPTH, SHIFT_W), jnp.float32, 0.0, 1.0),
        "w0": jax.random.uniform(ks[11], (DEPTH, RW), jnp.float32, -4.0, -0.5),
        "w_up": nrm(ks[12], (DEPTH, LORA_W, RW), 0.5 * LORA_W ** -0.5),
        "a0": nrm(ks[13], (DEPTH, RW), 0.1),
        "a_up": nrm(ks[14], (DEPTH, LORA_A, RW), LORA_A ** -0.5),
        "k_k": 0.85 + nrm(ks[15], (DEPTH, RW), 0.05),
        "k_a": 1.0 + nrm(ks[16], (DEPTH, RW), 0.05),
        "r_k": nrm(ks[17], (DEPTH, N_HEADS, HEAD_DIM), 0.1),
        "ln_w": 1.0 + nrm(ks[18], (DEPTH, RW), 0.02),
        "ln_b": nrm(ks[19], (DEPTH, RW), 0.02),
        "w_out": nrm(ks[20], (DEPTH, MIX_W, D_MODEL), MIX_W ** -0.5),
        "w_pg": nrm(ks[21], (DEPTH, D_MODEL, D_MODEL), D_MODEL ** -0.5),
        "w_pp": nrm(ks[22], (DEPTH, D_PLE, D_MODEL), D_PLE ** -0.5),
        "final_g": 1.0 + nrm(ks[23], (D_MODEL,), 0.02),
    }


def reference(x_prompt, x_sample, p_prompt, p_sample, state_conv, state_shift, state_wkv,
              norm_g, w_in, conv_w, mu_shift, w0, w_up, a0, a_up, k_k, k_a, r_k, ln_w, ln_b,
              w_out, w_pg, w_pp, final_g):
    hp, hs = x_prompt, x_sample
    bp = x_prompt.shape[0]
    cps, sps, wps, css, sss, wss = [], [], [], [], [], []
    for i in range(DEPTH):
        wi = (norm_g[i], w_in[i], conv_w[i], mu_shift[i], w0[i], w_up[i], a0[i], a_up[i],
              k_k[i], k_a[i], r_k[i], ln_w[i], ln_b[i], w_out[i], w_pg[i], w_pp[i])
        zc = jnp.zeros((bp, CONV_K - 1, CONV_CH), hp.dtype)
        zsh = jnp.zeros((bp, SHIFT_W), hp.dtype)
        zw = jnp.zeros((bp, N_HEADS, HEAD_DIM, HEAD_DIM), jnp.float32)
        hp, cp, sp, wp = _layer(hp, p_prompt[i], zc, zsh, zw, *wi)
        hs, cs, ss, ws = _layer(hs, p_sample[i], state_conv[i], state_shift[i], state_wkv[i], *wi)
        cps.append(cp); sps.append(sp); wps.append(wp)
        css.append(cs); sss.append(ss); wss.append(ws)
    y_prompt = _rmsnorm(hp, final_g)
    y_sample = _rmsnorm(hs, final_g)
    return (y_prompt, y_sample, jnp.stack(cps), jnp.stack(sps), jnp.stack(wps),
            jnp.stack(css), jnp.stack(sss), jnp.stack(wss))
```

```python
import numpy as np
from contextlib import ExitStack
import concourse.bass as bass
import concourse.mybir as mybir
from concourse.bass_utils import run_bass_kernel_spmd

F32 = mybir.dt.float32
BF16 = mybir.dt.bfloat16
AF = mybir.ActivationFunctionType
ALU = mybir.AluOpType
AX = mybir.AxisListType

ENGS = ("tensor", "vector", "scalar", "gpsimd", "sync")
NCORES = 8
D = 1024
SEQ = 2048
NT = 256
NST = SEQ // NT
NS = 16
INW = 4224
CDEC = 0.6065306597126334
RMS_EPS = 1e-6
GN_EPS = 64e-5

G_, MU_, CW_, W0_, A0_, KK_, KA_, RK_, LW_, LB_ = 0, 8, 21, 33, 37, 41, 45, 49, 53, 57
NPRM_IN = 61
OMU_, OMKA_, GNE_, RME_ = 64, 77, 94, 95
NPRM = 96

DMA_SEMS = ["d_w1", "d_w2", "d_w3", "d_w4", "d_w5", "d_c1", "d_c2", "d_c3", "d_c4", "d_x", "d_g", "d_p", "d_y",
            "d_o", "d_s1", "d_s2", "d_l", "d_ws", "d_s3", "d_s4"]


class Sched:
    def __init__(self):
        self.ops = {e: [] for e in ENGS}
        self.count = {e: 0 for e in ENGS}
        self.dcount = {}
        self.know = {e: {} for e in ENGS}
        self.clock = {}
        self.last_write = {}
        self.readers = {}
        self.bank_tokens = {}
        self.bank_pending = {}
        self.all_events = {}
        self.nwaits = 0
        self.pending_nosig = {e: False for e in ENGS}

    def _need(self, eng, ev, waits):
        if ev is None:
            return
        s, v = ev
        if s == "tensor" and eng == "tensor":
            return
        if self.know[eng].get(s, 0) >= v:
            return
        if waits.get(s, 0) < v:
            waits[s] = v

    def _absorb(self, eng, waits):
        items = sorted(waits.items(), key=lambda kv: -len(self.clock.get(kv, ())))
        k = self.know[eng]
        kept = []
        for s, v in items:
            if k.get(s, 0) >= v:
                continue
            kept.append((s, v))
            for s2, v2 in self.clock.get((s, v), {}).items():
                if k.get(s2, 0) < v2:
                    k[s2] = v2
            if k.get(s, 0) < v:
                k[s] = v
        self.nwaits += len(kept)
        return kept

    def retire_bank(self, b):
        evs = []
        for t in self.bank_tokens.get(b, ()):
            if t in self.last_write:
                evs.append(self.last_write.pop(t))
            evs.extend(self.readers.pop(t, ()))
        self.bank_tokens[b] = set()
        self.bank_pending[b] = evs

    def _touch(self, t):
        if isinstance(t, tuple) and t and t[0] == "ps":
            b = t[1]
            if t not in self.bank_tokens.setdefault(b, set()):
                self.bank_tokens[b].add(t)
                if t not in self.last_write and t not in self.readers:
                    self.readers[t] = list(self.bank_pending.get(b, ()))

    def op(self, eng, fn, reads=(), writes=(), dma=None, sig=True):
        waits = {}
        for t in list(reads) + list(writes):
            self._touch(t)
        for r in reads:
            self._need(eng, self.last_write.get(r), waits)
            if isinstance(r, tuple) and r and r[0] == "ps":
                for u in self.bank_tokens.get(r[1], ()):
                    if u != r:
                        self._need(eng, self.last_write.get(u), waits)
                    for ev in self.readers.get(u, ()):
                        if ev[0] != eng:
                            self._need(eng, ev, waits)
        for w in writes:
            if isinstance(w, tuple) and w and w[0] == "ps":
                for u in self.bank_tokens.get(w[1], ()):
                    if u != w:
                        for ev in self.readers.get(u, ()):
                            self._need(eng, ev, waits)
        for w in writes:
            self._need(eng, self.last_write.get(w), waits)
            for ev in self.readers.get(w, ()):
                self._need(eng, ev, waits)
        kept = self._absorb(eng, waits)
        if dma is None and not sig:
            ev = (eng, self.count[eng] + 1)
            inc = None
            self.pending_nosig[eng] = True
        elif dma is None:
            self.count[eng] += 1
            ev = (eng, self.count[eng])
            inc = (eng, 1)
            self.pending_nosig[eng] = False
        else:
            self.dcount[dma] = self.dcount.get(dma, 0) + 16
            ev = (dma, self.dcount[dma])
            inc = (dma, 16)
        ck = dict(self.know[eng])
        ck[ev[0]] = ev[1]
        self.clock[ev] = ck
        self.all_events[ev[0]] = max(self.all_events.get(ev[0], 0), ev[1])
        for w in writes:
            self.last_write[w] = ev
            self.readers[w] = []
        for r in reads:
            if r not in writes:
                self.readers.setdefault(r, []).append(ev)
        self.ops[eng].append((kept, fn, inc))
        return ev

    def barrier(self, engs=ENGS):
        assert not any(self.pending_nosig.values()), "non-signalling op not followed by a signalling one"
        evs = list(self.all_events.items())
        for eng in engs:
            waits = {}
            for ev in evs:
                self._need(eng, ev, waits)
            kept = self._absorb(eng, waits)
            self.ops[eng].append((kept, None, None))

    def replay(self, eng, e, sems):
        for waits, fn, inc in self.ops[eng]:
            for s, v in waits:
                e.wait_ge(sems[s], v)
            if fn is not None:
                if inc is None:
                    fn(e)
                else:
                    fn(e).then_inc(sems[inc[0]], inc[1])


def build_nc(n_st=NST, do_sample=True):
    nc = bass.Bass("TRN2", target_bir_lowering=False)

    def din(name, shape, dt=F32):
        return nc.dram_tensor(name, list(shape), dt, kind="ExternalInput").ap()

    def dout(name, shape, dt=F32):
        return nc.dram_tensor(name, list(shape), dt, kind="ExternalOutput").ap()

    x_d = din("x", [SEQ, D]); p_d = din("p", [SEQ, 256])
    xs_d = din("xs", [NS, D]); psm_d = din("psm", [NS, 256])
    scT_d = din("scT", [128, 4, 2, NS])
    ssT_d = din("ssT", [128, 13, NS])
    swkv_d = din("swkv", [128, 4096])
    prm_d = din("prm", [128, NPRM_IN])
    win_d = din("w_in", [D, INW]); wup_d = din("w_up", [64, 512]); aup_d = din("a_up", [64, 512])
    wout_d = din("w_out", [D, D]); wpg_d = din("w_pg", [D, D]); wpp_d = din("w_pp", [256, D])
    fg_d = din("final_g", [1, D])

    y_d = dout("y", [SEQ, D]); ys_d = dout("ys", [NS, D])
    convp_d = dout("convp", [128, 4, 2])
    shiftp_d = dout("shiftp", [128, 13])
    wkvp_d = dout("wkvp", [8, 64, 64])
    convs_d = dout("convs", [128, 4, 2, NS])
    shifts_d = dout("shifts", [128, 13, NS])
    wkvs_d = dout("wkvs", [128, 4096])
    scr1_d = nc.dram_tensor("scr1", [NS, 8, 6, 64], F32, kind="Internal").ap()
    scr2_d = nc.dram_tensor("scr2", [NS, 8, 64], F32, kind="Internal").ap()
    ym_d = nc.dram_tensor("scr_ym", [128, 8, SEQ + 128], BF16, kind="Internal").ap()

    S = Sched()
    es = ExitStack()

    def sb(name, shape, dt=F32):
        return es.enter_context(nc.sbuf_tensor("s_" + name, list(shape), dt))

    win = sb("win", [128, 8, INW], BF16)
    wua = sb("wua", [128, 512], BF16)
    prm = sb("prm", [128, NPRM])
    ident = sb("ident", [128, 128], BF16)
    identf = sb("identf", [128, 128])
    blk = sb("blk", [128, 128])
    blkb = sb("blkb", [128, 128], BF16)
    blk64 = sb("blk64", [128, 128], BF16)
    blkrk = sb("blkrk", [128, 4, 128], BF16)
    maskA = sb("maskA", [128, 256], BF16)
    maskN = sb("maskN", [128, 128], BF16)
    scanm = sb("scanm", [128, NT])
    zlast = sb("zlast", [128, 13])
    uext = sb("uext", [128, 4, NT + 2])
    T32 = sb("T32", [128, 4, 128])
    T16 = sb("T16", [128, 4, 128], BF16)
    pcs = sb("pcs", [128, 4, 4])
    xnTf = [sb("xnTa", [128, 8 * (NT + 2)], BF16), sb("xnTb", [128, 8 * (NT + 2)], BF16)]
    xnT2 = [t_[:, :].rearrange("p (c n) -> p c n", c=8) for t_ in xnTf]
    CUR = {"x": 0}
    ymixT = sb("ymixT", [128, 8, NT], BF16)
    stat = sb("stat", [128, 32])
    thad = sb("thad", [128, NT], BF16)

    class TH:
        pass

    def mk_thread(g, ring, ybank):
        B = TH()
        B.g = g
        B.arena = sb(f"arena{g}", [128, 20, NT])
        B.xb = sb(f"xb{g}", [128, D], BF16)
        B.KR = sb(f"KR{g}", [128, 2, 2, NT], BF16)
        B.KB = sb(f"KB{g}", [128, 2, 2, NT], BF16)
        B.VT = sb(f"VT{g}", [128, 2, NT], BF16)
        B.bv = sb(f"bv{g}", [128, 2, NT])
        B.TK = sb(f"TK{g}", [128, 2, 4, 256], BF16)
        B.SC = sb(f"SC{g}", [128, 2, 4, 512], BF16)
        B.D7 = sb(f"D7{g}", [128, 7, 4, 128], BF16)
        B.dA = [B.D7[:, 0], B.D7[:, 1]]
        B.dB = [B.D7[:, 2], B.D7[:, 3]]
        B.dP = [B.D7[:, 4], B.D7[:, 5]]
        B.NB = B.D7[:, 6]
        B.W1 = sb(f"W1{g}", [128, 4, 64], BF16)
        B.U1b = sb(f"U1b{g}", [128, 2, 256], BF16)
        B.KPT = sb(f"KPT{g}", [128, 2, 2, 128], BF16)
        B.Y1T = sb(f"Y1T{g}", [128, 2, 2, 128])
        B.H1 = sb(f"H1{g}", [128, 8, 128])
        B.U0b = sb(f"U0b{g}", [128, 256], BF16)
        B.Tsum = sb(f"Tsum{g}", [128, 2, 128])
        B.ring = ring
        B.rs = {"ri": 0}
        B.ybank = ybank
        B.pbank = g
        return B

    psb = [es.enter_context(nc.psum_tensor(f"psb{i}", [128, 512], F32)) for i in range(6)]
    ptbs = [es.enter_context(nc.psum_tensor(f"ptb16_{i}", [128, 1024], BF16)) for i in range(2)]
    TH0 = mk_thread(0, [0, 1], 2)
    TH1 = mk_thread(1, [3, 4], 5)
    THS = [TH0, TH1]
    import copy as _copy
    CTH = []
    for B_ in THS:
        B_.ring4 = list(B_.ring[0:2]) + [B_.ybank, 6 + B_.g]
        B_.rs4 = {"ri": 0}
        Bc = _copy.copy(B_)
        Bc.ring = B_.ring4
        Bc.rs = B_.rs4
        CTH.append(Bc)
        Bw = _copy.copy(B_)
        Bw.ring = list(B_.ring[0:2]) + [B_.ybank, 6 + B_.g]
        Bw.rs = {"ri": 0}
        B_.wide = Bw
        Bn = _copy.copy(B_)
        Bn.ring = [B_.ring[0]]
        Bn.rs = {"ri": 0}
        B_.chain = Bn
        B_.ring3 = list(B_.ring[0:2]) + [6 + B_.g]
    WC = _copy.copy(TH0)
    WC.ring = [TH0.ring[1]]
    WC.rs = {"ri": 0}

    sem_names = list(ENGS) + DMA_SEMS + ["d_x1", "d_g1", "d_p1", "d_y1", "d_m0", "d_m1", "d_ym", "d_g2", "d_g3", "d_p2", "d_p3", "d_y2", "d_y3", "d_m2", "d_m3", "d_l2", "d_ws2", "d_s1b"] + [f"d_stg{i}" for i in range(6)]
    sems = {n: es.enter_context(nc.semaphore(n)) for n in sem_names}

    def TE(fn, r, w, sig=True): return S.op("tensor", fn, r, w, sig=sig)
    def VE(fn, r, w): return S.op("vector", fn, r, w)
    def AE(fn, r, w): return S.op("scalar", fn, r, w)
    def GE(fn, r, w): return S.op("gpsimd", fn, r, w)
    def DMA(fn, r, w, sem, q="sync"): return S.op(q, fn, r, w, dma=sem)

    gen = {"g": 0}

    oring = {"ri": 0}

    ptf32 = [pt_[:, :].bitcast(F32) for pt_ in ptbs]

    def psalloc(B):
        if getattr(B, "gring", False):
            b = oring["ri"] % 6
            oring["ri"] += 1
        else:
            b = B.ring[B.rs["ri"] % len(B.ring)]
            B.rs["ri"] += 1
        gen["g"] += 1
        S.retire_bank(b)
        return (psb[b] if b < 6 else ptf32[b - 6]), ("ps", b, gen["g"])

    def psy(B):
        gen["g"] += 1
        S.retire_bank(B.ybank)
        return psb[B.ybank], ("ps", B.ybank, gen["g"])

    def ptalloc(B):
        gen["g"] += 1
        vb = 6 + B.pbank
        S.retire_bank(vb)
        return ptbs[B.pbank], ("ps", vb, gen["g"])

    def pcol(c):
        return prm[:, c:c + 1]

    def ar(B, k, n=NT):
        return B.arena[:, k, 0:n]

    def at(B, k):
        return ("ar", B.g, k)

    def arb(B, k, n=NT):
        return B.arena[:, k, :].bitcast(BF16)[:, 0:n]

    def tk(B, *name):
        return (B.g,) + name

    cast_rr = {"i": 0}

    def load_cast(chunks, slots, tag):
        for idx, ch in enumerate(chunks):
            dst, src, dtok = ch[0], ch[1], ch[2]
            scl = ch[3] if len(ch) > 3 else None
            k = idx % len(slots)
            sap, stok = slots[k]
            W = dst.shape[-1]
            DMA(lambda e, sap=sap, src=src, W=W: e.dma_start(out=sap[:, 0:W], in_=src), [], [stok], f"d_stg{k}")
            if scl is not None:
                if cast_rr["i"] % 2 == 0:
                    AE(lambda e, dst=dst, sap=sap, W=W, scl=scl: e.mul(out=dst, in_=sap[:, 0:W], mul=scl),
                       [stok, "prm"], [dtok])
                else:
                    VE(lambda e, dst=dst, sap=sap, W=W, scl=scl: e.tensor_scalar(out=dst, in0=sap[:, 0:W], scalar1=scl,
                                                                                 scalar2=None, op0=ALU.mult),
                       [stok, "prm"], [dtok])
            elif cast_rr["i"] % 2 == 0:
                AE(lambda e, dst=dst, sap=sap, W=W: e.copy(out=dst, in_=sap[:, 0:W]), [stok], [dtok])
            else:
                VE(lambda e, dst=dst, sap=sap, W=W: e.tensor_copy(out=dst, in_=sap[:, 0:W]), [stok], [dtok])
            cast_rr["i"] += 1

    DMA(lambda e: e.dma_start(out=prm[:, 0:NPRM_IN], in_=prm_d), [], ["prm"], "d_c1")

    tmpc = TH1.arena[:, 15, :]
    GE(lambda e: e.memset(identf[:], 1.0), [], ["identf"])
    GE(lambda e: e.affine_select(out=identf[:], in_=identf[:], pattern=[[-1, 128]], compare_op=ALU.is_equal,
                                 fill=0.0, base=0, channel_multiplier=1), ["identf"], ["identf"])
    VE(lambda e: e.tensor_copy(out=ident[:], in_=identf[:]), ["identf"], ["ident"])
    GE(lambda e: e.memset(blk[:], 0.0), [], ["blk"])
    GE(lambda e: e.memset(blk[0:64, 0:64], 1.0), ["blk"], ["blk"])
    GE(lambda e: e.memset(blk[64:128, 64:128], 1.0), ["blk"], ["blk"])
    VE(lambda e: e.tensor_scalar(out=blk64[:], in0=blk[:], scalar1=1.0 / 64, scalar2=None, op0=ALU.mult),
       ["blk"], ["blk64"])
    VE(lambda e: e.tensor_copy(out=blkb[:], in_=blk[:]), ["blk"], ["blkb"])
    GE(lambda e: e.memset(prm[:, GNE_:GNE_ + 1], GN_EPS), ["prm"], ["prm"])
    GE(lambda e: e.memset(prm[:, RME_:RME_ + 1], RMS_EPS), ["prm"], ["prm"])
    for j in range(4):
        VE(lambda e, j=j: e.tensor_scalar(out=blkrk[:, j, :], in0=blk[:], scalar1=pcol(RK_ + j), scalar2=None,
                                          op0=ALU.mult), ["blk", "prm"], ["blkrk"])
    GE(lambda e: e.memset(tmpc[:], 1.0), [], ["tmpc"])
    GE(lambda e: e.affine_select(out=tmpc[:, 0:128], in_=tmpc[:, 0:128], pattern=[[1, 128]], compare_op=ALU.is_ge,
                                 fill=0.0, base=-1, channel_multiplier=-1), ["tmpc"], ["tmpc"])
    GE(lambda e: e.affine_select(out=tmpc[:, 128:256], in_=tmpc[:, 128:256], pattern=[[1, 128]],
                                 compare_op=ALU.is_ge, fill=0.0, base=0, channel_multiplier=-1),
       ["tmpc"], ["tmpc"])
    GE(lambda e: e.memset(tmpc[0:64, 64:128], 0.0), ["tmpc"], ["tmpc"])
    GE(lambda e: e.memset(tmpc[0:64, 192:256], 0.0), ["tmpc"], ["tmpc"])
    VE(lambda e: e.tensor_copy(out=maskA[:], in_=tmpc[:]), ["tmpc"], ["maskA"])
    GE(lambda e: e.memset(tmpc[:, 0:128], 1.0), ["tmpc"], ["tmpc"])
    GE(lambda e: e.affine_select(out=tmpc[:, 0:128], in_=tmpc[:, 0:128], pattern=[[-1, 128]],
                                 compare_op=ALU.is_ge, fill=0.0, base=-1, channel_multiplier=1),
       ["tmpc"], ["tmpc"])
    GE(lambda e: e.memset(tmpc[64:128, 0:64], 0.0), ["tmpc"], ["tmpc"])
    VE(lambda e: e.tensor_copy(out=maskN[:], in_=tmpc[:, 0:128]), ["tmpc"], ["maskN"])
    GE(lambda e: e.memset(scanm[:], 1.0), [], ["scanm"])
    GE(lambda e: e.memset(scanm[:].rearrange("p (c t) -> p c t", t=64)[:, :, 0:1], 0.0), ["scanm"], ["scanm"])
    VE(lambda e: e.tensor_scalar(out=prm[:, OMU_:OMU_ + 13], in0=prm[:, MU_:MU_ + 13], scalar1=-1.0, scalar2=1.0,
                                 op0=ALU.mult, op1=ALU.add), ["prm"], ["prm"])
    VE(lambda e: e.tensor_scalar(out=prm[:, OMKA_:OMKA_ + 4], in0=prm[:, KA_:KA_ + 4], scalar1=-1.0, scalar2=1.0,
                                 op0=ALU.mult, op1=ALU.add), ["prm"], ["prm"])
    GE(lambda e: e.memset(zlast[:], 0.0), [], [("zl", q) for q in range(13)])
    GE(lambda e: e.memset(uext[:], 0.0), [], [("u", j) for j in range(4)])
    GE(lambda e: e.memset(T32[:], 0.0), [], [("T", j) for j in range(4)])
    GE(lambda e: e.memset(T16[:], 0.0), [], [("T16", j) for j in range(4)])
    DMA(lambda e: e.dma_start(out=wua[0:64, :], in_=wup_d), [], ["wua"], "d_w2", q="gpsimd")
    DMA(lambda e: e.dma_start(out=wua[64:128, :], in_=aup_d), [], ["wua"], "d_w2", q="gpsimd")
    S.barrier(("gpsimd", "vector", "scalar"))
    stg = []
    for g_ in range(2):
        fl = THS[g_].arena[:].rearrange("p a b -> p (a b)")
        for k_ in range(3):
            stg.append((fl[:, k_ * 1056:(k_ + 1) * 1056], ("stg", g_, k_)))
    chunks = []
    for cb in range(0, INW, 1056):
        for c in range(8):
            chunks.append((win[:, c, cb:cb + 1056], win_d[c * 128:(c + 1) * 128, cb:cb + 1056], "win",
                           pcol(G_ + c)))
    load_cast(chunks, stg, "win")
    S.barrier()

    def p1_gen(Bs, xsrc, tok0s, rows, col0s, par, d7stage=False, carry=None):
        if d7stage:
            xts = {B.g: B.D7[:, 0:4].rearrange("p a b c -> p (a b c)").bitcast(F32) for B in Bs}
            xtt = {B.g: [tk(B, "dA0"), tk(B, "dA1"), tk(B, "dB0"), tk(B, "dB1")] for B in Bs}
        else:
            xts = {B.g: B.arena[:, 12:16, :].rearrange("p a b -> p (a b)") for B in Bs}
            xtt = {B.g: [at(B, 12), at(B, 13), at(B, 14), at(B, 15)] for B in Bs}
        xdst = xnT2[par]
        for B, tok0 in zip(Bs, tok0s):
            DMA(lambda e, B=B, tok0=tok0: e.dma_start(out=xts[B.g][0:rows, :], in_=xsrc[tok0:tok0 + rows, :]),
                [], xtt[B.g], f"d_x{B.g}" if B.g else "d_x")
        yield
        for B in Bs:
            so = 8 * B.g
            AE(lambda e, B=B, so=so: e.activation(out=B.xb[0:rows, :], in_=xts[B.g][0:rows, :], func=AF.Square,
                                                  accum_out=stat[0:rows, so:so + 1]),
               xtt[B.g], [tk(B, "xb"), tk(B, "st0")])
        yield
        for B in Bs:
            so = 8 * B.g
            AE(lambda e, so=so: e.activation(out=stat[0:rows, so + 1:so + 2], in_=stat[0:rows, so:so + 1],
                                             func=AF.Ln, scale=1.0 / D, bias=prm[0:rows, RME_:RME_ + 1]),
               [tk(B, "st0"), "prm"], [tk(B, "st1")])
        for B in Bs:
            so = 8 * B.g
            AE(lambda e, so=so: e.activation(out=stat[0:rows, so + 2:so + 3], in_=stat[0:rows, so + 1:so + 2],
                                             func=AF.Exp, scale=-0.5), [tk(B, "st1")], [tk(B, "st2")])
        yield
        for B in Bs:
            so = 8 * B.g
            VE(lambda e, B=B, so=so: e.tensor_scalar(out=B.xb[0:rows, :], in0=xts[B.g][0:rows, :],
                                                     scalar1=stat[0:rows, so + 2:so + 3], scalar2=None, op0=ALU.mult),
               xtt[B.g] + [tk(B, "st2"), tk(B, "xb")], [tk(B, "xb")])
        yield
        pts = {}
        for B in Bs:
            pt, tk_ = ptalloc(B)
            t = tk_ + ("x",)
            pts[B.g] = (pt, t)
            for c in range(8):
                TE(lambda e, c=c, pt=pt, B=B: e.transpose(pt[:, c * 128:c * 128 + rows],
                                                          B.xb[0:rows, c * 128:(c + 1) * 128],
                                                          ident[0:rows, 0:rows]), [tk(B, "xb"), "ident"], [t], sig=(c == 7))
        yield
        for B, col0 in zip(Bs, col0s):
            pt, t = pts[B.g]
            AE(lambda e, pt=pt, col0=col0: e.copy(out=xdst[:, :, 1 + col0:1 + col0 + rows],
                                                  in_=pt[:, :].rearrange("p (c t) -> p c t", t=128)[:, :, 0:rows]),
               [t], [("xnT", par, B.g)])
            yield
        if carry == "zero":
            GE(lambda e: e.memset(xdst[:, :, 0:1], 0.0), [], [("xnT", par, "c")])
        elif carry == "prev":
            GE(lambda e: e.tensor_copy(out=xdst[:, :, 0:1], in_=xnT2[1 - par][:, :, NT:NT + 1]),
               [("xnT", 1 - par, 1)], [("xnT", par, "c")])
        yield

    def load_norm_T(Bs, xsrc, tok0s, rows, col0s, par=0, carry=None):
        for _ in p1_gen(Bs, xsrc, tok0s, rows, col0s, par, carry=carry):
            pass

    def projx(B, m, n, par=None, shift=False):
        if par is None:
            par = CUR["x"]
        xa = xnT2[par]
        xnt = [("xnT", par, 0), ("xnT", par, 1)]
        c0, nn = (0, n + 1) if shift else (1, n)
        if shift:
            xnt = xnt + [("xnT", par, "c")]
        pz, tok = psalloc(B)
        t = tok + ("z",)
        for c in range(8):
            TE(lambda e, c=c, pz=pz, xa=xa: e.matmul(pz[:, 0:nn], lhsT=win[:, c, m * 128:(m + 1) * 128],
                                                     rhs=xa[:, c, c0:c0 + nn], start=(c == 0), stop=(c == 7)),
               ["win"] + xnt, [t], sig=(c == 7))
        return pz, t

    def conv_gen(Bs, jf, n, sample, scT=None, convo=None, slots=(0, 1, 2)):
        zc, cv, sl = slots
        P = {}
        for B in Bs:
            P[B.g] = projx(B, 4 + jf(B), n)
        yield
        for B in Bs:
            pz, t = P[B.g]
            AE(lambda e, pz=pz, B=B: e.copy(out=ar(B, zc, n), in_=pz[:, 0:n]), [t], [at(B, zc)])
        yield
        for B in Bs:
            P[B.g] = projx(B, 8 + jf(B), n)
        yield
        if not sample:
            for B in Bs:
                pz, t = P[B.g]
                j = jf(B)
                VE(lambda e, pz=pz, j=j, B=B: e.tensor_tensor(out=uext[:, j, 2:2 + n], in0=ar(B, zc, n), in1=pz[:, 0:n],
                                                              op=ALU.mult), [t, at(B, zc)], [("u", j)])
            yield
            for B in Bs:
                j = jf(B)
                VE(lambda e, j=j, B=B: e.tensor_scalar(out=ar(B, cv, n), in0=uext[:, j, 0:n], scalar1=pcol(CW_ + j),
                                                       scalar2=None, op0=ALU.mult), [("u", j), "prm"], [at(B, cv)])
            yield
            for tap in (1, 2):
                for B in Bs:
                    j = jf(B)
                    VE(lambda e, j=j, B=B, tap=tap: e.scalar_tensor_tensor(
                        out=ar(B, cv, n), in0=uext[:, j, tap:n + tap], scalar=pcol(CW_ + 4 * tap + j),
                        in1=ar(B, cv, n), op0=ALU.mult, op1=ALU.add), [("u", j), "prm", at(B, cv)], [at(B, cv)])
                yield
            for B in Bs:
                j = jf(B)
                GE(lambda e, j=j: e.tensor_copy(out=uext[:, j, 0:2], in_=uext[:, j, n:n + 2]), [("u", j)], [("u", j)])
            yield
        else:
            for B in Bs:
                pz, t = P[B.g]
                j = jf(B)
                VE(lambda e, pz=pz, j=j, B=B: e.tensor_tensor(out=convo[:, j, 1, :], in0=ar(B, zc, n), in1=pz[:, 0:n],
                                                              op=ALU.mult), [t, at(B, zc)], [("cvo", j)])
                GE(lambda e, j=j: e.tensor_copy(out=convo[:, j, 0, :], in_=scT[:, j, 1, :]), ["scT"], [("cvo0", j)])
                VE(lambda e, j=j, B=B: e.tensor_scalar(out=ar(B, cv, n), in0=scT[:, j, 0, :], scalar1=pcol(CW_ + j),
                                                       scalar2=None, op0=ALU.mult), ["scT", "prm"], [at(B, cv)])
                VE(lambda e, j=j, B=B: e.scalar_tensor_tensor(out=ar(B, cv, n), in0=scT[:, j, 1, :],
                                                              scalar=pcol(CW_ + 4 + j), in1=ar(B, cv, n),
                                                              op0=ALU.mult, op1=ALU.add),
                   ["scT", "prm", at(B, cv)], [at(B, cv)])
                VE(lambda e, j=j, B=B: e.scalar_tensor_tensor(out=ar(B, cv, n), in0=convo[:, j, 1, :],
                                                              scalar=pcol(CW_ + 8 + j), in1=ar(B, cv, n),
                                                              op0=ALU.mult, op1=ALU.add),
                   [("cvo", j), "prm", at(B, cv)], [at(B, cv)])
        for B in Bs:
            P[B.g] = projx(B, jf(B), n)
        yield
        for B in Bs:
            pz, t = P[B.g]
            VE(lambda e, pz=pz, B=B: e.tensor_tensor(out=ar(B, cv, n), in0=ar(B, cv, n), in1=pz[:, 0:n], op=ALU.mult),
               [t, at(B, cv)], [at(B, cv)])
        yield
        for B in Bs:
            P[B.g] = projx(B, 12 + jf(B), n)
        yield
        for B in Bs:
            pz, t = P[B.g]
            AE(lambda e, pz=pz, B=B: e.activation(out=ar(B, sl, n), in_=pz[:, 0:n], func=AF.Sigmoid), [t], [at(B, sl)])
        yield
        for B in Bs:
            pz, t = P[B.g]
            VE(lambda e, pz=pz, B=B: e.tensor_tensor(out=ar(B, cv, n), in0=ar(B, cv, n), in1=pz[:, 0:n], op=ALU.mult),
               [t, at(B, cv)], [at(B, cv)])
        yield
        for B in Bs:
            j = jf(B)
            GE(lambda e, j=j, B=B: e.tensor_tensor(out=ymixT[:, j, 0:n], in0=ar(B, cv, n), in1=ar(B, sl, n),
                                                   op=ALU.mult), [at(B, cv), at(B, sl)], [("ym", j)])
        yield

    def conv_branch(Bs, jf, n, sample, scT=None, convo=None):
        for _ in conv_gen(Bs, jf, n, sample, scT, convo):
            pass

    FILL = {"gen": None, "ok": False, "busy": False}

    def fill():
        if FILL["ok"] and FILL["gen"] is not None and not FILL["busy"]:
            FILL["busy"] = True
            try:
                next(FILL["gen"])
                if FILL.get("k2"):
                    next(FILL["gen"])
            except StopIteration:
                FILL["gen"] = None
            FILL["busy"] = False

    def flush_fill():
        if FILL["gen"] is not None:
            FILL["busy"] = True
            for _ in FILL["gen"]:
                pass
            FILL["busy"] = False
            FILL["gen"] = None

    def ck(C):
        return (C.g, getattr(C, "sb", 0))

    def shift_chunk(Cs, qf, n, dkf, sample, ssT=None, zso=None, par=None, save_last=False):
        P = {}
        o1 = 0 if sample else 1
        for C in Cs:
            P[ck(C)] = projx(C, 16 + qf(C), n, par, shift=not sample)
        for C in Cs:
            pz, t = P[ck(C)]
            q, dk = qf(C), dkf(C)
            AE(lambda e, pz=pz, C=C, q=q, dk=dk: e.mul(out=ar(C, dk, n), in_=pz[:, o1:o1 + n], mul=pcol(OMU_ + q)),
               [t, "prm"], [at(C, dk)])
        if not sample:
            for C in Cs:
                pz, t = P[ck(C)]
                q, dk = qf(C), dkf(C)
                VE(lambda e, pz=pz, C=C, q=q, dk=dk: e.scalar_tensor_tensor(
                    out=ar(C, dk, n), in0=pz[:, 0:n], scalar=pcol(MU_ + q), in1=ar(C, dk, n),
                    op0=ALU.mult, op1=ALU.add), [t, "prm", at(C, dk)], [at(C, dk)])
            if save_last:
                for C in Cs:
                    pz, t = P[ck(C)]
                    q, dk = qf(C), dkf(C)
                    AE(lambda e, pz=pz, q=q: e.copy(out=zlast[:, q:q + 1], in_=pz[:, n:n + 1]), [t, at(C, dk)],
                       [("zl", q)])
            fill()
            fill()
        else:
            for C in Cs:
                pz, t = P[ck(C)]
                q, dk = qf(C), dkf(C)
                VE(lambda e, C=C, q=q, dk=dk: e.scalar_tensor_tensor(out=ar(C, dk, n), in0=ssT[:, q, :],
                                                                     scalar=pcol(MU_ + q), in1=ar(C, dk, n),
                                                                     op0=ALU.mult, op1=ALU.add),
                   ["ssT", "prm", at(C, dk)], [at(C, dk)])
                AE(lambda e, pz=pz, q=q: e.copy(out=zso[:, q, :], in_=pz[:, 0:n]), [t], [("zso", q)])

    WD = 19

    def thad_ops(B, n):
        AE(lambda e: e.activation(out=thad[0:64, 0:n], in_=ar(B, WD, n)[0:64, :], func=AF.Tanh), [at(B, WD)], ["thad"])
        GE(lambda e: e.tensor_copy(out=thad[64:128, 0:n], in_=ar(B, WD, n)[64:128, :]), [at(B, WD), "thad"], ["thad"])

    def wdad_chunk(B, n, sample, ssT=None, zso=None, par=None, save_last=False, defer=False):
        shift_chunk([B], lambda C: 12, n, lambda C: WD, sample, ssT, zso, par, save_last)
        if not defer:
            thad_ops(B, n)

    ZR, ZK, ZV, SG, CSG, PIN, AA, S0, S1 = 0, 1, 2, 3, 4, 5, 6, 7, 8
    CEX = PEX = SG
    PINV = CSG
    LASTF = {"v": False}

    def sl(C, k):
        return getattr(C, "sb", 0) + k

    def pair_pre(Cs, jf, n, sample, bvf, ssT=None, zso=None):
        FILL["ok"] = True
        sv = LASTF["v"] and not sample
        shift_chunk(Cs, lambda C: 4 + jf(C), n, lambda C: sl(C, ZK), sample, ssT, zso, save_last=sv)
        P = {}

        def A(C, k):
            return ar(C, sl(C, k), n)

        def Ab(C, k):
            return arb(C, sl(C, k), n)

        def T(C, k):
            return at(C, sl(C, k))

        def each(f):
            for C in Cs:
                f(C, jf(C))
            fill()

        def mm1(C, j, lhs, rhs, rtok, nm):
            pz, t = psalloc(C)
            t = t + (nm,)
            TE(lambda e, pz=pz: e.matmul(pz[:, 0:n], lhsT=lhs, rhs=rhs, start=True, stop=True), rtok, [t])
            P[ck(C)] = (pz, t)

        each(lambda C, j: GE(lambda e: e.tensor_scalar(out=A(C, S0), in0=A(C, ZK), scalar1=pcol(KK_ + j),
                                                       scalar2=0.0, op0=ALU.mult, op1=ALU.add),
                             [T(C, ZK), "prm"], [T(C, S0)]))
        each(lambda C, j: AE(lambda e: e.activation(out=Ab(C, S1), in_=A(C, ZK), func=AF.Square,
                                                    scale=pcol(KK_ + j)), [T(C, ZK), "prm"], [T(C, S1)]))
        each(lambda C, j: mm1(C, j, blkb[:], Ab(C, S1), ["blkb", T(C, S1)], "n"))
        each(lambda C, j: VE(lambda e, pz=P[ck(C)][0]: e.tensor_scalar(out=A(C, S1), in0=pz[:, 0:n], scalar1=1e-24,
                                                                       scalar2=None, op0=ALU.max),
                             [P[ck(C)][1], T(C, S1)], [T(C, S1)]))
        shift_chunk(Cs, lambda C: jf(C), n, lambda C: sl(C, ZR), sample, ssT, zso, save_last=sv)
        shift_chunk(Cs, lambda C: 8 + jf(C), n, lambda C: sl(C, ZV), sample, ssT, zso, save_last=sv)
        each(lambda C, j: mm1(C, j, wua[0:64, j * 128:(j + 1) * 128], thad[0:64, 0:n], ["wua", "thad"], "w"))
        each(lambda C, j: AE(lambda e, pz=P[ck(C)][0]: e.activation(out=A(C, SG), in_=pz[:, 0:n], func=AF.Sigmoid,
                                                                    bias=pcol(W0_ + j)), [P[ck(C)][1], "prm"],
                             [T(C, SG)]))
        each(lambda C, j: mm1(C, j, wua[64:128, j * 128:(j + 1) * 128], thad[64:128, 0:n], ["wua", "thad"], "a"))
        each(lambda C, j: AE(lambda e, pz=P[ck(C)][0]: e.activation(out=A(C, AA), in_=pz[:, 0:n], func=AF.Sigmoid,
                                                                    bias=pcol(A0_ + j)), [P[ck(C)][1], "prm"],
                             [T(C, AA)]))
        def bvd(C):
            return bvf(C)[0]

        def bvb(C):
            return bvf(C)[0].bitcast(BF16)[:, 0:n]
        each(lambda C, j: VE(lambda e: e.scalar_tensor_tensor(out=bvd(C), in0=A(C, ZK), scalar=pcol(KA_ + j),
                                                              in1=A(C, AA), op0=ALU.mult, op1=ALU.mult),
                             [T(C, ZK), T(C, AA), "prm"], [bvf(C)[1]]))
        each(lambda C, j: VE(lambda e: e.scalar_tensor_tensor(out=A(C, ZK), in0=A(C, ZK), scalar=pcol(OMKA_ + j),
                                                              in1=bvd(C), op0=ALU.mult, op1=ALU.add),
                             [T(C, ZK), bvf(C)[1], "prm"], [T(C, ZK)]))
        each(lambda C, j: GE(lambda e: e.tensor_tensor(out=bvb(C), in0=A(C, ZR), in1=A(C, ZK),
                                                       op=ALU.mult), [T(C, ZR), T(C, ZK), bvf(C)[1]], [bvf(C)[1]]))
        each(lambda C, j: mm1(C, j, blkrk[:, j, :], bvb(C), ["blkrk", bvf(C)[1]], "b"))
        each(lambda C, j: VE(lambda e, pz=P[ck(C)][0], d=bvf(C)[0]: e.tensor_tensor(out=d, in0=A(C, ZV),
                                                                                    in1=pz[:, 0:n], op=ALU.mult),
                             [P[ck(C)][1], T(C, ZV), bvf(C)[1]], [bvf(C)[1]]))
        if not sample:
            each(lambda C, j: VE(lambda e: e.tensor_tensor_scan(out=A(C, CSG), data0=scanm[:, 0:n],
                                                                data1=A(C, SG), initial=0.0, op0=ALU.mult,
                                                                op1=ALU.add), ["scanm", T(C, SG)], [T(C, CSG)]))
            each(lambda C, j: GE(lambda e: e.tensor_tensor(out=A(C, CEX), in0=A(C, CSG), in1=A(C, SG),
                                                           op=ALU.subtract), [T(C, CSG), T(C, SG)], [T(C, CEX)]))
        if FILL.get("flush_at_ln"):
            flush_fill()
        FILL["ok"] = False
        each(lambda C, j: AE(lambda e: e.activation(out=A(C, S1), in_=A(C, S1), func=AF.Ln),
                             [T(C, S1)], [T(C, S1)]))
        each(lambda C, j: AE(lambda e: e.activation(out=A(C, S1), in_=A(C, S1), func=AF.Exp, scale=-0.5),
                             [T(C, S1)], [T(C, S1)]))
        if not sample:
            each(lambda C, j: AE(lambda e: e.activation(out=A(C, PIN), in_=A(C, CSG), func=AF.Exp,
                                                        scale=-CDEC), [T(C, CSG)], [T(C, PIN)]))
            each(lambda C, j: AE(lambda e: e.activation(out=A(C, PEX), in_=A(C, CEX), func=AF.Exp,
                                                        scale=-CDEC), [T(C, CEX)], [T(C, PEX)]))
            each(lambda C, j: AE(lambda e: e.activation(out=A(C, PINV), in_=A(C, CSG), func=AF.Exp,
                                                        scale=CDEC), [T(C, CSG), T(C, PIN)], [T(C, PINV)]))
            each(lambda C, j: GE(lambda e: e.tensor_copy(
                out=pcs[:, j, :], in_=A(C, PIN).rearrange("p (c t) -> p c t", t=64)[:, :, 63]),
                [T(C, PIN)], [("pcs", j)]))
        else:
            each(lambda C, j: AE(lambda e: e.activation(out=A(C, PIN), in_=A(C, SG), func=AF.Exp,
                                                        scale=-CDEC), [T(C, SG)], [T(C, PIN)]))
        each(lambda C, j: VE(lambda e: e.tensor_tensor(out=A(C, S0), in0=A(C, S0), in1=A(C, S1),
                                                       op=ALU.mult), [T(C, S0), T(C, S1)], [T(C, S0)]))
        each(lambda C, j: GE(lambda e: e.tensor_tensor(out=A(C, S1), in0=A(C, S0), in1=A(C, AA),
                                                       op=ALU.mult), [T(C, S0), T(C, AA), T(C, S1)], [T(C, S1)]))
        if not sample:
            each(lambda C, j: VE(lambda e: e.tensor_tensor(out=C.KR[:, C.ppi, 1, 0:n], in0=A(C, ZR), in1=A(C, PIN),
                                                           op=ALU.mult), [T(C, ZR), T(C, PIN)],
                                 [tk(C, "KR1", C.ppi)]))
            each(lambda C, j: VE(lambda e: e.tensor_tensor(out=C.KR[:, C.ppi, 0, 0:n], in0=A(C, S0),
                                                           in1=A(C, PEX), op=ALU.mult),
                                 [T(C, S0), T(C, PEX)], [tk(C, "KR0", C.ppi)]))
            each(lambda C, j: VE(lambda e: e.tensor_tensor(out=C.KB[:, C.ppi, 0, 0:n], in0=A(C, ZK),
                                                           in1=A(C, PINV), op=ALU.mult),
                                 [T(C, ZK), T(C, PINV)], [tk(C, "KB0", C.ppi)]))
            each(lambda C, j: VE(lambda e: e.scalar_tensor_tensor(out=C.KB[:, C.ppi, 1, 0:n], in0=A(C, S1),
                                                                  scalar=-1.0, in1=A(C, PINV), op0=ALU.mult,
                                                                  op1=ALU.mult), [T(C, S1), T(C, PINV)],
                                 [tk(C, "KB1", C.ppi)]))
            each(lambda C, j: GE(lambda e: e.tensor_copy(out=C.VT[:, C.ppi, 0:n], in_=A(C, ZV)), [T(C, ZV)],
                                 [tk(C, "VT", C.ppi)]))
        return ZR, PIN, ZK, ZV, S0, S1

    def post_pair(Bs, jf, n, ysf, bvf):
        P = {}
        Y = {}

        def so(B):
            return getattr(B, "soff", 0)

        def pk(B):
            return (B.g, getattr(B, "soff", 0))
        for B in Bs:
            ysrc, ytok = ysf(B)
            if ytok[0] == "ps":
                VE(lambda e, src=ysrc, B=B: e.tensor_copy(out=ar(B, 4 + so(B), n), in_=src), [ytok], [at(B, 4 + so(B))])
                ysrc, ytok = ar(B, 4 + so(B), n), at(B, 4 + so(B))
            Y[pk(B)] = (ysrc, ytok)

        def mm1(B, lhs, rhs, rtok, nm):
            pz, t = psalloc(B)
            t = t + (nm,)
            TE(lambda e, pz=pz: e.matmul(pz[:, 0:n], lhsT=lhs, rhs=rhs, start=True, stop=True), rtok, [t])
            P[pk(B)] = (pz, t)

        for B in Bs:
            AE(lambda e, B=B, ys=Y[pk(B)][0]: e.copy(out=arb(B, 6 + so(B), n), in_=ys), [Y[pk(B)][1]],
               [at(B, 6 + so(B))])
        for B in Bs:
            mm1(B, blk64[:], arb(B, 6 + so(B), n), ["blk64", at(B, 6 + so(B))], "m")
        for B in Bs:
            VE(lambda e, pz=P[pk(B)][0], ys=Y[pk(B)][0], B=B: e.tensor_tensor(out=ar(B, 5 + so(B), n), in0=ys, in1=pz[:, 0:n],
                                                                          op=ALU.subtract),
               [P[pk(B)][1], Y[pk(B)][1]], [at(B, 5 + so(B))])
        for B in Bs:
            AE(lambda e, B=B: e.activation(out=arb(B, 6 + so(B), n), in_=ar(B, 5 + so(B), n), func=AF.Square),
               [at(B, 5 + so(B)), at(B, 6 + so(B))], [at(B, 6 + so(B))])
        for B in Bs:
            mm1(B, blk64[:], arb(B, 6 + so(B), n), ["blk64", at(B, 6 + so(B))], "v")
        for B in Bs:
            AE(lambda e, pz=P[pk(B)][0], B=B: e.activation(out=ar(B, 6 + so(B), n), in_=pz[:, 0:n], func=AF.Ln,
                                                         bias=pcol(GNE_)), [P[pk(B)][1], "prm", at(B, 6 + so(B))], [at(B, 6 + so(B))])
        G = {}
        for B in Bs:
            G[pk(B)] = projx(B, 29 + jf(B), n)
        for B in Bs:
            AE(lambda e, B=B: e.activation(out=ar(B, 6 + so(B), n), in_=ar(B, 6 + so(B), n), func=AF.Exp, scale=-0.5),
               [at(B, 6 + so(B))], [at(B, 6 + so(B))])
        for B in Bs:
            AE(lambda e, pz=G[pk(B)][0], B=B: e.activation(out=ar(B, 4 + so(B), n), in_=pz[:, 0:n], func=AF.Silu),
               [G[pk(B)][1], at(B, 4 + so(B)), Y[pk(B)][1]], [at(B, 4 + so(B))])
        for B in Bs:
            VE(lambda e, B=B: e.tensor_tensor(out=ar(B, 7 + so(B), n), in0=ar(B, 5 + so(B), n), in1=ar(B, 6 + so(B), n), op=ALU.mult),
               [at(B, 5 + so(B)), at(B, 6 + so(B))], [at(B, 7 + so(B))])
        for B in Bs:
            j = jf(B)
            GE(lambda e, B=B, j=j: e.tensor_scalar(out=ar(B, 7 + so(B), n), in0=ar(B, 7 + so(B), n), scalar1=pcol(LW_ + j),
                                                   scalar2=pcol(LB_ + j), op0=ALU.mult, op1=ALU.add),
               [at(B, 7 + so(B)), "prm"], [at(B, 7 + so(B))])
        for B in Bs:
            GE(lambda e, B=B, bv_=bvf(B)[0]: e.tensor_tensor(out=ar(B, 7 + so(B), n), in0=ar(B, 7 + so(B), n), in1=bv_, op=ALU.add),
               [at(B, 7 + so(B)), bvf(B)[1]], [at(B, 7 + so(B))])
        for B in Bs:
            j = jf(B)
            VE(lambda e, B=B, j=j: e.tensor_tensor(out=ymixT[:, 4 + j, 0:n], in0=ar(B, 7 + so(B), n), in1=ar(B, 4 + so(B), n),
                                                   op=ALU.mult), [at(B, 7 + so(B)), at(B, 4 + so(B))], [("ym", 4 + j)])

    def sample_group():
        n = NS
        B = TH0
        A1 = TH1.arena
        SQs = [A1[:, 0:3, :].rearrange("p a b -> p (a b)").rearrange("p (q c) -> p q c", c=128),
               A1[:, 11:14, :].rearrange("p a b -> p (a b)").rearrange("p (q c) -> p q c", c=128)]
        yq = A1[:, 3:5, :].rearrange("p a b -> p (a b)")
        VQ = A1[:, 5:7, :].rearrange("p a b -> p (a b)")[:, 0:384].rearrange("p (q c) -> p q c", c=64)
        scT = A1[:, 7, 0:128].rearrange("p (j t b) -> p j t b", j=4, t=2)
        convo = A1[:, 7, 128:256].rearrange("p (j t b) -> p j t b", j=4, t=2)
        ssT = A1[:, 8, 0:208].rearrange("p (q b) -> p q b", b=NS)
        zso = A1[:, 9, 0:208].rearrange("p (q b) -> p q b", b=NS)
        bvS = A1[:, 10, 0:64].rearrange("p (j b) -> p j b", b=NS)
        sa = A1[:, 10, 64:128]
        yv = A1[:, 10, 128:192]
        DMA(lambda e: e.dma_start(out=scT, in_=scT_d), [], ["scT"], "d_c3")
        DMA(lambda e: e.dma_start(out=ssT, in_=ssT_d), [], ["ssT"], "d_c4")
        load_norm_T([B], xs_d, [0], NS, [0])
        wdad_chunk(B, n, True, ssT, zso)
        S.barrier()
        SBs = []
        for k_ in range(4):
            Bk = TH()
            Bk.g = 10 + k_
            Bk.arena = TH0.arena[:, :, 64 * k_:64 * k_ + 64]
            Bk.ring = [[0, 1, 3, 4][k_]]
            Bk.rs = {"ri": 0}
            SBs.append(Bk)
        sj = lambda Bk: Bk.g - 10
        conv_branch(SBs, sj, n, True, scT, convo)
        DMA(lambda e: e.dma_start(out=convs_d, in_=convo),
            [("cvo", j) for j in range(4)] + [("cvo0", j) for j in range(4)], [], "d_o")
        slots = pair_pre(SBs, sj, n, True, lambda Bk: (bvS[:, sj(Bk), :], ("bvS", sj(Bk))), ssT, zso)
        for j in range(4):
            B = SBs[j]
            SQ = SQs[j % 2]
            for part, qs in ((0, slots[0:4]), (1, slots[4:6])):
                pzt, tk_ = psalloc(B)
                tt = tk_ + ("sq",)
                for qi, sl in enumerate(qs):
                    TE(lambda e, qi=qi, sl=sl, pzt=pzt, B=B: e.transpose(pzt[0:NS, qi * 128:(qi + 1) * 128],
                                                                          ar(B, sl, n), identf[:, :]),
                       [at(B, sl), "identf"], [tt])
                nq = len(qs)
                AE(lambda e, pzt=pzt, part=part, nq=nq, SQ=SQ: e.copy(
                    out=SQ[0:NS, part * 4:part * 4 + nq, :],
                    in_=pzt[0:NS, 0:nq * 128].rearrange("p (q c) -> p q c", c=128)), [tt], [("SQ", j % 2, part)])
            for h in range(2):
                DMA(lambda e, j=j, h=h, SQ=SQ: e.dma_start(out=scr1_d[:, 2 * j + h, :, :],
                                                           in_=SQ[0:NS, :, h * 64:(h + 1) * 64]),
                    [("SQ", j % 2, 0), ("SQ", j % 2, 1)], [("scr1", j, h)], ["d_s1", "d_s1b"][j % 2])
        DMA(lambda e: e.dma_start(out=shifts_d, in_=zso), [("zso", q) for q in range(13)], [], "d_o")
        DMA(lambda e: e.dma_start(out=VQ, in_=scr1_d.rearrange("b h q k -> (b h) q k")),
            [("scr1", j, h) for j in range(4) for h in range(2)], ["VQ"], "d_s2")
        B = TH0
        S.barrier()
        def bk(q):
            return VQ[:, q, :].unsqueeze(1).broadcast_to([128, 16, 64])

        def state_load(qt):
            sl0 = 8 * (qt % 2)
            S3 = B.arena[:, sl0:sl0 + 4, :].rearrange("p a b -> p (a b)").rearrange("p (v k) -> p v k", k=64)
            s3t = [at(B, k) for k in range(sl0, sl0 + 4)]
            DMA(lambda e: e.dma_start(out=S3, in_=swkv_d[:, qt * 1024:(qt + 1) * 1024]
                                      .rearrange("p (v k) -> p v k", k=64)), [], s3t, ["d_l", "d_l2"][qt % 2])

        def q_views(qt):
            sl0 = 8 * (qt % 2)
            S3 = B.arena[:, sl0:sl0 + 4, :].rearrange("p a b -> p (a b)").rearrange("p (v k) -> p v k", k=64)
            TM = B.arena[:, sl0 + 4:sl0 + 8, :].rearrange("p a b -> p (a b)").rearrange("p (v k) -> p v k", k=64)
            s3t = [at(B, k) for k in range(sl0, sl0 + 4)]
            tmt = [at(B, k) for k in range(sl0 + 4, sl0 + 8)]
            return S3, TM, s3t, tmt

        def state_a(qt):
            v0 = qt * 16
            S3, TM, s3t, tmt = q_views(qt)

            def bvv(ap2):
                return ap2[:, v0:v0 + 16].unsqueeze(2).broadcast_to([128, 16, 64])
            VE(lambda e: e.tensor_tensor(out=TM, in0=S3, in1=bk(4), op=ALU.mult), s3t + ["VQ"], tmt)
            VE(lambda e: e.tensor_reduce(out=sa[:, v0:v0 + 16], in_=TM, axis=AX.X, op=ALU.add, negate=True),
               tmt, [("sa", qt)])
            GE(lambda e: e.tensor_tensor(out=S3, in0=S3, in1=bk(1), op=ALU.mult), s3t + ["VQ"], s3t)
            VE(lambda e: e.tensor_tensor(out=TM, in0=bvv(sa), in1=bk(5), op=ALU.mult), [("sa", qt), "VQ"] + tmt, tmt)
            GE(lambda e: e.tensor_tensor(out=S3, in0=S3, in1=TM, op=ALU.add), s3t + tmt, s3t)

        def state_b(qt):
            v0 = qt * 16
            S3, TM, s3t, tmt = q_views(qt)
            wsm = ["d_ws", "d_ws2"][qt % 2]

            def bvv(ap2):
                return ap2[:, v0:v0 + 16].unsqueeze(2).broadcast_to([128, 16, 64])
            VE(lambda e: e.tensor_tensor(out=TM, in0=bvv(VQ[:, 3, :]), in1=bk(2), op=ALU.mult), ["VQ"] + tmt, tmt)
            GE(lambda e: e.tensor_tensor(out=S3, in0=S3, in1=TM, op=ALU.add), s3t + tmt, s3t)
            DMA(lambda e: e.dma_start(out=wkvs_d[:, qt * 1024:(qt + 1) * 1024]
                                      .rearrange("p (v k) -> p v k", k=64), in_=S3), s3t, [], wsm, q="scalar")
            VE(lambda e: e.tensor_tensor(out=TM, in0=S3, in1=bk(0), op=ALU.mult), s3t + ["VQ"] + tmt, tmt)
            VE(lambda e: e.tensor_reduce(out=yv[:, v0:v0 + 16], in_=TM, axis=AX.X, op=ALU.add), tmt, [("yv", qt)])

        p1g = p1_gen(THS, x_d, [0, 128], 128, [0, 128], 1, d7stage=True, carry="zero") if n_st > 0 else iter(())

        def p1step(k=2):
            for _ in range(k):
                next(p1g, None)
        state_load(0)
        state_load(1)
        state_a(0)
        p1step()
        state_a(1)
        p1step()
        state_b(0)
        state_load(2)
        p1step()
        state_b(1)
        state_load(3)
        p1step()
        state_a(2)
        p1step()
        state_a(3)
        p1step()
        state_b(2)
        state_b(3)
        for _ in p1g:
            pass
        DMA(lambda e: e.dma_start(out=scr2_d.rearrange("b h v -> (b h) v"), in_=yv), [("yv", q_) for q_ in range(4)],
            ["scr2"], "d_s3")
        DMA(lambda e: e.dma_start(out=yq[0:NS, :], in_=scr2_d.rearrange("b h v -> b (h v)")), ["scr2"], ["yq"], "d_s4")
        S.barrier()
        YP = {}
        for Bk in SBs:
            j = sj(Bk)
            pzt, tk_ = psalloc(Bk)
            tt = tk_ + ("yt",)
            YP[Bk.g] = (pzt, tt)
            TE(lambda e, j=j, pzt=pzt: e.transpose(pzt[:, 0:NS], yq[0:NS, j * 128:(j + 1) * 128], identf[0:NS, 0:NS]),
               ["yq", "identf"], [tt])
        post_pair(SBs, sj, n, lambda Bk: (YP[Bk.g][0][:, 0:NS], YP[Bk.g][1]),
                  lambda Bk: (bvS[:, sj(Bk), :], ("bvS", sj(Bk))))
        DMA(lambda e: e.dma_start(out=ym_d[:, :, SEQ:SEQ + NS], in_=ymixT[:, :, 0:NS]),
            [("ym", k) for k in range(8)], [], "d_ym")

    def bc4(ap2):
        return ap2.unsqueeze(1).broadcast_to([128, 4, 128])

    def halves_wkv(Bs, front=None):
        n = NT
        jb = lambda B, pp: 2 * B.g + pp
        PC4 = []
        for pp in range(2):
            for B in Bs:
                C = _copy.copy(B)
                C.ring, C.rs = B.ring4, B.rs4
                C.sb = 9 * pp
                C.ppi = pp
                PC4.append(C)
        FILL["k2"] = True
        pair_pre(PC4, lambda C: jb(C, C.ppi), n, False, lambda C: (C.bv[:, C.ppi, 0:n], tk(C, "bv", C.ppi)))
        FILL["ok"] = True
        flush_fill()
        FILL["ok"] = False
        FILL["k2"] = False

        def tkt(B, i):
            return [tk(B, "TK", i, 0), tk(B, "TK", i, 1)]

        def tk_round(i, rnd, qs):
            PT = {}
            for B in Bs:
                pt, tk_ = ptalloc(B)
                t = tk_ + ("tk",)
                PT[B.g] = (pt, t)
                for qi, q in enumerate(qs):
                    for pp in range(2):
                        if q == 0:
                            src, nm = B.KB[:, pp, 0, i * 128:(i + 1) * 128], "KB0"
                        elif q == 1:
                            src, nm = B.KB[:, pp, 1, i * 128:(i + 1) * 128], "KB1"
                        elif q == 2:
                            src, nm = B.VT[:, pp, i * 128:(i + 1) * 128], "VT"
                        else:
                            src, nm = B.KR[:, pp, 0, i * 128:(i + 1) * 128], "KR0"
                        TE(lambda e, src=src, pt=pt, qi=qi, pp=pp: e.transpose(
                            pt[:, qi * 256 + pp * 128:qi * 256 + (pp + 1) * 128], src, ident[:, :]),
                           [tk(B, nm, pp), "ident"], [t], sig=(qi == len(qs) - 1 and pp == 1))
            for B in Bs:
                pt, t = PT[B.g]
                AE(lambda e, pt=pt, B=B: e.copy(
                    out=B.TK[:, i, :, :].rearrange("p q c -> p (q c)"), in_=pt[:, :]),
                   [t], [tk(B, "TK", i, 0), tk(B, "TK", i, 1)])

        def score_head(i, hh):
            tsl = slice(i * 128, (i + 1) * 128)
            pp, h = hh // 2, hh % 2
            hs = slice(64 * h, 64 * h + 64)
            P = {}
            for B in Bs:
                ps, tk_ = psalloc(B.wide)
                t = tk_ + ("s",)
                P[B.g] = (ps, t)
                TE(lambda e, ps=ps, B=B: e.matmul(ps[:, 0:256], lhsT=B.KB[hs, pp, 0, tsl], rhs=B.KR[hs, pp, :, tsl],
                                                  start=True, stop=True),
                   [tk(B, "KB0", pp), tk(B, "KR0", pp), tk(B, "KR1", pp)], [t])
                TE(lambda e, ps=ps, B=B: e.matmul(ps[:, 256:512], lhsT=B.KB[hs, pp, 1, tsl],
                                                  rhs=B.KR[hs, pp, :, tsl], start=True, stop=True),
                   [tk(B, "KB1", pp), tk(B, "KR0", pp), tk(B, "KR1", pp)], [t])
            if hh % 2 == 0:
                for B in Bs:
                    ps, t = P[B.g]
                    VE(lambda e, ps=ps, B=B: e.tensor_tensor(
                        out=B.SC[:, i, hh, :].rearrange("p (a c) -> p a c", a=2),
                        in0=ps[:, :].rearrange("p (a c) -> p a c", a=2),
                        in1=maskA[:, :].unsqueeze(1).broadcast_to([128, 2, 256]), op=ALU.mult),
                       [t, "maskA"], [tk(B, "SC", i, hh)])
            else:
                for B in Bs:
                    ps, t = P[B.g]
                    AE(lambda e, ps=ps, B=B: e.copy(out=B.SC[:, i, hh, :], in_=ps[:, :]), [t], [tk(B, "SC", i, hh)])
                for B in Bs:
                    GE(lambda e, B=B: e.tensor_tensor(
                        out=B.SC[:, i, hh, :].rearrange("p (a c) -> p a c", a=2),
                        in0=B.SC[:, i, hh, :].rearrange("p (a c) -> p a c", a=2),
                        in1=maskA[:, :].unsqueeze(1).broadcast_to([128, 2, 256]), op=ALU.mult),
                       [tk(B, "SC", i, hh), "maskA"], [tk(B, "SC", i, hh)])

        def score_N(i, h):
            tsl = slice(i * 128, (i + 1) * 128)
            hs = slice(64 * h, 64 * h + 64)
            P = {}
            for B in Bs:
                ps, tk_ = psalloc(B.wide)
                t = tk_ + ("n",)
                P[B.g] = (ps, t)
                for pp in range(2):
                    TE(lambda e, ps=ps, pp=pp, B=B: e.matmul(ps[:, pp * 128:(pp + 1) * 128],
                                                             lhsT=B.KR[hs, pp, 0, tsl], rhs=B.KB[hs, pp, 1, tsl],
                                                             start=True, stop=True),
                       [tk(B, "KR0", pp), tk(B, "KB1", pp)], [t])
            for pp in range(2):
                for B in Bs:
                    ps, t = P[B.g]
                    VE(lambda e, ps=ps, pp=pp, B=B: e.tensor_tensor(out=B.NB[:, pp * 2 + h, :],
                                                                    in0=ps[:, pp * 128:(pp + 1) * 128],
                                                                    in1=maskN[:, :], op=ALU.mult),
                       [t, "maskN"], [tk(B, "NB", pp * 2 + h)])

        def sct(B, i):
            return [tk(B, "SC", i, hh) for hh in range(4)]

        def nbt(B):
            return [tk(B, "NB", hh) for hh in range(4)]

        def dbl_level(i, lv):
            cur, nxt = (lv - 1) % 2, lv % 2

            def Ap(B, hh):
                return B.SC[:, i, hh, 256:384] if lv == 1 else B.dA[cur][:, hh, :]

            def Bp(B, hh):
                return B.NB[:, hh, :] if lv == 1 else B.dB[cur][:, hh, :]

            def abt(B):
                return (sct(B, i) + nbt(B)) if lv == 1 else [tk(B, f"dA{cur}"), tk(B, f"dB{cur}")]
            PB, PA, PP_ = {}, {}, {}
            for B in Bs:
                ps, tk_ = psalloc(B.wide)
                t = tk_ + ("B",)
                PB[B.g] = (ps, t)
                for hh in range(4):
                    TE(lambda e, ps=ps, hh=hh, a=Ap(B, hh), b=Bp(B, hh): e.matmul(
                        ps[:, hh * 128:(hh + 1) * 128], lhsT=a, rhs=b, start=True, stop=True), abt(B), [t],
                       sig=(hh == 3))
            for B in Bs:
                ps, t = PB[B.g]
                AE(lambda e, ps=ps, B=B: e.copy(out=B.dB[nxt][:, :, :].rearrange("p a c -> p (a c)"), in_=ps[:, :]),
                   [t], [tk(B, f"dB{nxt}")])
            if lv <= 4:
                for B in Bs:
                    ps, tk_ = psalloc(B.wide)
                    t = tk_ + ("A",)
                    PA[B.g] = (ps, t)
                    for hh in range(4):
                        TE(lambda e, ps=ps, hh=hh, a=Ap(B, hh), b=Bp(B, hh): e.matmul(
                            ps[:, hh * 128:(hh + 1) * 128], lhsT=b, rhs=a, start=True, stop=True), abt(B), [t],
                           sig=(hh == 3))
                for B in Bs:
                    ps, t = PA[B.g]
                    AE(lambda e, ps=ps, B=B: e.copy(out=B.dA[nxt][:, :, :].rearrange("p a c -> p (a c)"),
                                                    in_=ps[:, :]), [t], [tk(B, f"dA{nxt}")])
            for B in Bs:
                ps, tk_ = psalloc(B.wide)
                t = tk_ + ("P",)
                PP_[B.g] = (ps, t)
                for hh in range(4):
                    TE(lambda e, ps=ps, hh=hh, B=B: e.matmul(ps[:, hh * 128:(hh + 1) * 128],
                                                             lhsT=B.dB[nxt][:, hh, :], rhs=B.dP[cur][:, hh, :],
                                                             start=True, stop=True),
                       [tk(B, f"dB{nxt}"), tk(B, f"dP{cur}")], [t], sig=(hh == 3))
            for B in Bs:
                ps, t = PP_[B.g]
                VE(lambda e, ps=ps, B=B: e.tensor_tensor(
                    out=B.dP[nxt][:, :, :].rearrange("p a c -> p (a c)"), in0=ps[:, :],
                    in1=B.dP[cur][:, :, :].rearrange("p a c -> p (a c)"), op=ALU.add),
                   [t, tk(B, f"dP{cur}")], [tk(B, f"dP{nxt}")])

        def mtt(B):
            return [tk(B, "dP1")]

        def prec(i):
            P = {}
            for B in Bs:
                ps, tk_ = psalloc(B.wide)
                t = tk_ + ("w1",)
                P[B.g] = (ps, t)
                for hh in range(4):
                    TE(lambda e, ps=ps, hh=hh, B=B: e.matmul(ps[:, hh * 64:(hh + 1) * 64], lhsT=B.SC[:, i, hh, 0:128],
                                                             rhs=B.TK[:, i, 2, hh * 64:(hh + 1) * 64],
                                                             start=True, stop=True),
                       [tk(B, "SC", i, hh)] + tkt(B, i), [t])
            for B in Bs:
                ps, t = P[B.g]
                AE(lambda e, ps=ps, B=B: e.copy(out=B.W1[:, :, :].rearrange("p a c -> p (a c)"), in_=ps[:, 0:256]),
                   [t], [tk(B, "W1")])
            for B in Bs:
                ps, tk_ = psalloc(B.wide)
                t = tk_ + ("u1",)
                P[B.g] = (ps, t)
                for hh in range(4):
                    TE(lambda e, ps=ps, hh=hh, B=B: e.matmul(ps[:, hh * 64:(hh + 1) * 64], lhsT=B.dP[1][:, hh, :],
                                                             rhs=B.W1[:, hh, :], start=True, stop=True),
                       mtt(B) + [tk(B, "W1")], [t])
            for B in Bs:
                ps, t = P[B.g]
                AE(lambda e, ps=ps, B=B: e.copy(out=B.U1b[:, i, :], in_=ps[:, 0:256]), [t], [tk(B, "U1b", i)])
            for B in Bs:
                ps, tk_ = psalloc(B.wide)
                t = tk_ + ("kp",)
                P[B.g] = (ps, t)
                for hh in range(4):
                    pp, h = hh // 2, hh % 2
                    TE(lambda e, ps=ps, hh=hh, pp=pp, h=h, B=B: e.matmul(
                        ps[64 * h:64 * h + 64, pp * 128:(pp + 1) * 128], lhsT=B.TK[:, i, 3, hh * 64:(hh + 1) * 64],
                        rhs=B.dP[1][:, hh, :], start=True, stop=True), mtt(B) + tkt(B, i), [t])
            for B in Bs:
                ps, t = P[B.g]
                VE(lambda e, ps=ps, B=B: e.tensor_copy(out=B.KPT[:, i, :, :].rearrange("p a c -> p (a c)"),
                                                       in_=ps[:, 0:256]), [t], [tk(B, "KPT", i)])
            for B in Bs:
                ps, tk_ = psalloc(B.wide)
                t = tk_ + ("y1",)
                P[B.g] = (ps, t)
                for hh in range(4):
                    pp, h = hh // 2, hh % 2
                    TE(lambda e, ps=ps, hh=hh, pp=pp, h=h, B=B: e.matmul(
                        ps[64 * h:64 * h + 64, pp * 128:(pp + 1) * 128], lhsT=B.TK[:, i, 2, hh * 64:(hh + 1) * 64],
                        rhs=B.SC[:, i, hh, 128:256], start=True, stop=False), tkt(B, i) + [tk(B, "SC", i, hh)], [t])
                    TE(lambda e, ps=ps, hh=hh, pp=pp, h=h, B=B: e.matmul(
                        ps[64 * h:64 * h + 64, pp * 128:(pp + 1) * 128], lhsT=B.U1b[:, i, hh * 64:(hh + 1) * 64],
                        rhs=B.SC[:, i, hh, 384:512], start=False, stop=True),
                       [tk(B, "U1b", i), tk(B, "SC", i, hh)], [t])
            for B in Bs:
                ps, t = P[B.g]
                AE(lambda e, ps=ps, B=B: e.copy(out=B.Y1T[:, i, :, :].rearrange("p a c -> p (a c)"), in_=ps[:, 0:256]),
                   [t], [tk(B, "Y1T", i)])

        def h1_part(i, ee):
            es_ = slice(64 * ee, 64 * ee + 64)
            s0_ = i * 4 + ee * 2
            P = {}
            for B in Bs:
                ps, tk_ = psalloc(B.wide)
                t = tk_ + ("h1",)
                P[B.g] = (ps, t)
                for pp in range(2):
                    cs = slice(pp * 128, (pp + 1) * 128)
                    TE(lambda e, ps=ps, cs=cs, B=B: e.matmul(ps[:, cs], lhsT=B.TK[es_, i, 0, cs],
                                                             rhs=B.TK[es_, i, 2, cs], start=True, stop=False),
                       tkt(B, i), [t])
                    TE(lambda e, ps=ps, cs=cs, B=B: e.matmul(ps[:, cs], lhsT=B.TK[es_, i, 1, cs],
                                                             rhs=B.U1b[es_, i, cs], start=False, stop=True),
                       tkt(B, i) + [tk(B, "U1b", i)], [t])
            for B in Bs:
                ps, t = P[B.g]
                AE(lambda e, ps=ps, B=B: e.copy(out=B.H1[:, s0_:s0_ + 2, :].rearrange("p a c -> p (a c)"),
                                                in_=ps[:, 0:256]), [t],
                   [tk(B, "H1", s0_), tk(B, "H1", s0_ + 1)])

        def Tt(B):
            return [("T", 2 * B.g), ("T", 2 * B.g + 1)]

        def chain_chunk(i, ee, PY):
            es_ = slice(64 * ee, 64 * ee + 64)
            ch = i * 2 + ee
            PU, PD = {}, {}
            fill()
            for B in Bs:
                psU, tk_ = psalloc(B.chain)
                tU = tk_ + ("U",)
                PU[B.g] = (psU, tU)
                for pp in range(2):
                    TE(lambda e, psU=psU, pp=pp, B=B: e.matmul(psU[:, pp * 128:(pp + 1) * 128],
                                                               lhsT=B.KPT[:, i, pp, :], rhs=T16[:, 2 * B.g + pp, :],
                                                               start=True, stop=True),
                       [tk(B, "KPT", i), ("T16", 2 * B.g + pp)], [tU])
            fill()
            for B in Bs:
                psU, tU = PU[B.g]
                VE(lambda e, psU=psU, B=B: e.tensor_copy(out=B.U0b[es_, :], in_=psU[es_, 0:256]), [tU],
                   [tk(B, "U0b")])
            fill()
            for pp in range(2):
                slot = i * 4 + ee * 2 + pp
                for B in Bs:
                    GE(lambda e, pp=pp, B=B, slot=slot: e.tensor_tensor(out=B.Tsum[:, pp, :],
                                                                        in0=T32[:, 2 * B.g + pp, :],
                                                                        in1=B.H1[:, slot, :], op=ALU.add),
                       [("T", 2 * B.g + pp), tk(B, "H1", slot)], [tk(B, "Ts", pp)])
            fill()
            for B in Bs:
                psD, tk_ = psalloc(B.chain)
                tD = tk_ + ("D",)
                PD[B.g] = (psD, tD)
                for pp in range(2):
                    cs = slice(pp * 128, (pp + 1) * 128)
                    TE(lambda e, psD=psD, cs=cs, B=B: e.matmul(psD[:, cs], lhsT=B.TK[es_, i, 1, cs],
                                                               rhs=B.U0b[es_, cs], start=True, stop=True),
                       tkt(B, i) + [tk(B, "U0b")], [tD])
            fill()
            for B in Bs:
                psY, tY = PY[B.g]
                for pp in range(2):
                    ycs = slice(pp * 128 + 64 * ee, pp * 128 + 64 * ee + 64)
                    TE(lambda e, psY=psY, pp=pp, ycs=ycs, B=B: e.matmul(
                        psY[:, ycs], lhsT=T16[:, 2 * B.g + pp, :],
                        rhs=B.KR[:, pp, 1, i * 128 + 64 * ee:i * 128 + 64 * ee + 64], start=True, stop=False),
                       [("T16", 2 * B.g + pp), tk(B, "KR1", pp)], [tY])
                    for h in range(2):
                        hh = pp * 2 + h
                        TE(lambda e, psY=psY, ycs=ycs, h=h, hh=hh, B=B: e.matmul(
                            psY[64 * h:64 * h + 64, ycs], lhsT=B.U0b[es_, hh * 64:(hh + 1) * 64],
                            rhs=B.SC[es_, i, hh, 384 + 64 * ee:384 + 64 * ee + 64], start=False, stop=True),
                           [tk(B, "U0b"), tk(B, "SC", i, hh)], [tY])
            fill()
            for h in range(2):
                hs = slice(64 * h, 64 * h + 64)
                for B in Bs:
                    psD, tD = PD[B.g]
                    VE(lambda e, psD=psD, B=B, hs=hs: e.tensor_tensor(
                        out=B.Tsum[hs, :, hs], in0=psD[hs, 0:256].rearrange("p (a c) -> p a c", a=2)[:, :, hs],
                        in1=B.Tsum[hs, :, hs], op=ALU.add), [tD, tk(B, "Ts", 0), tk(B, "Ts", 1)],
                       [tk(B, "Ts", 0), tk(B, "Ts", 1)])
                for B in Bs:
                    VE(lambda e, B=B, hs=hs: e.tensor_tensor(
                        out=T16[hs, 2 * B.g:2 * B.g + 2, hs], in0=B.Tsum[hs, :, hs],
                        in1=pcs[hs, 2 * B.g:2 * B.g + 2, ch:ch + 1].broadcast_to([64, 2, 64]), op=ALU.mult),
                       [tk(B, "Ts", 0), tk(B, "Ts", 1), ("pcs", 2 * B.g), ("pcs", 2 * B.g + 1)],
                       [("T16", 2 * B.g), ("T16", 2 * B.g + 1)])
                for B in Bs:
                    GE(lambda e, B=B, hs=hs: e.tensor_tensor(
                        out=T32[hs, 2 * B.g:2 * B.g + 2, hs], in0=B.Tsum[hs, :, hs],
                        in1=pcs[hs, 2 * B.g:2 * B.g + 2, ch:ch + 1].broadcast_to([64, 2, 64]), op=ALU.mult),
                       [tk(B, "Ts", 0), tk(B, "Ts", 1), ("pcs", 2 * B.g), ("pcs", 2 * B.g + 1)], Tt(B))

        def chain_tile(i):
            PY = {}
            for B in Bs:
                psY, tk_ = psy(B)
                PY[B.g] = (psY, tk_ + ("Y",))
            for ee in range(2):
                chain_chunk(i, ee, PY)
            for B in Bs:
                psY, tY = PY[B.g]
                VE(lambda e, psY=psY, B=B: e.tensor_tensor(
                    out=B.arena[:, 12:14, i * 128:(i + 1) * 128], in0=psY[:, 0:256].rearrange("p (a c) -> p a c", a=2),
                    in1=B.Y1T[:, i, :, :], op=ALU.add), [tY, tk(B, "Y1T", i)], [at(B, 12), at(B, 13)])

        for i in range(2):
            tk_round(i, 0, (0, 1, 2, 3))
        for i in range(2):
            for hh in range(4):
                score_head(i, hh)
            for h in range(2):
                score_N(i, h)
            for B in Bs:
                VE(lambda e, B=B, i=i: e.tensor_tensor(out=B.dP[0][:, :, :], in0=B.SC[:, i, :, 256:384],
                                                       in1=bc4(ident[:, :]), op=ALU.add),
                   sct(B, i) + ["ident"], [tk(B, "dP0")])
            for lv in range(1, 6):
                dbl_level(i, lv)
            prec(i)
            for ee in range(2):
                h1_part(i, ee)
        if front is not None:
            FILL["gen"] = front
        FILL["ok"] = True
        for i in range(2):
            chain_tile(i)
        flush_fill()
        FILL["ok"] = False
        PB4 = []
        for pp in range(2):
            for B in Bs:
                Bq = _copy.copy(B.wide)
                Bq.soff = 4 * pp
                Bq.ppi = pp
                PB4.append(Bq)
        post_pair(PB4, lambda B: jb(B, B.ppi), n, lambda B: (ar(B, 12 + B.ppi, n), at(B, 12 + B.ppi)),
                  lambda B: (B.bv[:, B.ppi, 0:n], tk(B, "bv", B.ppi)))

    if do_sample:
        sample_group()
        S.barrier()
    xpar = lambda st: (st + 1) % 2

    def front_gen(st1):
        yield from p1_gen(THS, x_d, [st1 * NT, st1 * NT + 128], 128, [0, 128], xpar(st1), d7stage=True, carry="prev")
        wdad_chunk(WC, NT, False, par=xpar(st1), save_last=(st1 == n_st - 1), defer=True)
        yield

    if n_st > 0:
        if not do_sample:
            load_norm_T(THS, x_d, [0, 128], 128, [0, 128], xpar(0), carry="zero")
        wdad_chunk(TH0, NT, False, par=xpar(0), save_last=(n_st == 1))
    for st in range(n_st):
        CUR["x"] = xpar(st)
        LASTF["v"] = (st == n_st - 1)
        if st > 0:
            thad_ops(TH0, NT)

        def conv_all():
            for jj in range(2):
                yield from conv_gen(CTH, lambda B, jj=jj: 2 * B.g + jj, NT, False, slots=(18, 18, 19))
        FILL["gen"] = conv_all()
        halves_wkv(THS, front_gen(st + 1) if st + 1 < n_st else None)
        DMA(lambda e, st=st: e.dma_start(out=ym_d[:, :, st * NT:(st + 1) * NT], in_=ymixT[:, :, :]),
            [("ym", k) for k in range(8)], [], "d_ym")
    DMA(lambda e: e.dma_start(out=convp_d, in_=uext[:, :, 0:2]), [("u", j) for j in range(4)], [], "d_o")
    DMA(lambda e: e.dma_start(out=shiftp_d, in_=zlast[:, :]), [("zl", q) for q in range(13)], [], "d_o")
    for j in range(4):
        for h in range(2):
            hs = slice(64 * h, 64 * h + 64)
            DMA(lambda e, j=j, h=h, hs=hs: e.dma_start(out=wkvp_d[2 * j + h, :, :], in_=T32[hs, j, hs]),
                [("T", j)], [], "d_o")
    S.barrier()

    wflat = win[:].rearrange("p c d -> p (c d)")
    wout = wflat[:, 0:8192].rearrange("p (c d) -> p c d", c=8)
    wpg = wflat[:, 8192:16384].rearrange("p (c d) -> p c d", c=8)
    wpp = wflat[:, 16384:18432].rearrange("p (c d) -> p c d", c=2)
    gbc = wflat[:, 29696:31744].bitcast(F32)
    stgB = [(xnTf[0][:, 0:2048].bitcast(F32), "stgB0"),
            (ymixT[:].rearrange("p a b -> p (a b)").bitcast(F32), "stgB1"),
            (uext[:].rearrange("p a b -> p (a b)")[:, 0:1024], "stgB2")]
    chunksB = []
    for c in range(8):
        chunksB.append((wout[:, c, :], wout_d[c * 128:(c + 1) * 128, :], "wout"))
    for c in range(8):
        chunksB.append((wpg[:, c, :], wpg_d[c * 128:(c + 1) * 128, :], "wpg"))
    for c in range(2):
        chunksB.append((wpp[:, c, :], wpp_d[c * 128:(c + 1) * 128, :], "wpp"))
    load_cast(chunksB, stgB, "wB")
    DMA(lambda e: e.dma_start(out=gbc, in_=fg_d.partition_broadcast(128)), [], ["gbc"], "d_c2")

    class TB:
        pass

    OT = []
    for t_ in range(4):
        O = TB()
        O.g = t_
        O.pbank = t_ % 2
        if t_ < 2:
            base = 18432 + t_ * 5632
            O.h1b = wflat[:, base:base + 1024]
            O.h1T = wflat[:, base + 1024:base + 2048].rearrange("p (c t) -> p c t", c=8)
            O.sigb = wflat[:, base + 2048:base + 3072]
            O.ptb = wflat[:, base + 3072:base + 3328]
            O.pT = wflat[:, base + 3328:base + 3584].rearrange("p (c t) -> p c t", c=2)
            O.ymi = wflat[:, base + 3584:base + 4608].rearrange("p (c t) -> p c t", c=8)
            A = THS[t_].arena
            O.h1 = A[:, 0:4, :].rearrange("p a b -> p (a b)")
            O.gg = A[:, 4:8, :].rearrange("p a b -> p (a b)")
            O.pt32 = A[:, 8, :]
            O.ring = [t_, 4 + t_]
        else:
            Bx = THS[t_ - 2]
            scf = Bx.SC[:].rearrange("p a b c -> p (a b c)")
            tkf = Bx.TK[:].rearrange("p a b c -> p (a b c)")
            O.h1b = scf[:, 0:1024]
            O.h1T = scf[:, 1024:2048].rearrange("p (c t) -> p c t", c=8)
            O.sigb = scf[:, 2048:3072]
            O.ymi = scf[:, 3072:4096].rearrange("p (c t) -> p c t", c=8)
            O.ptb = tkf[:, 0:256]
            O.pT = tkf[:, 256:512].rearrange("p (c t) -> p c t", c=2)
            O.h1 = Bx.H1[:].rearrange("p a b -> p (a b)")
            O.gg = Bx.arena[:, 9:13, :].rearrange("p a b -> p (a b)")
            O.pt32 = Bx.arena[:, 13, :]
            O.ring = [t_]
        if t_ < 2:
            O.yo = THS[t_].D7[:].rearrange("p a b c -> p (a b c)").bitcast(F32)[:, 0:1024]
            O.yot = ("o", t_, "yo")
        elif t_ == 2:
            O.yo = xnTf[1][:, 0:2048].bitcast(F32)
            O.yot = ("o", t_, "yo")
        else:
            O.yo = uext[:].rearrange("p a b -> p (a b)")[:, 0:1024]
            O.yot = "stgB2"
        O.rs = {"ri": 0}
        O.gring = True
        OT.append(O)

    def ot(O, *nm):
        return ("o", O.g) + nm

    def dsem(p, O):
        if p == "d_m":
            return f"d_m{O.g}"
        return f"{p}{O.g}" if O.g else p

    def o_loads(Os, specs):
        for O, (xsrc, psrc, ydst, tok0, rows, ymcol) in zip(Os, specs):
            DMA(lambda e, O=O, xsrc=xsrc, tok0=tok0, rows=rows: e.dma_start(out=O.gg[0:rows, :],
                                                                            in_=xsrc[tok0:tok0 + rows, :]),
                [], [ot(O, "g")], dsem("d_g", O))
            DMA(lambda e, O=O, psrc=psrc, tok0=tok0, rows=rows: e.dma_start(out=O.pt32[0:rows, :],
                                                                            in_=psrc[tok0:tok0 + rows, :]),
                [], [ot(O, "pt32")], dsem("d_p", O))
            DMA(lambda e, O=O, ymcol=ymcol, rows=rows: e.dma_start(out=O.ymi[:, :, 0:rows],
                                                                   in_=ym_d[:, :, ymcol:ymcol + rows]),
                [], [ot(O, "ymi")], dsem("d_m", O))

    def o_compute(Os, specs, after=None):
        for O, sp in zip(Os, specs):
            rows = sp[4]
            GE(lambda e, O=O, rows=rows: e.tensor_copy(out=O.ptb[0:rows, :], in_=O.pt32[0:rows, :]),
               [ot(O, "pt32")], [ot(O, "ptb")])
        for hf in range(2):
            P = {}
            for O, sp in zip(Os, specs):
                rows = sp[4]
                po, tk_ = psalloc(O)
                t = tk_ + ("o",)
                P[O.g] = (po, t)
                for c in range(8):
                    TE(lambda e, hf=hf, c=c, po=po, O=O, rows=rows: e.matmul(po[0:rows, :], lhsT=O.ymi[:, c, 0:rows],
                                                                      rhs=wout[:, c, hf * 512:(hf + 1) * 512],
                                                                      start=(c == 0), stop=(c == 7)),
                       [ot(O, "ymi"), "wout"], [t], sig=(c == 7))
            for O, sp in zip(Os, specs):
                rows = sp[4]
                po, t = P[O.g]
                VE(lambda e, hf=hf, po=po, O=O, rows=rows: e.tensor_tensor(out=O.h1[0:rows, hf * 512:(hf + 1) * 512],
                                                                    in0=po[0:rows, :],
                                                                    in1=O.gg[0:rows, hf * 512:(hf + 1) * 512],
                                                                    op=ALU.add), [t, ot(O, "g")], [ot(O, "h1", hf)])
        h1t = lambda O: [ot(O, "h1", 0), ot(O, "h1", 1)]
        for O, sp in zip(Os, specs):
            rows = sp[4]
            AE(lambda e, hf=hf, O=O, rows=rows: e.copy(out=O.h1b[0:rows, :], in_=O.h1[0:rows, :]), h1t(O), [ot(O, "h1b")])
        for O, sp in zip(Os, specs):
            rows = sp[4]
            pt, tk_ = ptalloc(O)
            t = tk_ + ("h",)
            for c in range(8):
                TE(lambda e, c=c, pt=pt, O=O, rows=rows: e.transpose(
                    pt[:, c * 128:c * 128 + rows], O.h1b[0:rows, c * 128:(c + 1) * 128], ident[0:rows, 0:rows]),
                   [ot(O, "h1b"), "ident"], [t], sig=(c == 7))
            AE(lambda e, pt=pt, O=O, rows=rows: e.copy(
                out=O.h1T[:, :, 0:rows],
                in_=pt[:, :].rearrange("p (c t) -> p c t", t=128)[:, :, 0:rows]), [t],
               [ot(O, "h1T", 0), ot(O, "h1T", 1)])
        for O, sp in zip(Os, specs):
            rows = sp[4]
            pt, tk_ = ptalloc(O)
            t = tk_ + ("p",)
            for c in range(2):
                TE(lambda e, c=c, pt=pt, O=O, rows=rows: e.transpose(pt[:, c * 128:c * 128 + rows],
                                                                     O.ptb[0:rows, c * 128:(c + 1) * 128],
                                                                     ident[0:rows, 0:rows]), [ot(O, "ptb"), "ident"], [t])
            VE(lambda e, pt=pt, O=O, rows=rows: e.tensor_copy(
                out=O.pT[:, :, 0:rows], in_=pt[:, 0:256].rearrange("p (c t) -> p c t", t=128)[:, :, 0:rows]),
               [t], [ot(O, "pT")])
        for hf in range(2):
            PG, PQ = {}, {}
            for O, sp in zip(Os, specs):
                rows = sp[4]
                pg, tk_ = psalloc(O)
                tg = tk_ + ("g",)
                PG[O.g] = (pg, tg)
                for c in range(8):
                    TE(lambda e, hf=hf, c=c, pg=pg, O=O, rows=rows: e.matmul(pg[0:rows, :], lhsT=O.h1T[:, c, 0:rows],
                                                                      rhs=wpg[:, c, hf * 512:(hf + 1) * 512],
                                                                      start=(c == 0), stop=(c == 7)),
                       [ot(O, "h1T", 0), ot(O, "h1T", 1), "wpg"], [tg], sig=(c == 7))
            for O, sp in zip(Os, specs):
                rows = sp[4]
                pg, tg = PG[O.g]
                AE(lambda e, hf=hf, pg=pg, O=O, rows=rows: e.activation(out=O.sigb[0:rows, hf * 512:(hf + 1) * 512],
                                                                 in_=pg[0:rows, :], func=AF.Sigmoid),
                   [tg], [ot(O, "sig", hf)])
            for O, sp in zip(Os, specs):
                rows = sp[4]
                pq, tk_ = psalloc(O)
                tq = tk_ + ("q",)
                PQ[O.g] = (pq, tq)
                for c in range(2):
                    TE(lambda e, hf=hf, c=c, pq=pq, O=O, rows=rows: e.matmul(pq[0:rows, :], lhsT=O.pT[:, c, 0:rows],
                                                                      rhs=wpp[:, c, hf * 512:(hf + 1) * 512],
                                                                      start=(c == 0), stop=(c == 1)),
                       [ot(O, "pT"), "wpp"], [tq], sig=(c == 1))
            for O, sp in zip(Os, specs):
                rows = sp[4]
                pq, tq = PQ[O.g]
                VE(lambda e, hf=hf, pq=pq, O=O, rows=rows: e.tensor_tensor(out=O.gg[0:rows, hf * 512:(hf + 1) * 512],
                                                                    in0=O.sigb[0:rows, hf * 512:(hf + 1) * 512],
                                                                    in1=pq[0:rows, :], op=ALU.mult),
                   [tq, ot(O, "sig", hf), ot(O, "g")], [ot(O, "g")])
        for O, sp in zip(Os, specs):
            rows = sp[4]
            GE(lambda e, hf=hf, O=O, rows=rows: e.tensor_tensor(out=O.h1[0:rows, :], in0=O.h1[0:rows, :], in1=O.gg[0:rows, :],
                                                         op=ALU.add), h1t(O) + [ot(O, "g")], h1t(O))
        if after is not None:
            after()
        for O, sp in zip(Os, specs):
            rows = sp[4]
            so = 8 * O.g
            AE(lambda e, hf=hf, O=O, rows=rows, so=so: e.activation(out=O.sigb[0:rows, :], in_=O.h1[0:rows, :], func=AF.Square,
                                                             accum_out=stat[0:rows, so + 4:so + 5]),
               h1t(O) + [ot(O, "sig", 0), ot(O, "sig", 1)], [ot(O, "sig", 0), ot(O, "sig", 1), ot(O, "st4")])
        for O, sp in zip(Os, specs):
            rows = sp[4]
            so = 8 * O.g
            AE(lambda e, hf=hf, rows=rows, so=so: e.activation(out=stat[0:rows, so + 5:so + 6], in_=stat[0:rows, so + 4:so + 5],
                                                        func=AF.Ln, scale=1.0 / D, bias=prm[0:rows, RME_:RME_ + 1]),
               [ot(O, "st4"), "prm"], [ot(O, "st5")])
        for O, sp in zip(Os, specs):
            rows = sp[4]
            so = 8 * O.g
            AE(lambda e, rows=rows, so=so: e.activation(out=stat[0:rows, so + 6:so + 7], in_=stat[0:rows, so + 5:so + 6],
                                                        func=AF.Exp, scale=-0.5), [ot(O, "st5")], [ot(O, "st6")])
        for O, sp in zip(Os, specs):
            rows = sp[4]
            so = 8 * O.g
            VE(lambda e, hf=hf, O=O, rows=rows, so=so: e.scalar_tensor_tensor(out=O.yo[0:rows, :], in0=O.h1[0:rows, :],
                                                                       scalar=stat[0:rows, so + 6:so + 7],
                                                                       in1=gbc[0:rows, :], op0=ALU.mult, op1=ALU.mult),
               h1t(O) + [ot(O, "st6"), "gbc"], [O.yot])
        for O, (xsrc, psrc, ydst, tok0, rows, ymcol) in zip(Os, specs):
            DMA(lambda e, hf=hf, O=O, ydst=ydst, tok0=tok0, rows=rows: e.dma_start(out=ydst[tok0:tok0 + rows, :],
                                                                            in_=O.yo[0:rows, :]),
                [O.yot], [], dsem("d_y", O))

    groups = []
    if do_sample:
        groups.append(([OT[0]], [(xs_d, psm_d, ys_d, 0, NS, SEQ)]))
    ntile = n_st * 2
    for tt in range(0, ntile, 4):
        k_ = min(4, ntile - tt)
        groups.append((OT[:k_], [(x_d, p_d, y_d, (tt + k) * 128, 128, (tt + k) * 128) for k in range(k_)]))
    if groups:
        o_loads(*groups[0])
    for gi, (Os_, sp_) in enumerate(groups):
        nxt = (lambda gi=gi: o_loads(*groups[gi + 1])) if gi + 1 < len(groups) else None
        o_compute(Os_, sp_, nxt)
    S.barrier()

    with nc.allow_low_precision("bf16 matmul operands, fp32 accumulation"):
        with nc.allow_non_contiguous_dma("small strided state DMAs"):
            with nc.Block() as block:
                @block.tensor
                def _(e): S.replay("tensor", e, sems)
                @block.vector
                def _(e): S.replay("vector", e, sems)
                @block.scalar
                def _(e): S.replay("scalar", e, sems)
                @block.gpsimd
                def _(e): S.replay("gpsimd", e, sems)
                @block.sync
                def _(e): S.replay("sync", e, sems)
    es.close()
    return nc


_NC_CACHE = {}


def make_in_maps(inputs):
    f = lambda a: np.ascontiguousarray(np.asarray(a, dtype=np.float32))
    g = {k: f(v) for k, v in inputs.items()}
    rows = [g["norm_g"][0].reshape(8, 128), g["mu_shift"][0].reshape(13, 128), g["conv_w"][0].reshape(12, 128),
            g["w0"][0].reshape(4, 128), g["a0"][0].reshape(4, 128), g["k_k"][0].reshape(4, 128),
            g["k_a"][0].reshape(4, 128), g["r_k"][0].reshape(4, 128), g["ln_w"][0].reshape(4, 128),
            g["ln_b"][0].reshape(4, 128)]
    prm = f(np.concatenate(rows, axis=0).T)
    shared = {"prm": prm, "w_in": g["w_in"][0], "w_up": g["w_up"][0], "a_up": g["a_up"][0], "w_out": g["w_out"][0],
              "w_pg": g["w_pg"][0], "w_pp": g["w_pp"][0], "final_g": g["final_g"].reshape(1, D)}
    maps = []
    for c in range(NCORES):
        sl = slice(NS * c, NS * (c + 1))
        sc = g["state_conv"][0, sl]
        scT = f(sc.reshape(NS, 2, 4, 128).transpose(3, 2, 1, 0))
        ss = g["state_shift"][0, sl]
        ssT = f(ss.reshape(NS, 13, 128).transpose(2, 1, 0))
        m = dict(shared)
        m.update({"x": g["x_prompt"][c], "p": g["p_prompt"][0, c], "xs": g["x_sample"][sl, 0],
                  "psm": g["p_sample"][0, sl, 0], "scT": scT, "ssT": ssT,
                  "swkv": f(g["state_wkv"][0, sl].reshape(NS * 8, 4096))})
        maps.append(m)
    return maps


def assemble(results):
    y = np.stack([r["y"] for r in results], 0).astype(np.float32)
    ys = np.concatenate([r["ys"] for r in results], 0).reshape(NCORES * NS, 1, D).astype(np.float32)
    convp = np.stack([r["convp"].transpose(2, 1, 0).reshape(2, 512) for r in results], 0)[None].astype(np.float32)
    shiftp = np.stack([r["shiftp"].T.reshape(1664) for r in results], 0)[None].astype(np.float32)
    wkvp = np.stack([r["wkvp"].transpose(0, 2, 1) for r in results], 0)[None].astype(np.float32)
    convs = np.concatenate([r["convs"].transpose(3, 2, 1, 0).reshape(NS, 2, 512) for r in results], 0)[None]
    shifts = np.concatenate([r["shifts"].transpose(2, 1, 0).reshape(NS, 1664) for r in results], 0)[None]
    wkvs = np.concatenate([r["wkvs"].reshape(NS, 8, 64, 64) for r in results], 0)[None]
    return (y, ys, np.ascontiguousarray(convp), np.ascontiguousarray(shiftp), np.ascontiguousarray(wkvp),
            np.ascontiguousarray(convs.astype(np.float32)), np.ascontiguousarray(shifts.astype(np.float32)),
            np.ascontiguousarray(wkvs.astype(np.float32)))


def kernel(**inputs):
    if "nc" not in _NC_CACHE:
        _NC_CACHE["nc"] = build_nc()
    nc = _NC_CACHE["nc"]
    maps = make_in_maps(inputs)
    res = run_bass_kernel_spmd(nc, maps, core_ids=list(range(NCORES)))
    return assemble(res.results)
```

```python
import numpy as np
from contextlib import ExitStack
import concourse.bass as bass
import concourse.mybir as mybir
from concourse.bass_utils import run_bass_kernel_spmd

F32 = mybir.dt.float32
BF16 = mybir.dt.bfloat16
AF = mybir.ActivationFunctionType
ALU = mybir.AluOpType
AX = mybir.AxisListType

ENGS = ("tensor", "vector", "scalar", "gpsimd", "sync")
NCORES = 8
D = 1024
SEQ = 2048
NT = 256
NST = SEQ // NT
NS = 16
INW = 4224
CDEC = 0.6065306597126334
RMS_EPS = 1e-6
GN_EPS = 64e-5

G_, MU_, CW_, W0_, A0_, KK_, KA_, RK_, LW_, LB_ = 0, 8, 21, 33, 37, 41, 45, 49, 53, 57
NPRM_IN = 61
OMU_, OMKA_, GNE_, RME_ = 64, 77, 94, 95
NPRM = 96

DMA_SEMS = ["d_w1", "d_w2", "d_w3", "d_w4", "d_w5", "d_c1", "d_c2", "d_c3", "d_c4", "d_x", "d_g", "d_p", "d_y",
            "d_o", "d_s1", "d_s2", "d_l", "d_ws", "d_s3", "d_s4"]


class Sched:
    def __init__(self):
        self.ops = {e: [] for e in ENGS}
        self.count = {e: 0 for e in ENGS}
        self.dcount = {}
        self.know = {e: {} for e in ENGS}
        self.clock = {}
        self.last_write = {}
        self.readers = {}
        self.bank_tokens = {}
        self.bank_pending = {}
        self.all_events = {}
        self.nwaits = 0
        self.pending_nosig = {e: False for e in ENGS}

    def _need(self, eng, ev, waits):
        if ev is None:
            return
        s, v = ev
        if s == "tensor" and eng == "tensor":
            return
        if self.know[eng].get(s, 0) >= v:
            return
        if waits.get(s, 0) < v:
            waits[s] = v

    def _absorb(self, eng, waits):
        items = sorted(waits.items(), key=lambda kv: -len(self.clock.get(kv, ())))
        k = self.know[eng]
        kept = []
        for s, v in items:
            if k.get(s, 0) >= v:
                continue
            kept.append((s, v))
            for s2, v2 in self.clock.get((s, v), {}).items():
                if k.get(s2, 0) < v2:
                    k[s2] = v2
            if k.get(s, 0) < v:
                k[s] = v
        self.nwaits += len(kept)
        return kept

    def retire_bank(self, b):
        evs = []
        for t in self.bank_tokens.get(b, ()):
            if t in self.last_write:
                evs.append(self.last_write.pop(t))
            evs.extend(self.readers.pop(t, ()))
        self.bank_tokens[b] = set()
        self.bank_pending[b] = evs

    def _touch(self, t):
        if isinstance(t, tuple) and t and t[0] == "ps":
            b = t[1]
            if t not in self.bank_tokens.setdefault(b, set()):
                self.bank_tokens[b].add(t)
                if t not in self.last_write and t not in self.readers:
                    self.readers[t] = list(self.bank_pending.get(b, ()))

    def op(self, eng, fn, reads=(), writes=(), dma=None, sig=True):
        waits = {}
        for t in list(reads) + list(writes):
            self._touch(t)
        for r in reads:
            self._need(eng, self.last_write.get(r), waits)
            if isinstance(r, tuple) and r and r[0] == "ps":
                for u in self.bank_tokens.get(r[1], ()):
                    if u != r:
                        self._need(eng, self.last_write.get(u), waits)
                    for ev in self.readers.get(u, ()):
                        if ev[0] != eng:
                            self._need(eng, ev, waits)
        for w in writes:
            if isinstance(w, tuple) and w and w[0] == "ps":
                for u in self.bank_tokens.get(w[1], ()):
                    if u != w:
                        for ev in self.readers.get(u, ()):
                            self._need(eng, ev, waits)
        for w in writes:
            self._need(eng, self.last_write.get(w), waits)
            for ev in self.readers.get(w, ()):
                self._need(eng, ev, waits)
        kept = self._absorb(eng, waits)
        if dma is None and not sig:
            ev = (eng, self.count[eng] + 1)
            inc = None
            self.pending_nosig[eng] = True
        elif dma is None:
            self.count[eng] += 1
            ev = (eng, self.count[eng])
            inc = (eng, 1)
            self.pending_nosig[eng] = False
        else:
            self.dcount[dma] = self.dcount.get(dma, 0) + 16
            ev = (dma, self.dcount[dma])
            inc = (dma, 16)
        ck = dict(self.know[eng])
        ck[ev[0]] = ev[1]
        self.clock[ev] = ck
        self.all_events[ev[0]] = max(self.all_events.get(ev[0], 0), ev[1])
        for w in writes:
            self.last_write[w] = ev
            self.readers[w] = []
        for r in reads:
            if r not in writes:
                self.readers.setdefault(r, []).append(ev)
        self.ops[eng].append((kept, fn, inc))
        return ev

    def barrier(self, engs=ENGS):
        assert not any(self.pending_nosig.values()), "non-signalling op not followed by a signalling one"
        evs = list(self.all_events.items())
        for eng in engs:
            waits = {}
            for ev in evs:
                self._need(eng, ev, waits)
            kept = self._absorb(eng, waits)
            self.ops[eng].append((kept, None, None))

    def replay(self, eng, e, sems):
        for waits, fn, inc in self.ops[eng]:
            for s, v in waits:
                e.wait_ge(sems[s], v)
            if fn is not None:
                if inc is None:
                    fn(e)
                else:
                    fn(e).then_inc(sems[inc[0]], inc[1])


def build_nc(n_st=NST, do_sample=True):
    nc = bass.Bass("TRN2", target_bir_lowering=False)

    def din(name, shape, dt=F32):
        return nc.dram_tensor(name, list(shape), dt, kind="ExternalInput").ap()

    def dout(name, shape, dt=F32):
        return nc.dram_tensor(name, list(shape), dt, kind="ExternalOutput").ap()

    x_d = din("x", [SEQ, D]); p_d = din("p", [SEQ, 256])
    xs_d = din("xs", [NS, D]); psm_d = din("psm", [NS, 256])
    scT_d = din("scT", [128, 4, 2, NS])
    ssT_d = din("ssT", [128, 13, NS])
    swkv_d = din("swkv", [128, 4096])
    prm_d = din("prm", [128, NPRM_IN])
    win_d = din("w_in", [D, INW]); wup_d = din("w_up", [64, 512]); aup_d = din("a_up", [64, 512])
    wout_d = din("w_out", [D, D]); wpg_d = din("w_pg", [D, D]); wpp_d = din("w_pp", [256, D])
    fg_d = din("final_g", [1, D])

    y_d = dout("y", [SEQ, D]); ys_d = dout("ys", [NS, D])
    convp_d = dout("convp", [128, 4, 2])
    shiftp_d = dout("shiftp", [128, 13])
    wkvp_d = dout("wkvp", [8, 64, 64])
    convs_d = dout("convs", [128, 4, 2, NS])
    shifts_d = dout("shifts", [128, 13, NS])
    wkvs_d = dout("wkvs", [128, 4096])
    scr1_d = nc.dram_tensor("scr1", [NS, 8, 6, 64], F32, kind="Internal").ap()
    scr2_d = nc.dram_tensor("scr2", [NS, 8, 64], F32, kind="Internal").ap()
    ym_d = nc.dram_tensor("scr_ym", [128, 8, SEQ + 128], BF16, kind="Internal").ap()

    S = Sched()
    es = ExitStack()

    def sb(name, shape, dt=F32):
        return es.enter_context(nc.sbuf_tensor("s_" + name, list(shape), dt))

    win = sb("win", [128, 8, INW], BF16)
    wua = sb("wua", [128, 512], BF16)
    prm = sb("prm", [128, NPRM])
    ident = sb("ident", [128, 128], BF16)
    identf = sb("identf", [128, 128])
    blk = sb("blk", [128, 128])
    blkb = sb("blkb", [128, 128], BF16)
    blk64 = sb("blk64", [128, 128], BF16)
    blkrk = sb("blkrk", [128, 4, 128], BF16)
    maskA = sb("maskA", [128, 256], BF16)
    maskN = sb("maskN", [128, 128], BF16)
    scanm = sb("scanm", [128, NT])
    zlast = sb("zlast", [128, 13])
    uext = sb("uext", [128, 4, NT + 2])
    T32 = sb("T32", [128, 4, 128])
    T16 = sb("T16", [128, 4, 128], BF16)
    pcs = sb("pcs", [128, 4, 4])
    xnTf = [sb("xnTa", [128, 8 * (NT + 2)], BF16), sb("xnTb", [128, 8 * (NT + 2)], BF16)]
    xnT2 = [t_[:, :].rearrange("p (c n) -> p c n", c=8) for t_ in xnTf]
    CUR = {"x": 0}
    ymixT = sb("ymixT", [128, 8, NT], BF16)
    stat = sb("stat", [128, 32])
    thad = sb("thad", [128, NT], BF16)

    class TH:
        pass

    def mk_thread(g, ring, ybank):
        B = TH()
        B.g = g
        B.arena = sb(f"arena{g}", [128, 20, NT])
        B.xb = sb(f"xb{g}", [128, D], BF16)
        B.KR = sb(f"KR{g}", [128, 2, 2, NT], BF16)
        B.KB = sb(f"KB{g}", [128, 2, 2, NT], BF16)
        B.VT = sb(f"VT{g}", [128, 2, NT], BF16)
        B.bv = sb(f"bv{g}", [128, 2, NT])
        B.TK = sb(f"TK{g}", [128, 2, 4, 256], BF16)
        B.SC = sb(f"SC{g}", [128, 2, 4, 512], BF16)
        B.D7 = sb(f"D7{g}", [128, 7, 4, 128], BF16)
        B.dA = [B.D7[:, 0], B.D7[:, 1]]
        B.dB = [B.D7[:, 2], B.D7[:, 3]]
        B.dP = [B.D7[:, 4], B.D7[:, 5]]
        B.NB = B.D7[:, 6]
        B.W1 = sb(f"W1{g}", [128, 4, 64], BF16)
        B.U1b = sb(f"U1b{g}", [128, 2, 256], BF16)
        B.KPT = sb(f"KPT{g}", [128, 2, 2, 128], BF16)
        B.Y1T = sb(f"Y1T{g}", [128, 2, 2, 128])
        B.H1 = sb(f"H1{g}", [128, 8, 128])
        B.U0b = sb(f"U0b{g}", [128, 256], BF16)
        B.Tsum = sb(f"Tsum{g}", [128, 2, 128])
        B.ring = ring
        B.rs = {"ri": 0}
        B.ybank = ybank
        B.pbank = g
        return B

    psb = [es.enter_context(nc.psum_tensor(f"psb{i}", [128, 512], F32)) for i in range(6)]
    ptbs = [es.enter_context(nc.psum_tensor(f"ptb16_{i}", [128, 1024], BF16)) for i in range(2)]
    TH0 = mk_thread(0, [0, 1], 2)
    TH1 = mk_thread(1, [3, 4], 5)
    THS = [TH0, TH1]
    import copy as _copy
    CTH = []
    for B_ in THS:
        B_.ring4 = list(B_.ring[0:2]) + [B_.ybank, 6 + B_.g]
        B_.rs4 = {"ri": 0}
        Bc = _copy.copy(B_)
        Bc.ring = B_.ring4
        Bc.rs = B_.rs4
        CTH.append(Bc)
        Bw = _copy.copy(B_)
        Bw.ring = list(B_.ring[0:2]) + [B_.ybank, 6 + B_.g]
        Bw.rs = {"ri": 0}
        B_.wide = Bw
        Bn = _copy.copy(B_)
        Bn.ring = [B_.ring[0]]
        Bn.rs = {"ri": 0}
        B_.chain = Bn
        B_.ring3 = list(B_.ring[0:2]) + [6 + B_.g]
    WC = _copy.copy(TH0)
    WC.ring = [TH0.ring[1]]
    WC.rs = {"ri": 0}

    sem_names = list(ENGS) + DMA_SEMS + ["d_x1", "d_g1", "d_p1", "d_y1", "d_m0", "d_m1", "d_ym", "d_g2", "d_g3", "d_p2", "d_p3", "d_y2", "d_y3", "d_m2", "d_m3", "d_l2", "d_ws2", "d_s1b"] + [f"d_stg{i}" for i in range(6)]
    sems = {n: es.enter_context(nc.semaphore(n)) for n in sem_names}

    def TE(fn, r, w, sig=True): return S.op("tensor", fn, r, w, sig=sig)
    def VE(fn, r, w): return S.op("vector", fn, r, w)
    def AE(fn, r, w): return S.op("scalar", fn, r, w)
    def GE(fn, r, w): return S.op("gpsimd", fn, r, w)
    def DMA(fn, r, w, sem, q="sync"): return S.op(q, fn, r, w, dma=sem)

    gen = {"g": 0}

    oring = {"ri": 0}

    ptf32 = [pt_[:, :].bitcast(F32) for pt_ in ptbs]

    def psalloc(B):
        if getattr(B, "gring", False):
            b = oring["ri"] % 6
            oring["ri"] += 1
        else:
            b = B.ring[B.rs["ri"] % len(B.ring)]
            B.rs["ri"] += 1
        gen["g"] += 1
        S.retire_bank(b)
        return (psb[b] if b < 6 else ptf32[b - 6]), ("ps", b, gen["g"])

    def psy(B):
        gen["g"] += 1
        S.retire_bank(B.ybank)
        return psb[B.ybank], ("ps", B.ybank, gen["g"])

    def ptalloc(B):
        gen["g"] += 1
        vb = 6 + B.pbank
        S.retire_bank(vb)
        return ptbs[B.pbank], ("ps", vb, gen["g"])

    def pcol(c):
        return prm[:, c:c + 1]

    def ar(B, k, n=NT):
        return B.arena[:, k, 0:n]

    def at(B, k):
        return ("ar", B.g, k)

    def arb(B, k, n=NT):
        return B.arena[:, k, :].bitcast(BF16)[:, 0:n]

    def tk(B, *name):
        return (B.g,) + name

    cast_rr = {"i": 0}

    def load_cast(chunks, slots, tag):
        for idx, ch in enumerate(chunks):
            dst, src, dtok = ch[0], ch[1], ch[2]
            scl = ch[3] if len(ch) > 3 else None
            k = idx % len(slots)
            sap, stok = slots[k]
            W = dst.shape[-1]
            DMA(lambda e, sap=sap, src=src, W=W: e.dma_start(out=sap[:, 0:W], in_=src), [], [stok], f"d_stg{k}")
            if scl is not None:
                if cast_rr["i"] % 2 == 0:
                    AE(lambda e, dst=dst, sap=sap, W=W, scl=scl: e.mul(out=dst, in_=sap[:, 0:W], mul=scl),
                       [stok, "prm"], [dtok])
                else:
                    VE(lambda e, dst=dst, sap=sap, W=W, scl=scl: e.tensor_scalar(out=dst, in0=sap[:, 0:W], scalar1=scl,
                                                                                 scalar2=None, op0=ALU.mult),
                       [stok, "prm"], [dtok])
            elif cast_rr["i"] % 2 == 0:
                AE(lambda e, dst=dst, sap=sap, W=W: e.copy(out=dst, in_=sap[:, 0:W]), [stok], [dtok])
            else:
                VE(lambda e, dst=dst, sap=sap, W=W: e.tensor_copy(out=dst, in_=sap[:, 0:W]), [stok], [dtok])
            cast_rr["i"] += 1

    DMA(lambda e: e.dma_start(out=prm[:, 0:NPRM_IN], in_=prm_d), [], ["prm"], "d_c1")

    tmpc = TH1.arena[:, 15, :]
    GE(lambda e: e.memset(identf[:], 1.0), [], ["identf"])
    GE(lambda e: e.affine_select(out=identf[:], in_=identf[:], pattern=[[-1, 128]], compare_op=ALU.is_equal,
                                 fill=0.0, base=0, channel_multiplier=1), ["identf"], ["identf"])
    VE(lambda e: e.tensor_copy(out=ident[:], in_=identf[:]), ["identf"], ["ident"])
    GE(lambda e: e.memset(blk[:], 0.0), [], ["blk"])
    GE(lambda e: e.memset(blk[0:64, 0:64], 1.0), ["blk"], ["blk"])
    GE(lambda e: e.memset(blk[64:128, 64:128], 1.0), ["blk"], ["blk"])
    VE(lambda e: e.tensor_scalar(out=blk64[:], in0=blk[:], scalar1=1.0 / 64, scalar2=None, op0=ALU.mult),
       ["blk"], ["blk64"])
    VE(lambda e: e.tensor_copy(out=blkb[:], in_=blk[:]), ["blk"], ["blkb"])
    GE(lambda e: e.memset(prm[:, GNE_:GNE_ + 1], GN_EPS), ["prm"], ["prm"])
    GE(lambda e: e.memset(prm[:, RME_:RME_ + 1], RMS_EPS), ["prm"], ["prm"])
    for j in range(4):
        VE(lambda e, j=j: e.tensor_scalar(out=blkrk[:, j, :], in0=blk[:], scalar1=pcol(RK_ + j), scalar2=None,
                                          op0=ALU.mult), ["blk", "prm"], ["blkrk"])
    GE(lambda e: e.memset(tmpc[:], 1.0), [], ["tmpc"])
    GE(lambda e: e.affine_select(out=tmpc[:, 0:128], in_=tmpc[:, 0:128], pattern=[[1, 128]], compare_op=ALU.is_ge,
                                 fill=0.0, base=-1, channel_multiplier=-1), ["tmpc"], ["tmpc"])
    GE(lambda e: e.affine_select(out=tmpc[:, 128:256], in_=tmpc[:, 128:256], pattern=[[1, 128]],
                                 compare_op=ALU.is_ge, fill=0.0, base=0, channel_multiplier=-1),
       ["tmpc"], ["tmpc"])
    GE(lambda e: e.memset(tmpc[0:64, 64:128], 0.0), ["tmpc"], ["tmpc"])
    GE(lambda e: e.memset(tmpc[0:64, 192:256], 0.0), ["tmpc"], ["tmpc"])
    VE(lambda e: e.tensor_copy(out=maskA[:], in_=tmpc[:]), ["tmpc"], ["maskA"])
    GE(lambda e: e.memset(tmpc[:, 0:128], 1.0), ["tmpc"], ["tmpc"])
    GE(lambda e: e.affine_select(out=tmpc[:, 0:128], in_=tmpc[:, 0:128], pattern=[[-1, 128]],
                                 compare_op=ALU.is_ge, fill=0.0, base=-1, channel_multiplier=1),
       ["tmpc"], ["tmpc"])
    GE(lambda e: e.memset(tmpc[64:128, 0:64], 0.0), ["tmpc"], ["tmpc"])
    VE(lambda e: e.tensor_copy(out=maskN[:], in_=tmpc[:, 0:128]), ["tmpc"], ["maskN"])
    GE(lambda e: e.memset(scanm[:], 1.0), [], ["scanm"])
    GE(lambda e: e.memset(scanm[:].rearrange("p (c t) -> p c t", t=64)[:, :, 0:1], 0.0), ["scanm"], ["scanm"])
    VE(lambda e: e.tensor_scalar(out=prm[:, OMU_:OMU_ + 13], in0=prm[:, MU_:MU_ + 13], scalar1=-1.0, scalar2=1.0,
                                 op0=ALU.mult, op1=ALU.add), ["prm"], ["prm"])
    VE(lambda e: e.tensor_scalar(out=prm[:, OMKA_:OMKA_ + 4], in0=prm[:, KA_:KA_ + 4], scalar1=-1.0, scalar2=1.0,
                                 op0=ALU.mult, op1=ALU.add), ["prm"], ["prm"])
    GE(lambda e: e.memset(zlast[:], 0.0), [], [("zl", q) for q in range(13)])
    GE(lambda e: e.memset(uext[:], 0.0), [], [("u", j) for j in range(4)])
    GE(lambda e: e.memset(T32[:], 0.0), [], [("T", j) for j in range(4)])
    GE(lambda e: e.memset(T16[:], 0.0), [], [("T16", j) for j in range(4)])
    DMA(lambda e: e.dma_start(out=wua[0:64, :], in_=wup_d), [], ["wua"], "d_w2", q="gpsimd")
    DMA(lambda e: e.dma_start(out=wua[64:128, :], in_=aup_d), [], ["wua"], "d_w2", q="gpsimd")
    S.barrier(("gpsimd", "vector", "scalar"))
    stg = []
    for g_ in range(2):
        fl = THS[g_].arena[:].rearrange("p a b -> p (a b)")
        for k_ in range(3):
            stg.append((fl[:, k_ * 1056:(k_ + 1) * 1056], ("stg", g_, k_)))
    chunks = []
    for cb in range(0, INW, 1056):
        for c in range(8):
            chunks.append((win[:, c, cb:cb + 1056], win_d[c * 128:(c + 1) * 128, cb:cb + 1056], "win",
                           pcol(G_ + c)))
    load_cast(chunks, stg, "win")
    S.barrier()

    def p1_gen(Bs, xsrc, tok0s, rows, col0s, par, d7stage=False, carry=None):
        if d7stage:
            xts = {B.g: B.D7[:, 0:4].rearrange("p a b c -> p (a b c)").bitcast(F32) for B in Bs}
            xtt = {B.g: [tk(B, "dA0"), tk(B, "dA1"), tk(B, "dB0"), tk(B, "dB1")] for B in Bs}
        else:
            xts = {B.g: B.arena[:, 12:16, :].rearrange("p a b -> p (a b)") for B in Bs}
            xtt = {B.g: [at(B, 12), at(B, 13), at(B, 14), at(B, 15)] for B in Bs}
        xdst = xnT2[par]
        for B, tok0 in zip(Bs, tok0s):
            DMA(lambda e, B=B, tok0=tok0: e.dma_start(out=xts[B.g][0:rows, :], in_=xsrc[tok0:tok0 + rows, :]),
                [], xtt[B.g], f"d_x{B.g}" if B.g else "d_x")
        yield
        for B in Bs:
            so = 8 * B.g
            AE(lambda e, B=B, so=so: e.activation(out=B.xb[0:rows, :], in_=xts[B.g][0:rows, :], func=AF.Square,
                                                  accum_out=stat[0:rows, so:so + 1]),
               xtt[B.g], [tk(B, "xb"), tk(B, "st0")])
        yield
        for B in Bs:
            so = 8 * B.g
            AE(lambda e, so=so: e.activation(out=stat[0:rows, so + 1:so + 2], in_=stat[0:rows, so:so + 1],
                                             func=AF.Ln, scale=1.0 / D, bias=prm[0:rows, RME_:RME_ + 1]),
               [tk(B, "st0"), "prm"], [tk(B, "st1")])
        for B in Bs:
            so = 8 * B.g
            AE(lambda e, so=so: e.activation(out=stat[0:rows, so + 2:so + 3], in_=stat[0:rows, so + 1:so + 2],
                                             func=AF.Exp, scale=-0.5), [tk(B, "st1")], [tk(B, "st2")])
        yield
        for B in Bs:
            so = 8 * B.g
            VE(lambda e, B=B, so=so: e.tensor_scalar(out=B.xb[0:rows, :], in0=xts[B.g][0:rows, :],
                                                     scalar1=stat[0:rows, so + 2:so + 3], scalar2=None, op0=ALU.mult),
               xtt[B.g] + [tk(B, "st2"), tk(B, "xb")], [tk(B, "xb")])
        yield
        pts = {}
        for B in Bs:
            pt, tk_ = ptalloc(B)
            t = tk_ + ("x",)
            pts[B.g] = (pt, t)
            for c in range(8):
                TE(lambda e, c=c, pt=pt, B=B: e.transpose(pt[:, c * 128:c * 128 + rows],
                                                          B.xb[0:rows, c * 128:(c + 1) * 128],
                                                          ident[0:rows, 0:rows]), [tk(B, "xb"), "ident"], [t], sig=(c == 7))
        yield
        for B, col0 in zip(Bs, col0s):
            pt, t = pts[B.g]
            AE(lambda e, pt=pt, col0=col0: e.copy(out=xdst[:, :, 1 + col0:1 + col0 + rows],
                                                  in_=pt[:, :].rearrange("p (c t) -> p c t", t=128)[:, :, 0:rows]),
               [t], [("xnT", par, B.g)])
            yield
        if carry == "zero":
            GE(lambda e: e.memset(xdst[:, :, 0:1], 0.0), [], [("xnT", par, "c")])
        elif carry == "prev":
            GE(lambda e: e.tensor_copy(out=xdst[:, :, 0:1], in_=xnT2[1 - par][:, :, NT:NT + 1]),
               [("xnT", 1 - par, 1)], [("xnT", par, "c")])
        yield

    def load_norm_T(Bs, xsrc, tok0s, rows, col0s, par=0, carry=None):
        for _ in p1_gen(Bs, xsrc, tok0s, rows, col0s, par, carry=carry):
            pass

    def projx(B, m, n, par=None, shift=False):
        if par is None:
            par = CUR["x"]
        xa = xnT2[par]
        xnt = [("xnT", par, 0), ("xnT", par, 1)]
        c0, nn = (0, n + 1) if shift else (1, n)
        if shift:
            xnt = xnt + [("xnT", par, "c")]
        pz, tok = psalloc(B)
        t = tok + ("z",)
        for c in range(8):
            TE(lambda e, c=c, pz=pz, xa=xa: e.matmul(pz[:, 0:nn], lhsT=win[:, c, m * 128:(m + 1) * 128],
                                                     rhs=xa[:, c, c0:c0 + nn], start=(c == 0), stop=(c == 7)),
               ["win"] + xnt, [t], sig=(c == 7))
        return pz, t

    def conv_gen(Bs, jf, n, sample, scT=None, convo=None, slots=(0, 1, 2)):
        zc, cv, sl = slots
        P = {}
        for B in Bs:
            P[B.g] = projx(B, 4 + jf(B), n)
        yield
        for B in Bs:
            pz, t = P[B.g]
            AE(lambda e, pz=pz, B=B: e.copy(out=ar(B, zc, n), in_=pz[:, 0:n]), [t], [at(B, zc)])
        yield
        for B in Bs:
            P[B.g] = projx(B, 8 + jf(B), n)
        yield
        if not sample:
            for B in Bs:
                pz, t = P[B.g]
                j = jf(B)
                VE(lambda e, pz=pz, j=j, B=B: e.tensor_tensor(out=uext[:, j, 2:2 + n], in0=ar(B, zc, n), in1=pz[:, 0:n],
                                                              op=ALU.mult), [t, at(B, zc)], [("u", j)])
            yield
            for B in Bs:
                j = jf(B)
                VE(lambda e, j=j, B=B: e.tensor_scalar(out=ar(B, cv, n), in0=uext[:, j, 0:n], scalar1=pcol(CW_ + j),
                                                       scalar2=None, op0=ALU.mult), [("u", j), "prm"], [at(B, cv)])
            yield
            for tap in (1, 2):
                for B in Bs:
                    j = jf(B)
                    VE(lambda e, j=j, B=B, tap=tap: e.scalar_tensor_tensor(
                        out=ar(B, cv, n), in0=uext[:, j, tap:n + tap], scalar=pcol(CW_ + 4 * tap + j),
                        in1=ar(B, cv, n), op0=ALU.mult, op1=ALU.add), [("u", j), "prm", at(B, cv)], [at(B, cv)])
                yield
            for B in Bs:
                j = jf(B)
                GE(lambda e, j=j: e.tensor_copy(out=uext[:, j, 0:2], in_=uext[:, j, n:n + 2]), [("u", j)], [("u", j)])
            yield
        else:
            for B in Bs:
                pz, t = P[B.g]
                j = jf(B)
                VE(lambda e, pz=pz, j=j, B=B: e.tensor_tensor(out=convo[:, j, 1, :], in0=ar(B, zc, n), in1=pz[:, 0:n],
                                                              op=ALU.mult), [t, at(B, zc)], [("cvo", j)])
                GE(lambda e, j=j: e.tensor_copy(out=convo[:, j, 0, :], in_=scT[:, j, 1, :]), ["scT"], [("cvo0", j)])
                VE(lambda e, j=j, B=B: e.tensor_scalar(out=ar(B, cv, n), in0=scT[:, j, 0, :], scalar1=pcol(CW_ + j),
                                                       scalar2=None, op0=ALU.mult), ["scT", "prm"], [at(B, cv)])
                VE(lambda e, j=j, B=B: e.scalar_tensor_tensor(out=ar(B, cv, n), in0=scT[:, j, 1, :],
                                                              scalar=pcol(CW_ + 4 + j), in1=ar(B, cv, n),
                                                              op0=ALU.mult, op1=ALU.add),
                   ["scT", "prm", at(B, cv)], [at(B, cv)])
                VE(lambda e, j=j, B=B: e.scalar_tensor_tensor(out=ar(B, cv, n), in0=convo[:, j, 1, :],
                                                              scalar=pcol(CW_ + 8 + j), in1=ar(B, cv, n),
                                                              op0=ALU.mult, op1=ALU.add),
                   [("cvo", j), "prm", at(B, cv)], [at(B, cv)])
        for B in Bs:
            P[B.g] = projx(B, jf(B), n)
        yield
        for B in Bs:
            pz, t = P[B.g]
            VE(lambda e, pz=pz, B=B: e.tensor_tensor(out=ar(B, cv, n), in0=ar(B, cv, n), in1=pz[:, 0:n], op=ALU.mult),
               [t, at(B, cv)], [at(B, cv)])
        yield
        for B in Bs:
            P[B.g] = projx(B, 12 + jf(B), n)
        yield
        for B in Bs:
            pz, t = P[B.g]
            AE(lambda e, pz=pz, B=B: e.activation(out=ar(B, sl, n), in_=pz[:, 0:n], func=AF.Sigmoid), [t], [at(B, sl)])
        yield
        for B in Bs:
            pz, t = P[B.g]
            VE(lambda e, pz=pz, B=B: e.tensor_tensor(out=ar(B, cv, n), in0=ar(B, cv, n), in1=pz[:, 0:n], op=ALU.mult),
               [t, at(B, cv)], [at(B, cv)])
        yield
        for B in Bs:
            j = jf(B)
            GE(lambda e, j=j, B=B: e.tensor_tensor(out=ymixT[:, j, 0:n], in0=ar(B, cv, n), in1=ar(B, sl, n),
                                                   op=ALU.mult), [at(B, cv), at(B, sl)], [("ym", j)])
        yield

    def conv_branch(Bs, jf, n, sample, scT=None, convo=None):
        for _ in conv_gen(Bs, jf, n, sample, scT, convo):
            pass

    FILL = {"gen": None, "ok": False, "busy": False}

    def fill():
        if FILL["ok"] and FILL["gen"] is not None and not FILL["busy"]:
            FILL["busy"] = True
            try:
                next(FILL["gen"])
                if FILL.get("k2"):
                    next(FILL["gen"])
            except StopIteration:
                FILL["gen"] = None
            FILL["busy"] = False

    def flush_fill():
        if FILL["gen"] is not None:
            FILL["busy"] = True
            for _ in FILL["gen"]:
                pass
            FILL["busy"] = False
            FILL["gen"] = None

    def ck(C):
        return (C.g, getattr(C, "sb", 0))

    def shift_chunk(Cs, qf, n, dkf, sample, ssT=None, zso=None, par=None, save_last=False):
        P = {}
        o1 = 0 if sample else 1
        for C in Cs:
            P[ck(C)] = projx(C, 16 + qf(C), n, par, shift=not sample)
        for C in Cs:
            pz, t = P[ck(C)]
            q, dk = qf(C), dkf(C)
            AE(lambda e, pz=pz, C=C, q=q, dk=dk: e.mul(out=ar(C, dk, n), in_=pz[:, o1:o1 + n], mul=pcol(OMU_ + q)),
               [t, "prm"], [at(C, dk)])
        if not sample:
            for C in Cs:
                pz, t = P[ck(C)]
                q, dk = qf(C), dkf(C)
                VE(lambda e, pz=pz, C=C, q=q, dk=dk: e.scalar_tensor_tensor(
                    out=ar(C, dk, n), in0=pz[:, 0:n], scalar=pcol(MU_ + q), in1=ar(C, dk, n),
                    op0=ALU.mult, op1=ALU.add), [t, "prm", at(C, dk)], [at(C, dk)])
            if save_last:
                for C in Cs:
                    pz, t = P[ck(C)]
                    q, dk = qf(C), dkf(C)
                    AE(lambda e, pz=pz, q=q: e.copy(out=zlast[:, q:q + 1], in_=pz[:, n:n + 1]), [t, at(C, dk)],
                       [("zl", q)])
            fill()
            fill()
        else:
            for C in Cs:
                pz, t = P[ck(C)]
                q, dk = qf(C), dkf(C)
                VE(lambda e, C=C, q=q, dk=dk: e.scalar_tensor_tensor(out=ar(C, dk, n), in0=ssT[:, q, :],
                                                                     scalar=pcol(MU_ + q), in1=ar(C, dk, n),
                                                                     op0=ALU.mult, op1=ALU.add),
                   ["ssT", "prm", at(C, dk)], [at(C, dk)])
                AE(lambda e, pz=pz, q=q: e.copy(out=zso[:, q, :], in_=pz[:, 0:n]), [t], [("zso", q)])

    WD = 19

    def thad_ops(B, n):
        AE(lambda e: e.activation(out=thad[0:64, 0:n], in_=ar(B, WD, n)[0:64, :], func=AF.Tanh), [at(B, WD)], ["thad"])
        GE(lambda e: e.tensor_copy(out=thad[64:128, 0:n], in_=ar(B, WD, n)[64:128, :]), [at(B, WD), "thad"], ["thad"])

    def wdad_chunk(B, n, sample, ssT=None, zso=None, par=None, save_last=False, defer=False):
        shift_chunk([B], lambda C: 12, n, lambda C: WD, sample, ssT, zso, par, save_last)
        if not defer:
            thad_ops(B, n)

    ZR, ZK, ZV, SG, CSG, PIN, AA, S0, S1 = 0, 1, 2, 3, 4, 5, 6, 7, 8
    CEX = PEX = SG
    PINV = CSG
    LASTF = {"v": False}

    def sl(C, k):
        return getattr(C, "sb", 0) + k

    def pair_pre(Cs, jf, n, sample, bvf, ssT=None, zso=None):
        FILL["ok"] = True
        sv = LASTF["v"] and not sample
        shift_chunk(Cs, lambda C: 4 + jf(C), n, lambda C: sl(C, ZK), sample, ssT, zso, save_last=sv)
        P = {}

        def A(C, k):
            return ar(C, sl(C, k), n)

        def Ab(C, k):
            return arb(C, sl(C, k), n)

        def T(C, k):
            return at(C, sl(C, k))

        def each(f):
            for C in Cs:
                f(C, jf(C))
            fill()

        def mm1(C, j, lhs, rhs, rtok, nm):
            pz, t = psalloc(C)
            t = t + (nm,)
            TE(lambda e, pz=pz: e.matmul(pz[:, 0:n], lhsT=lhs, rhs=rhs, start=True, stop=True), rtok, [t])
            P[ck(C)] = (pz, t)

        each(lambda C, j: GE(lambda e: e.tensor_scalar(out=A(C, S0), in0=A(C, ZK), scalar1=pcol(KK_ + j),
                                                       scalar2=0.0, op0=ALU.mult, op1=ALU.add),
                             [T(C, ZK), "prm"], [T(C, S0)]))
        each(lambda C, j: AE(lambda e: e.activation(out=Ab(C, S1), in_=A(C, ZK), func=AF.Square,
                                                    scale=pcol(KK_ + j)), [T(C, ZK), "prm"], [T(C, S1)]))
        each(lambda C, j: mm1(C, j, blkb[:], Ab(C, S1), ["blkb", T(C, S1)], "n"))
        each(lambda C, j: VE(lambda e, pz=P[ck(C)][0]: e.tensor_scalar(out=A(C, S1), in0=pz[:, 0:n], scalar1=1e-24,
                                                                       scalar2=None, op0=ALU.max),
                             [P[ck(C)][1], T(C, S1)], [T(C, S1)]))
        shift_chunk(Cs, lambda C: jf(C), n, lambda C: sl(C, ZR), sample, ssT, zso, save_last=sv)
        shift_chunk(Cs, lambda C: 8 + jf(C), n, lambda C: sl(C, ZV), sample, ssT, zso, save_last=sv)
        each(lambda C, j: mm1(C, j, wua[0:64, j * 128:(j + 1) * 128], thad[0:64, 0:n], ["wua", "thad"], "w"))
        each(lambda C, j: AE(lambda e, pz=P[ck(C)][0]: e.activation(out=A(C, SG), in_=pz[:, 0:n], func=AF.Sigmoid,
                                                                    bias=pcol(W0_ + j)), [P[ck(C)][1], "prm"],
                             [T(C, SG)]))
        each(lambda C, j: mm1(C, j, wua[64:128, j * 128:(j + 1) * 128], thad[64:128, 0:n], ["wua", "thad"], "a"))
        each(lambda C, j: AE(lambda e, pz=P[ck(C)][0]: e.activation(out=A(C, AA), in_=pz[:, 0:n], func=AF.Sigmoid,
                                                                    bias=pcol(A0_ + j)), [P[ck(C)][1], "prm"],
                             [T(C, AA)]))
        def bvd(C):
            return bvf(C)[0]

        def bvb(C):
            return bvf(C)[0].bitcast(BF16)[:, 0:n]
        each(lambda C, j: VE(lambda e: e.scalar_tensor_tensor(out=bvd(C), in0=A(C, ZK), scalar=pcol(KA_ + j),
                                                              in1=A(C, AA), op0=ALU.mult, op1=ALU.mult),
                             [T(C, ZK), T(C, AA), "prm"], [bvf(C)[1]]))
        each(lambda C, j: VE(lambda e: e.scalar_tensor_tensor(out=A(C, ZK), in0=A(C, ZK), scalar=pcol(OMKA_ + j),
                                                              in1=bvd(C), op0=ALU.mult, op1=ALU.add),
                             [T(C, ZK), bvf(C)[1], "prm"], [T(C, ZK)]))
        each(lambda C, j: GE(lambda e: e.tensor_tensor(out=bvb(C), in0=A(C, ZR), in1=A(C, ZK),
                                                       op=ALU.mult), [T(C, ZR), T(C, ZK), bvf(C)[1]], [bvf(C)[1]]))
        each(lambda C, j: mm1(C, j, blkrk[:, j, :], bvb(C), ["blkrk", bvf(C)[1]], "b"))
        each(lambda C, j: VE(lambda e, pz=P[ck(C)][0], d=bvf(C)[0]: e.tensor_tensor(out=d, in0=A(C, ZV),
                                                                                    in1=pz[:, 0:n], op=ALU.mult),
                             [P[ck(C)][1], T(C, ZV), bvf(C)[1]], [bvf(C)[1]]))
        if not sample:
            each(lambda C, j: VE(lambda e: e.tensor_tensor_scan(out=A(C, CSG), data0=scanm[:, 0:n],
                                                                data1=A(C, SG), initial=0.0, op0=ALU.mult,
                                                                op1=ALU.add), ["scanm", T(C, SG)], [T(C, CSG)]))
            each(lambda C, j: GE(lambda e: e.tensor_tensor(out=A(C, CEX), in0=A(C, CSG), in1=A(C, SG),
                                                           op=ALU.subtract), [T(C, CSG), T(C, SG)], [T(C, CEX)]))
        if FILL.get("flush_at_ln"):
            flush_fill()
        FILL["ok"] = False
        each(lambda C, j: AE(lambda e: e.activation(out=A(C, S1), in_=A(C, S1), func=AF.Ln),
                             [T(C, S1)], [T(C, S1)]))
        each(lambda C, j: AE(lambda e: e.activation(out=A(C, S1), in_=A(C, S1), func=AF.Exp, scale=-0.5),
                             [T(C, S1)], [T(C, S1)]))
        if not sample:
            each(lambda C, j: AE(lambda e: e.activation(out=A(C, PIN), in_=A(C, CSG), func=AF.Exp,
                                                        scale=-CDEC), [T(C, CSG)], [T(C, PIN)]))
            each(lambda C, j: AE(lambda e: e.activation(out=A(C, PEX), in_=A(C, CEX), func=AF.Exp,
                                                        scale=-CDEC), [T(C, CEX)], [T(C, PEX)]))
            each(lambda C, j: AE(lambda e: e.activation(out=A(C, PINV), in_=A(C, CSG), func=AF.Exp,
                                                        scale=CDEC), [T(C, CSG), T(C, PIN)], [T(C, PINV)]))
            each(lambda C, j: GE(lambda e: e.tensor_copy(
                out=pcs[:, j, :], in_=A(C, PIN).rearrange("p (c t) -> p c t", t=64)[:, :, 63]),
                [T(C, PIN)], [("pcs", j)]))
        else:
            each(lambda C, j: AE(lambda e: e.activation(out=A(C, PIN), in_=A(C, SG), func=AF.Exp,
                                                        scale=-CDEC), [T(C, SG)], [T(C, PIN)]))
        each(lambda C, j: VE(lambda e: e.tensor_tensor(out=A(C, S0), in0=A(C, S0), in1=A(C, S1),
                                                       op=ALU.mult), [T(C, S0), T(C, S1)], [T(C, S0)]))
        each(lambda C, j: GE(lambda e: e.tensor_tensor(out=A(C, S1), in0=A(C, S0), in1=A(C, AA),
                                                       op=ALU.mult), [T(C, S0), T(C, AA), T(C, S1)], [T(C, S1)]))
        if not sample:
            each(lambda C, j: VE(lambda e: e.tensor_tensor(out=C.KR[:, C.ppi, 1, 0:n], in0=A(C, ZR), in1=A(C, PIN),
                                                           op=ALU.mult), [T(C, ZR), T(C, PIN)],
                                 [tk(C, "KR1", C.ppi)]))
            each(lambda C, j: VE(lambda e: e.tensor_tensor(out=C.KR[:, C.ppi, 0, 0:n], in0=A(C, S0),
                                                           in1=A(C, PEX), op=ALU.mult),
                                 [T(C, S0), T(C, PEX)], [tk(C, "KR0", C.ppi)]))
            each(lambda C, j: VE(lambda e: e.tensor_tensor(out=C.KB[:, C.ppi, 0, 0:n], in0=A(C, ZK),
                                                           in1=A(C, PINV), op=ALU.mult),
                                 [T(C, ZK), T(C, PINV)], [tk(C, "KB0", C.ppi)]))
            each(lambda C, j: VE(lambda e: e.scalar_tensor_tensor(out=C.KB[:, C.ppi, 1, 0:n], in0=A(C, S1),
                                                                  scalar=-1.0, in1=A(C, PINV), op0=ALU.mult,
                                                                  op1=ALU.mult), [T(C, S1), T(C, PINV)],
                                 [tk(C, "KB1", C.ppi)]))
            each(lambda C, j: GE(lambda e: e.tensor_copy(out=C.VT[:, C.ppi, 0:n], in_=A(C, ZV)), [T(C, ZV)],
                                 [tk(C, "VT", C.ppi)]))
        return ZR, PIN, ZK, ZV, S0, S1

    def post_pair(Bs, jf, n, ysf, bvf):
        P = {}
        Y = {}

        def so(B):
            return getattr(B, "soff", 0)

        def pk(B):
            return (B.g, getattr(B, "soff", 0))
        for B in Bs:
            ysrc, ytok = ysf(B)
            if ytok[0] == "ps":
                VE(lambda e, src=ysrc, B=B: e.tensor_copy(out=ar(B, 4 + so(B), n), in_=src), [ytok], [at(B, 4 + so(B))])
                ysrc, ytok = ar(B, 4 + so(B), n), at(B, 4 + so(B))
            Y[pk(B)] = (ysrc, ytok)

        def mm1(B, lhs, rhs, rtok, nm):
            pz, t = psalloc(B)
            t = t + (nm,)
            TE(lambda e, pz=pz: e.matmul(pz[:, 0:n], lhsT=lhs, rhs=rhs, start=True, stop=True), rtok, [t])
            P[pk(B)] = (pz, t)

        for B in Bs:
            AE(lambda e, B=B, ys=Y[pk(B)][0]: e.copy(out=arb(B, 6 + so(B), n), in_=ys), [Y[pk(B)][1]],
               [at(B, 6 + so(B))])
        for B in Bs:
            mm1(B, blk64[:], arb(B, 6 + so(B), n), ["blk64", at(B, 6 + so(B))], "m")
        for B in Bs:
            VE(lambda e, pz=P[pk(B)][0], ys=Y[pk(B)][0], B=B: e.tensor_tensor(out=ar(B, 5 + so(B), n), in0=ys, in1=pz[:, 0:n],
                                                                          op=ALU.subtract),
               [P[pk(B)][1], Y[pk(B)][1]], [at(B, 5 + so(B))])
        for B in Bs:
            AE(lambda e, B=B: e.activation(out=arb(B, 6 + so(B), n), in_=ar(B, 5 + so(B), n), func=AF.Square),
               [at(B, 5 + so(B)), at(B, 6 + so(B))], [at(B, 6 + so(B))])
        for B in Bs:
            mm1(B, blk64[:], arb(B, 6 + so(B), n), ["blk64", at(B, 6 + so(B))], "v")
        for B in Bs:
            AE(lambda e, pz=P[pk(B)][0], B=B: e.activation(out=ar(B, 6 + so(B), n), in_=pz[:, 0:n], func=AF.Ln,
                                                         bias=pcol(GNE_)), [P[pk(B)][1], "prm", at(B, 6 + so(B))], [at(B, 6 + so(B))])
        G = {}
        for B in Bs:
            G[pk(B)] = projx(B, 29 + jf(B), n)
        for B in Bs:
            AE(lambda e, B=B: e.activation(out=ar(B, 6 + so(B), n), in_=ar(B, 6 + so(B), n), func=AF.Exp, scale=-0.5),
               [at(B, 6 + so(B))], [at(B, 6 + so(B))])
        for B in Bs:
            AE(lambda e, pz=G[pk(B)][0], B=B: e.activation(out=ar(B, 4 + so(B), n), in_=pz[:, 0:n], func=AF.Silu),
               [G[pk(B)][1], at(B, 4 + so(B)), Y[pk(B)][1]], [at(B, 4 + so(B))])
        for B in Bs:
            VE(lambda e, B=B: e.tensor_tensor(out=ar(B, 7 + so(B), n), in0=ar(B, 5 + so(B), n), in1=ar(B, 6 + so(B), n), op=ALU.mult),
               [at(B, 5 + so(B)), at(B, 6 + so(B))], [at(B, 7 + so(B))])
        for B in Bs:
            j = jf(B)
            GE(lambda e, B=B, j=j: e.tensor_scalar(out=ar(B, 7 + so(B), n), in0=ar(B, 7 + so(B), n), scalar1=pcol(LW_ + j),
                                                   scalar2=pcol(LB_ + j), op0=ALU.mult, op1=ALU.add),
               [at(B, 7 + so(B)), "prm"], [at(B, 7 + so(B))])
        for B in Bs:
            GE(lambda e, B=B, bv_=bvf(B)[0]: e.tensor_tensor(out=ar(B, 7 + so(B), n), in0=ar(B, 7 + so(B), n), in1=bv_, op=ALU.add),
               [at(B, 7 + so(B)), bvf(B)[1]], [at(B, 7 + so(B))])
        for B in Bs:
            j = jf(B)
            VE(lambda e, B=B, j=j: e.tensor_tensor(out=ymixT[:, 4 + j, 0:n], in0=ar(B, 7 + so(B), n), in1=ar(B, 4 + so(B), n),
                                                   op=ALU.mult), [at(B, 7 + so(B)), at(B, 4 + so(B))], [("ym", 4 + j)])

    def sample_group():
        n = NS
        B = TH0
        A1 = TH1.arena
        SQs = [A1[:, 0:3, :].rearrange("p a b -> p (a b)").rearrange("p (q c) -> p q c", c=128),
               A1[:, 11:14, :].rearrange("p a b -> p (a b)").rearrange("p (q c) -> p q c", c=128)]
        yq = A1[:, 3:5, :].rearrange("p a b -> p (a b)")
        VQ = A1[:, 5:7, :].rearrange("p a b -> p (a b)")[:, 0:384].rearrange("p (q c) -> p q c", c=64)
        scT = A1[:, 7, 0:128].rearrange("p (j t b) -> p j t b", j=4, t=2)
        convo = A1[:, 7, 128:256].rearrange("p (j t b) -> p j t b", j=4, t=2)
        ssT = A1[:, 8, 0:208].rearrange("p (q b) -> p q b", b=NS)
        zso = A1[:, 9, 0:208].rearrange("p (q b) -> p q b", b=NS)
        bvS = A1[:, 10, 0:64].rearrange("p (j b) -> p j b", b=NS)
        sa = A1[:, 10, 64:128]
        yv = A1[:, 10, 128:192]
        DMA(lambda e: e.dma_start(out=scT, in_=scT_d), [], ["scT"], "d_c3")
        DMA(lambda e: e.dma_start(out=ssT, in_=ssT_d), [], ["ssT"], "d_c4")
        load_norm_T([B], xs_d, [0], NS, [0])
        wdad_chunk(B, n, True, ssT, zso)
        S.barrier()
        SBs = []
        for k_ in range(4):
            Bk = TH()
            Bk.g = 10 + k_
            Bk.arena = TH0.arena[:, :, 64 * k_:64 * k_ + 64]
            Bk.ring = [[0, 1, 3, 4][k_]]
            Bk.rs = {"ri": 0}
            SBs.append(Bk)
        sj = lambda Bk: Bk.g - 10
        conv_branch(SBs, sj, n, True, scT, convo)
        DMA(lambda e: e.dma_start(out=convs_d, in_=convo),
            [("cvo", j) for j in range(4)] + [("cvo0", j) for j in range(4)], [], "d_o")
        slots = pair_pre(SBs, sj, n, True, lambda Bk: (bvS[:, sj(Bk), :], ("bvS", sj(Bk))), ssT, zso)
        for j in range(4):
            B = SBs[j]
            SQ = SQs[j % 2]
            for part, qs in ((0, slots[0:4]), (1, slots[4:6])):
                pzt, tk_ = psalloc(B)
                tt = tk_ + ("sq",)
                for qi, sl in enumerate(qs):
                    TE(lambda e, qi=qi, sl=sl, pzt=pzt, B=B: e.transpose(pzt[0:NS, qi * 128:(qi + 1) * 128],
                                                                          ar(B, sl, n), identf[:, :]),
                       [at(B, sl), "identf"], [tt])
                nq = len(qs)
                AE(lambda e, pzt=pzt, part=part, nq=nq, SQ=SQ: e.copy(
                    out=SQ[0:NS, part * 4:part * 4 + nq, :],
                    in_=pzt[0:NS, 0:nq * 128].rearrange("p (q c) -> p q c", c=128)), [tt], [("SQ", j % 2, part)])
            for h in range(2):
                DMA(lambda e, j=j, h=h, SQ=SQ: e.dma_start(out=scr1_d[:, 2 * j + h, :, :],
                                                           in_=SQ[0:NS, :, h * 64:(h + 1) * 64]),
                    [("SQ", j % 2, 0), ("SQ", j % 2, 1)], [("scr1", j, h)], ["d_s1", "d_s1b"][j % 2])
        DMA(lambda e: e.dma_start(out=shifts_d, in_=zso), [("zso", q) for q in range(13)], [], "d_o")
        DMA(lambda e: e.dma_start(out=VQ, in_=scr1_d.rearrange("b h q k -> (b h) q k")),
            [("scr1", j, h) for j in range(4) for h in range(2)], ["VQ"], "d_s2")
        B = TH0
        S.barrier()
        def bk(q):
            return VQ[:, q, :].unsqueeze(1).broadcast_to([128, 16, 64])

        def state_load(qt):
            sl0 = 8 * (qt % 2)
            S3 = B.arena[:, sl0:sl0 + 4, :].rearrange("p a b -> p (a b)").rearrange("p (v k) -> p v k", k=64)
            s3t = [at(B, k) for k in range(sl0, sl0 + 4)]
            DMA(lambda e: e.dma_start(out=S3, in_=swkv_d[:, qt * 1024:(qt + 1) * 1024]
                                      .rearrange("p (v k) -> p v k", k=64)), [], s3t, ["d_l", "d_l2"][qt % 2])

        def q_views(qt):
            sl0 = 8 * (qt % 2)
            S3 = B.arena[:, sl0:sl0 + 4, :].rearrange("p a b -> p (a b)").rearrange("p (v k) -> p v k", k=64)
            TM = B.arena[:, sl0 + 4:sl0 + 8, :].rearrange("p a b -> p (a b)").rearrange("p (v k) -> p v k", k=64)
            s3t = [at(B, k) for k in range(sl0, sl0 + 4)]
            tmt = [at(B, k) for k in range(sl0 + 4, sl0 + 8)]
            return S3, TM, s3t, tmt

        def state_a(qt):
            v0 = qt * 16
            S3, TM, s3t, tmt = q_views(qt)

            def bvv(ap2):
                return ap2[:, v0:v0 + 16].unsqueeze(2).broadcast_to([128, 16, 64])
            VE(lambda e: e.tensor_tensor(out=TM, in0=S3, in1=bk(4), op=ALU.mult), s3t + ["VQ"], tmt)
            VE(lambda e: e.tensor_reduce(out=sa[:, v0:v0 + 16], in_=TM, axis=AX.X, op=ALU.add, negate=True),
               tmt, [("sa", qt)])
            GE(lambda e: e.tensor_tensor(out=S3, in0=S3, in1=bk(1), op=ALU.mult), s3t + ["VQ"], s3t)
            VE(lambda e: e.tensor_tensor(out=TM, in0=bvv(sa), in1=bk(5), op=ALU.mult), [("sa", qt), "VQ"] + tmt, tmt)
            GE(lambda e: e.tensor_tensor(out=S3, in0=S3, in1=TM, op=ALU.add), s3t + tmt, s3t)

        def state_b(qt):
            v0 = qt * 16
            S3, TM, s3t, tmt = q_views(qt)
            wsm = ["d_ws", "d_ws2"][qt % 2]

            def bvv(ap2):
                return ap2[:, v0:v0 + 16].unsqueeze(2).broadcast_to([128, 16, 64])
            VE(lambda e: e.tensor_tensor(out=TM, in0=bvv(VQ[:, 3, :]), in1=bk(2), op=ALU.mult), ["VQ"] + tmt, tmt)
            GE(lambda e: e.tensor_tensor(out=S3, in0=S3, in1=TM, op=ALU.add), s3t + tmt, s3t)
            DMA(lambda e: e.dma_start(out=wkvs_d[:, qt * 1024:(qt + 1) * 1024]
                                      .rearrange("p (v k) -> p v k", k=64), in_=S3), s3t, [], wsm, q="scalar")
            VE(lambda e: e.tensor_tensor(out=TM, in0=S3, in1=bk(0), op=ALU.mult), s3t + ["VQ"] + tmt, tmt)
            VE(lambda e: e.tensor_reduce(out=yv[:, v0:v0 + 16], in_=TM, axis=AX.X, op=ALU.add), tmt, [("yv", qt)])

        p1g = p1_gen(THS, x_d, [0, 128], 128, [0, 128], 1, d7stage=True, carry="zero") if n_st > 0 else iter(())

        def p1step(k=2):
            for _ in range(k):
                next(p1g, None)
        state_load(0)
        state_load(1)
        state_a(0)
        p1step()
        state_a(1)
        p1step()
        state_b(0)
        state_load(2)
        p1step()
        state_b(1)
        state_load(3)
        p1step()
        state_a(2)
        p1step()
        state_a(3)
        p1step()
        state_b(2)
        state_b(3)
        for _ in p1g:
            pass
        DMA(lambda e: e.dma_start(out=scr2_d.rearrange("b h v -> (b h) v"), in_=yv), [("yv", q_) for q_ in range(4)],
            ["scr2"], "d_s3")
        DMA(lambda e: e.dma_start(out=yq[0:NS, :], in_=scr2_d.rearrange("b h v -> b (h v)")), ["scr2"], ["yq"], "d_s4")
        S.barrier()
        YP = {}
        for Bk in SBs:
            j = sj(Bk)
            pzt, tk_ = psalloc(Bk)
            tt = tk_ + ("yt",)
            YP[Bk.g] = (pzt, tt)
            TE(lambda e, j=j, pzt=pzt: e.transpose(pzt[:, 0:NS], yq[0:NS, j * 128:(j + 1) * 128], identf[0:NS, 0:NS]),
               ["yq", "identf"], [tt])
        post_pair(SBs, sj, n, lambda Bk: (YP[Bk.g][0][:, 0:NS], YP[Bk.g][1]),
                  lambda Bk: (bvS[:, sj(Bk), :], ("bvS", sj(Bk))))
        DMA(lambda e: e.dma_start(out=ym_d[:, :, SEQ:SEQ + NS], in_=ymixT[:, :, 0:NS]),
            [("ym", k) for k in range(8)], [], "d_ym")

    def bc4(ap2):
        return ap2.unsqueeze(1).broadcast_to([128, 4, 128])

    def halves_wkv(Bs, front=None):
        n = NT
        jb = lambda B, pp: 2 * B.g + pp
        PC4 = []
        for pp in range(2):
            for B in Bs:
                C = _copy.copy(B)
                C.ring, C.rs = B.ring4, B.rs4
                C.sb = 9 * pp
                C.ppi = pp
                PC4.append(C)
        FILL["k2"] = True
        pair_pre(PC4, lambda C: jb(C, C.ppi), n, False, lambda C: (C.bv[:, C.ppi, 0:n], tk(C, "bv", C.ppi)))
        FILL["ok"] = True
        flush_fill()
        FILL["ok"] = False
        FILL["k2"] = False

        def tkt(B, i):
            return [tk(B, "TK", i, 0), tk(B, "TK", i, 1)]

        def tk_round(i, rnd, qs):
            PT = {}
            for B in Bs:
                pt, tk_ = ptalloc(B)
                t = tk_ + ("tk",)
                PT[B.g] = (pt, t)
                for qi, q in enumerate(qs):
                    for pp in range(2):
                        if q == 0:
                            src, nm = B.KB[:, pp, 0, i * 128:(i + 1) * 128], "KB0"
                        elif q == 1:
                            src, nm = B.KB[:, pp, 1, i * 128:(i + 1) * 128], "KB1"
                        elif q == 2:
                            src, nm = B.VT[:, pp, i * 128:(i + 1) * 128], "VT"
                        else:
                            src, nm = B.KR[:, pp, 0, i * 128:(i + 1) * 128], "KR0"
                        TE(lambda e, src=src, pt=pt, qi=qi, pp=pp: e.transpose(
                            pt[:, qi * 256 + pp * 128:qi * 256 + (pp + 1) * 128], src, ident[:, :]),
                           [tk(B, nm, pp), "ident"], [t], sig=(qi == len(qs) - 1 and pp == 1))
            for B in Bs:
                pt, t = PT[B.g]
                AE(lambda e, pt=pt, B=B: e.copy(
                    out=B.TK[:, i, :, :].rearrange("p q c -> p (q c)"), in_=pt[:, :]),
                   [t], [tk(B, "TK", i, 0), tk(B, "TK", i, 1)])

        def score_head(i, hh):
            tsl = slice(i * 128, (i + 1) * 128)
            pp, h = hh // 2, hh % 2
            hs = slice(64 * h, 64 * h + 64)
            P = {}
            for B in Bs:
                ps, tk_ = psalloc(B.wide)
                t = tk_ + ("s",)
                P[B.g] = (ps, t)
                TE(lambda e, ps=ps, B=B: e.matmul(ps[:, 0:256], lhsT=B.KB[hs, pp, 0, tsl], rhs=B.KR[hs, pp, :, tsl],
                                                  start=True, stop=True),
                   [tk(B, "KB0", pp), tk(B, "KR0", pp), tk(B, "KR1", pp)], [t], sig=False)
                TE(lambda e, ps=ps, B=B: e.matmul(ps[:, 256:512], lhsT=B.KB[hs, pp, 1, tsl],
                                                  rhs=B.KR[hs, pp, :, tsl], start=True, stop=True),
                   [tk(B, "KB1", pp), tk(B, "KR0", pp), tk(B, "KR1", pp)], [t])
            if hh % 2 == 0:
                for B in Bs:
                    ps, t = P[B.g]
                    VE(lambda e, ps=ps, B=B: e.tensor_tensor(
                        out=B.SC[:, i, hh, :].rearrange("p (a c) -> p a c", a=2),
                        in0=ps[:, :].rearrange("p (a c) -> p a c", a=2),
                        in1=maskA[:, :].unsqueeze(1).broadcast_to([128, 2, 256]), op=ALU.mult),
                       [t, "maskA"], [tk(B, "SC", i, hh)])
            else:
                for B in Bs:
                    ps, t = P[B.g]
                    AE(lambda e, ps=ps, B=B: e.copy(out=B.SC[:, i, hh, :], in_=ps[:, :]), [t], [tk(B, "SC", i, hh)])
                for B in Bs:
                    GE(lambda e, B=B: e.tensor_tensor(
                        out=B.SC[:, i, hh, :].rearrange("p (a c) -> p a c", a=2),
                        in0=B.SC[:, i, hh, :].rearrange("p (a c) -> p a c", a=2),
                        in1=maskA[:, :].unsqueeze(1).broadcast_to([128, 2, 256]), op=ALU.mult),
                       [tk(B, "SC", i, hh), "maskA"], [tk(B, "SC", i, hh)])

        def score_N(i, h):
            tsl = slice(i * 128, (i + 1) * 128)
            hs = slice(64 * h, 64 * h + 64)
            P = {}
            for B in Bs:
                ps, tk_ = psalloc(B.wide)
                t = tk_ + ("n",)
                P[B.g] = (ps, t)
                for pp in range(2):
                    TE(lambda e, ps=ps, pp=pp, B=B: e.matmul(ps[:, pp * 128:(pp + 1) * 128],
                                                             lhsT=B.KR[hs, pp, 0, tsl], rhs=B.KB[hs, pp, 1, tsl],
                                                             start=True, stop=True),
                       [tk(B, "KR0", pp), tk(B, "KB1", pp)], [t], sig=(pp == 1))
            for pp in range(2):
                for B in Bs:
                    ps, t = P[B.g]
                    VE(lambda e, ps=ps, pp=pp, B=B: e.tensor_tensor(out=B.NB[:, pp * 2 + h, :],
                                                                    in0=ps[:, pp * 128:(pp + 1) * 128],
                                                                    in1=maskN[:, :], op=ALU.mult),
                       [t, "maskN"], [tk(B, "NB", pp * 2 + h)])

        def sct(B, i):
            return [tk(B, "SC", i, hh) for hh in range(4)]

        def nbt(B):
            return [tk(B, "NB", hh) for hh in range(4)]

        def dbl_level(i, lv):
            cur, nxt = (lv - 1) % 2, lv % 2

            def Ap(B, hh):
                return B.SC[:, i, hh, 256:384] if lv == 1 else B.dA[cur][:, hh, :]

            def Bp(B, hh):
                return B.NB[:, hh, :] if lv == 1 else B.dB[cur][:, hh, :]

            def abt(B):
                return (sct(B, i) + nbt(B)) if lv == 1 else [tk(B, f"dA{cur}"), tk(B, f"dB{cur}")]
            PB, PA, PP_ = {}, {}, {}
            for B in Bs:
                ps, tk_ = psalloc(B.wide)
                t = tk_ + ("B",)
                PB[B.g] = (ps, t)
                for hh in range(4):
                    TE(lambda e, ps=ps, hh=hh, a=Ap(B, hh), b=Bp(B, hh): e.matmul(
                        ps[:, hh * 128:(hh + 1) * 128], lhsT=a, rhs=b, start=True, stop=True), abt(B), [t],
                       sig=(hh == 3))
            for B in Bs:
                ps, t = PB[B.g]
                AE(lambda e, ps=ps, B=B: e.copy(out=B.dB[nxt][:, :, :].rearrange("p a c -> p (a c)"), in_=ps[:, :]),
                   [t], [tk(B, f"dB{nxt}")])
            if lv <= 4:
                for B in Bs:
                    ps, tk_ = psalloc(B.wide)
                    t = tk_ + ("A",)
                    PA[B.g] = (ps, t)
                    for hh in range(4):
                        TE(lambda e, ps=ps, hh=hh, a=Ap(B, hh), b=Bp(B, hh): e.matmul(
                            ps[:, hh * 128:(hh + 1) * 128], lhsT=b, rhs=a, start=True, stop=True), abt(B), [t],
                           sig=(hh == 3))
                for B in Bs:
                    ps, t = PA[B.g]
                    AE(lambda e, ps=ps, B=B: e.copy(out=B.dA[nxt][:, :, :].rearrange("p a c -> p (a c)"),
                                                    in_=ps[:, :]), [t], [tk(B, f"dA{nxt}")])
            for B in Bs:
                ps, tk_ = psalloc(B.wide)
                t = tk_ + ("P",)
                PP_[B.g] = (ps, t)
                for hh in range(4):
                    TE(lambda e, ps=ps, hh=hh, B=B: e.matmul(ps[:, hh * 128:(hh + 1) * 128],
                                                             lhsT=B.dB[nxt][:, hh, :], rhs=B.dP[cur][:, hh, :],
                                                             start=True, stop=True),
                       [tk(B, f"dB{nxt}"), tk(B, f"dP{cur}")], [t], sig=(hh == 3))
            for B in Bs:
                ps, t = PP_[B.g]
                VE(lambda e, ps=ps, B=B: e.tensor_tensor(
                    out=B.dP[nxt][:, :, :].rearrange("p a c -> p (a c)"), in0=ps[:, :],
                    in1=B.dP[cur][:, :, :].rearrange("p a c -> p (a c)"), op=ALU.add),
                   [t, tk(B, f"dP{cur}")], [tk(B, f"dP{nxt}")])

        def mtt(B):
            return [tk(B, "dP1")]

        def prec(i):
            P = {}
            for B in Bs:
                ps, tk_ = psalloc(B.wide)
                t = tk_ + ("w1",)
                P[B.g] = (ps, t)
                for hh in range(4):
                    TE(lambda e, ps=ps, hh=hh, B=B: e.matmul(ps[:, hh * 64:(hh + 1) * 64], lhsT=B.SC[:, i, hh, 0:128],
                                                             rhs=B.TK[:, i, 2, hh * 64:(hh + 1) * 64],
                                                             start=True, stop=True),
                       [tk(B, "SC", i, hh)] + tkt(B, i), [t], sig=(hh == 3))
            for B in Bs:
                ps, t = P[B.g]
                AE(lambda e, ps=ps, B=B: e.copy(out=B.W1[:, :, :].rearrange("p a c -> p (a c)"), in_=ps[:, 0:256]),
                   [t], [tk(B, "W1")])
            for B in Bs:
                ps, tk_ = psalloc(B.wide)
                t = tk_ + ("u1",)
                P[B.g] = (ps, t)
                for hh in range(4):
                    TE(lambda e, ps=ps, hh=hh, B=B: e.matmul(ps[:, hh * 64:(hh + 1) * 64], lhsT=B.dP[1][:, hh, :],
                                                             rhs=B.W1[:, hh, :], start=True, stop=True),
                       mtt(B) + [tk(B, "W1")], [t], sig=(hh == 3))
            for B in Bs:
                ps, t = P[B.g]
                AE(lambda e, ps=ps, B=B: e.copy(out=B.U1b[:, i, :], in_=ps[:, 0:256]), [t], [tk(B, "U1b", i)])
            for B in Bs:
                ps, tk_ = psalloc(B.wide)
                t = tk_ + ("kp",)
                P[B.g] = (ps, t)
                for hh in range(4):
                    pp, h = hh // 2, hh % 2
                    TE(lambda e, ps=ps, hh=hh, pp=pp, h=h, B=B: e.matmul(
                        ps[64 * h:64 * h + 64, pp * 128:(pp + 1) * 128], lhsT=B.TK[:, i, 3, hh * 64:(hh + 1) * 64],
                        rhs=B.dP[1][:, hh, :], start=True, stop=True), mtt(B) + tkt(B, i), [t], sig=(hh == 3))
            for B in Bs:
                ps, t = P[B.g]
                VE(lambda e, ps=ps, B=B: e.tensor_copy(out=B.KPT[:, i, :, :].rearrange("p a c -> p (a c)"),
                                                       in_=ps[:, 0:256]), [t], [tk(B, "KPT", i)])
            for B in Bs:
                ps, tk_ = psalloc(B.wide)
                t = tk_ + ("y1",)
                P[B.g] = (ps, t)
                for hh in range(4):
                    pp, h = hh // 2, hh % 2
                    TE(lambda e, ps=ps, hh=hh, pp=pp, h=h, B=B: e.matmul(
                        ps[64 * h:64 * h + 64, pp * 128:(pp + 1) * 128], lhsT=B.TK[:, i, 2, hh * 64:(hh + 1) * 64],
                        rhs=B.SC[:, i, hh, 128:256], start=True, stop=False), tkt(B, i) + [tk(B, "SC", i, hh)], [t],
                       sig=False)
                    TE(lambda e, ps=ps, hh=hh, pp=pp, h=h, B=B: e.matmul(
                        ps[64 * h:64 * h + 64, pp * 128:(pp + 1) * 128], lhsT=B.U1b[:, i, hh * 64:(hh + 1) * 64],
                        rhs=B.SC[:, i, hh, 384:512], start=False, stop=True),
                       [tk(B, "U1b", i), tk(B, "SC", i, hh)], [t], sig=(hh == 3))
            for B in Bs:
                ps, t = P[B.g]
                AE(lambda e, ps=ps, B=B: e.copy(out=B.Y1T[:, i, :, :].rearrange("p a c -> p (a c)"), in_=ps[:, 0:256]),
                   [t], [tk(B, "Y1T", i)])

        def h1_part(i, ee):
            es_ = slice(64 * ee, 64 * ee + 64)
            s0_ = i * 4 + ee * 2
            P = {}
            for B in Bs:
                ps, tk_ = psalloc(B.wide)
                t = tk_ + ("h1",)
                P[B.g] = (ps, t)
                for pp in range(2):
                    cs = slice(pp * 128, (pp + 1) * 128)
                    TE(lambda e, ps=ps, cs=cs, B=B: e.matmul(ps[:, cs], lhsT=B.TK[es_, i, 0, cs],
                                                             rhs=B.TK[es_, i, 2, cs], start=True, stop=False),
                       tkt(B, i), [t], sig=False)
                    TE(lambda e, ps=ps, cs=cs, B=B: e.matmul(ps[:, cs], lhsT=B.TK[es_, i, 1, cs],
                                                             rhs=B.U1b[es_, i, cs], start=False, stop=True),
                       tkt(B, i) + [tk(B, "U1b", i)], [t], sig=(pp == 1))
            for B in Bs:
                ps, t = P[B.g]
                AE(lambda e, ps=ps, B=B: e.copy(out=B.H1[:, s0_:s0_ + 2, :].rearrange("p a c -> p (a c)"),
                                                in_=ps[:, 0:256]), [t],
                   [tk(B, "H1", s0_), tk(B, "H1", s0_ + 1)])

        def Tt(B):
            return [("T", 2 * B.g), ("T", 2 * B.g + 1)]

        def chain_chunk(i, ee, PY):
            es_ = slice(64 * ee, 64 * ee + 64)
            ch = i * 2 + ee
            PU, PD = {}, {}
            fill()
            for B in Bs:
                psU, tk_ = psalloc(B.chain)
                tU = tk_ + ("U",)
                PU[B.g] = (psU, tU)
                for pp in range(2):
                    TE(lambda e, psU=psU, pp=pp, B=B: e.matmul(psU[:, pp * 128:(pp + 1) * 128],
                                                               lhsT=B.KPT[:, i, pp, :], rhs=T16[:, 2 * B.g + pp, :],
                                                               start=True, stop=True),
                       [tk(B, "KPT", i), ("T16", 2 * B.g + pp)], [tU], sig=(pp == 1))
            fill()
            for B in Bs:
                psU, tU = PU[B.g]
                VE(lambda e, psU=psU, B=B: e.tensor_copy(out=B.U0b[es_, :], in_=psU[es_, 0:256]), [tU],
                   [tk(B, "U0b")])
            fill()
            for pp in range(2):
                slot = i * 4 + ee * 2 + pp
                for B in Bs:
                    GE(lambda e, pp=pp, B=B, slot=slot: e.tensor_tensor(out=B.Tsum[:, pp, :],
                                                                        in0=T32[:, 2 * B.g + pp, :],
                                                                        in1=B.H1[:, slot, :], op=ALU.add),
                       [("T", 2 * B.g + pp), tk(B, "H1", slot)], [tk(B, "Ts", pp)])
            fill()
            for B in Bs:
                psD, tk_ = psalloc(B.chain)
                tD = tk_ + ("D",)
                PD[B.g] = (psD, tD)
                for pp in range(2):
                    cs = slice(pp * 128, (pp + 1) * 128)
                    TE(lambda e, psD=psD, cs=cs, B=B: e.matmul(psD[:, cs], lhsT=B.TK[es_, i, 1, cs],
                                                               rhs=B.U0b[es_, cs], start=True, stop=True),
                       tkt(B, i) + [tk(B, "U0b")], [tD], sig=(pp == 1))
            fill()
            for B in Bs:
                psY, tY = PY[B.g]
                for pp in range(2):
                    ycs = slice(pp * 128 + 64 * ee, pp * 128 + 64 * ee + 64)
                    TE(lambda e, psY=psY, pp=pp, ycs=ycs, B=B: e.matmul(
                        psY[:, ycs], lhsT=T16[:, 2 * B.g + pp, :],
                        rhs=B.KR[:, pp, 1, i * 128 + 64 * ee:i * 128 + 64 * ee + 64], start=True, stop=False),
                       [("T16", 2 * B.g + pp), tk(B, "KR1", pp)], [tY], sig=False)
                    for h in range(2):
                        hh = pp * 2 + h
                        TE(lambda e, psY=psY, ycs=ycs, h=h, hh=hh, B=B: e.matmul(
                            psY[64 * h:64 * h + 64, ycs], lhsT=B.U0b[es_, hh * 64:(hh + 1) * 64],
                            rhs=B.SC[es_, i, hh, 384 + 64 * ee:384 + 64 * ee + 64], start=False, stop=True),
                           [tk(B, "U0b"), tk(B, "SC", i, hh)], [tY], sig=(pp == 1 and h == 1))
            fill()
            for h in range(2):
                hs = slice(64 * h, 64 * h + 64)
                for B in Bs:
                    psD, tD = PD[B.g]
                    VE(lambda e, psD=psD, B=B, hs=hs: e.tensor_tensor(
                        out=B.Tsum[hs, :, hs], in0=psD[hs, 0:256].rearrange("p (a c) -> p a c", a=2)[:, :, hs],
                        in1=B.Tsum[hs, :, hs], op=ALU.add), [tD, tk(B, "Ts", 0), tk(B, "Ts", 1)],
                       [tk(B, "Ts", 0), tk(B, "Ts", 1)])
                for B in Bs:
                    VE(lambda e, B=B, hs=hs: e.tensor_tensor(
                        out=T16[hs, 2 * B.g:2 * B.g + 2, hs], in0=B.Tsum[hs, :, hs],
                        in1=pcs[hs, 2 * B.g:2 * B.g + 2, ch:ch + 1].broadcast_to([64, 2, 64]), op=ALU.mult),
                       [tk(B, "Ts", 0), tk(B, "Ts", 1), ("pcs", 2 * B.g), ("pcs", 2 * B.g + 1)],
                       [("T16", 2 * B.g), ("T16", 2 * B.g + 1)])
                for B in Bs:
                    GE(lambda e, B=B, hs=hs: e.tensor_tensor(
                        out=T32[hs, 2 * B.g:2 * B.g + 2, hs], in0=B.Tsum[hs, :, hs],
                        in1=pcs[hs, 2 * B.g:2 * B.g + 2, ch:ch + 1].broadcast_to([64, 2, 64]), op=ALU.mult),
                       [tk(B, "Ts", 0), tk(B, "Ts", 1), ("pcs", 2 * B.g), ("pcs", 2 * B.g + 1)], Tt(B))

        def chain_tile(i):
            PY = {}
            for B in Bs:
                psY, tk_ = psy(B)
                PY[B.g] = (psY, tk_ + ("Y",))
            for ee in range(2):
                chain_chunk(i, ee, PY)
            for B in Bs:
                psY, tY = PY[B.g]
                VE(lambda e, psY=psY, B=B: e.tensor_tensor(
                    out=B.arena[:, 12:14, i * 128:(i + 1) * 128], in0=psY[:, 0:256].rearrange("p (a c) -> p a c", a=2),
                    in1=B.Y1T[:, i, :, :], op=ALU.add), [tY, tk(B, "Y1T", i)], [at(B, 12), at(B, 13)])

        for i in range(2):
            tk_round(i, 0, (0, 1, 2, 3))
        for i in range(2):
            for hh in range(4):
                score_head(i, hh)
            for h in range(2):
                score_N(i, h)
            for B in Bs:
                VE(lambda e, B=B, i=i: e.tensor_tensor(out=B.dP[0][:, :, :], in0=B.SC[:, i, :, 256:384],
                                                       in1=bc4(ident[:, :]), op=ALU.add),
                   sct(B, i) + ["ident"], [tk(B, "dP0")])
            for lv in range(1, 6):
                dbl_level(i, lv)
            prec(i)
            for ee in range(2):
                h1_part(i, ee)
        if front is not None:
            FILL["gen"] = front
        FILL["ok"] = True
        for i in range(2):
            chain_tile(i)
        flush_fill()
        FILL["ok"] = False
        PB4 = []
        for pp in range(2):
            for B in Bs:
                Bq = _copy.copy(B.wide)
                Bq.soff = 4 * pp
                Bq.ppi = pp
                PB4.append(Bq)
        post_pair(PB4, lambda B: jb(B, B.ppi), n, lambda B: (ar(B, 12 + B.ppi, n), at(B, 12 + B.ppi)),
                  lambda B: (B.bv[:, B.ppi, 0:n], tk(B, "bv", B.ppi)))

    if do_sample:
        sample_group()
        S.barrier()
    xpar = lambda st: (st + 1) % 2

    def front_gen(st1):
        yield from p1_gen(THS, x_d, [st1 * NT, st1 * NT + 128], 128, [0, 128], xpar(st1), d7stage=True, carry="prev")
        wdad_chunk(WC, NT, False, par=xpar(st1), save_last=(st1 == n_st - 1), defer=True)
        yield

    if n_st > 0:
        if not do_sample:
            load_norm_T(THS, x_d, [0, 128], 128, [0, 128], xpar(0), carry="zero")
        wdad_chunk(TH0, NT, False, par=xpar(0), save_last=(n_st == 1))
    for st in range(n_st):
        CUR["x"] = xpar(st)
        LASTF["v"] = (st == n_st - 1)
        if st > 0:
            thad_ops(TH0, NT)

        def conv_all():
            for jj in range(2):
                yield from conv_gen(CTH, lambda B, jj=jj: 2 * B.g + jj, NT, False, slots=(18, 18, 19))
        FILL["gen"] = conv_all()
        halves_wkv(THS, front_gen(st + 1) if st + 1 < n_st else None)
        DMA(lambda e, st=st: e.dma_start(out=ym_d[:, :, st * NT:(st + 1) * NT], in_=ymixT[:, :, :]),
            [("ym", k) for k in range(8)], [], "d_ym")
    DMA(lambda e: e.dma_start(out=convp_d, in_=uext[:, :, 0:2]), [("u", j) for j in range(4)], [], "d_o")
    DMA(lambda e: e.dma_start(out=shiftp_d, in_=zlast[:, :]), [("zl", q) for q in range(13)], [], "d_o")
    for j in range(4):
        for h in range(2):
            hs = slice(64 * h, 64 * h + 64)
            DMA(lambda e, j=j, h=h, hs=hs: e.dma_start(out=wkvp_d[2 * j + h, :, :], in_=T32[hs, j, hs]),
                [("T", j)], [], "d_o")
    S.barrier()

    wflat = win[:].rearrange("p c d -> p (c d)")
    wout = wflat[:, 0:8192].rearrange("p (c d) -> p c d", c=8)
    wpg = wflat[:, 8192:16384].rearrange("p (c d) -> p c d", c=8)
    wpp = wflat[:, 16384:18432].rearrange("p (c d) -> p c d", c=2)
    gbc = wflat[:, 29696:31744].bitcast(F32)
    stgB = [(xnTf[0][:, 0:2048].bitcast(F32), "stgB0"),
            (ymixT[:].rearrange("p a b -> p (a b)").bitcast(F32), "stgB1"),
            (uext[:].rearrange("p a b -> p (a b)")[:, 0:1024], "stgB2")]
    chunksB = []
    for c in range(8):
        chunksB.append((wout[:, c, :], wout_d[c * 128:(c + 1) * 128, :], "wout"))
    for c in range(8):
        chunksB.append((wpg[:, c, :], wpg_d[c * 128:(c + 1) * 128, :], "wpg"))
    for c in range(2):
        chunksB.append((wpp[:, c, :], wpp_d[c * 128:(c + 1) * 128, :], "wpp"))
    load_cast(chunksB, stgB, "wB")
    DMA(lambda e: e.dma_start(out=gbc, in_=fg_d.partition_broadcast(128)), [], ["gbc"], "d_c2")

    class TB:
        pass

    OT = []
    for t_ in range(4):
        O = TB()
        O.g = t_
        O.pbank = t_ % 2
        if t_ < 2:
            base = 18432 + t_ * 5632
            O.h1b = wflat[:, base:base + 1024]
            O.h1T = wflat[:, base + 1024:base + 2048].rearrange("p (c t) -> p c t", c=8)
            O.sigb = wflat[:, base + 2048:base + 3072]
            O.ptb = wflat[:, base + 3072:base + 3328]
            O.pT = wflat[:, base + 3328:base + 3584].rearrange("p (c t) -> p c t", c=2)
            O.ymi = wflat[:, base + 3584:base + 4608].rearrange("p (c t) -> p c t", c=8)
            A = THS[t_].arena
            O.h1 = A[:, 0:4, :].rearrange("p a b -> p (a b)")
            O.gg = A[:, 4:8, :].rearrange("p a b -> p (a b)")
            O.pt32 = A[:, 8, :]
            O.ring = [t_, 4 + t_]
        else:
            Bx = THS[t_ - 2]
            scf = Bx.SC[:].rearrange("p a b c -> p (a b c)")
            tkf = Bx.TK[:].rearrange("p a b c -> p (a b c)")
            O.h1b = scf[:, 0:1024]
            O.h1T = scf[:, 1024:2048].rearrange("p (c t) -> p c t", c=8)
            O.sigb = scf[:, 2048:3072]
            O.ymi = scf[:, 3072:4096].rearrange("p (c t) -> p c t", c=8)
            O.ptb = tkf[:, 0:256]
            O.pT = tkf[:, 256:512].rearrange("p (c t) -> p c t", c=2)
            O.h1 = Bx.H1[:].rearrange("p a b -> p (a b)")
            O.gg = Bx.arena[:, 9:13, :].rearrange("p a b -> p (a b)")
            O.pt32 = Bx.arena[:, 13, :]
            O.ring = [t_]
        if t_ < 2:
            O.yo = THS[t_].D7[:].rearrange("p a b c -> p (a b c)").bitcast(F32)[:, 0:1024]
            O.yot = ("o", t_, "yo")
        elif t_ == 2:
            O.yo = xnTf[1][:, 0:2048].bitcast(F32)
            O.yot = ("o", t_, "yo")
        else:
            O.yo = uext[:].rearrange("p a b -> p (a b)")[:, 0:1024]
            O.yot = "stgB2"
        O.rs = {"ri": 0}
        O.gring = True
        OT.append(O)

    def ot(O, *nm):
        return ("o", O.g) + nm

    def dsem(p, O):
        if p == "d_m":
            return f"d_m{O.g}"
        return f"{p}{O.g}" if O.g else p

    def o_loads(Os, specs):
        for O, (xsrc, psrc, ydst, tok0, rows, ymcol) in zip(Os, specs):
            DMA(lambda e, O=O, xsrc=xsrc, tok0=tok0, rows=rows: e.dma_start(out=O.gg[0:rows, :],
                                                                            in_=xsrc[tok0:tok0 + rows, :]),
                [], [ot(O, "g")], dsem("d_g", O))
            DMA(lambda e, O=O, psrc=psrc, tok0=tok0, rows=rows: e.dma_start(out=O.pt32[0:rows, :],
                                                                            in_=psrc[tok0:tok0 + rows, :]),
                [], [ot(O, "pt32")], dsem("d_p", O))
            DMA(lambda e, O=O, ymcol=ymcol, rows=rows: e.dma_start(out=O.ymi[:, :, 0:rows],
                                                                   in_=ym_d[:, :, ymcol:ymcol + rows]),
                [], [ot(O, "ymi")], dsem("d_m", O))

    def o_compute(Os, specs, after=None):
        for O, sp in zip(Os, specs):
            rows = sp[4]
            GE(lambda e, O=O, rows=rows: e.tensor_copy(out=O.ptb[0:rows, :], in_=O.pt32[0:rows, :]),
               [ot(O, "pt32")], [ot(O, "ptb")])
        for hf in range(2):
            P = {}
            for O, sp in zip(Os, specs):
                rows = sp[4]
                po, tk_ = psalloc(O)
                t = tk_ + ("o",)
                P[O.g] = (po, t)
                for c in range(8):
                    TE(lambda e, hf=hf, c=c, po=po, O=O, rows=rows: e.matmul(po[0:rows, :], lhsT=O.ymi[:, c, 0:rows],
                                                                      rhs=wout[:, c, hf * 512:(hf + 1) * 512],
                                                                      start=(c == 0), stop=(c == 7)),
                       [ot(O, "ymi"), "wout"], [t], sig=(c == 7))
            for O, sp in zip(Os, specs):
                rows = sp[4]
                po, t = P[O.g]
                VE(lambda e, hf=hf, po=po, O=O, rows=rows: e.tensor_tensor(out=O.h1[0:rows, hf * 512:(hf + 1) * 512],
                                                                    in0=po[0:rows, :],
                                                                    in1=O.gg[0:rows, hf * 512:(hf + 1) * 512],
                                                                    op=ALU.add), [t, ot(O, "g")], [ot(O, "h1", hf)])
        h1t = lambda O: [ot(O, "h1", 0), ot(O, "h1", 1)]
        for O, sp in zip(Os, specs):
            rows = sp[4]
            AE(lambda e, hf=hf, O=O, rows=rows: e.copy(out=O.h1b[0:rows, :], in_=O.h1[0:rows, :]), h1t(O), [ot(O, "h1b")])
        for O, sp in zip(Os, specs):
            rows = sp[4]
            pt, tk_ = ptalloc(O)
            t = tk_ + ("h",)
            for c in range(8):
                TE(lambda e, c=c, pt=pt, O=O, rows=rows: e.transpose(
                    pt[:, c * 128:c * 128 + rows], O.h1b[0:rows, c * 128:(c + 1) * 128], ident[0:rows, 0:rows]),
                   [ot(O, "h1b"), "ident"], [t], sig=(c == 7))
            AE(lambda e, pt=pt, O=O, rows=rows: e.copy(
                out=O.h1T[:, :, 0:rows],
                in_=pt[:, :].rearrange("p (c t) -> p c t", t=128)[:, :, 0:rows]), [t],
               [ot(O, "h1T", 0), ot(O, "h1T", 1)])
        for O, sp in zip(Os, specs):
            rows = sp[4]
            pt, tk_ = ptalloc(O)
            t = tk_ + ("p",)
            for c in range(2):
                TE(lambda e, c=c, pt=pt, O=O, rows=rows: e.transpose(pt[:, c * 128:c * 128 + rows],
                                                                     O.ptb[0:rows, c * 128:(c + 1) * 128],
                                                                     ident[0:rows, 0:rows]), [ot(O, "ptb"), "ident"], [t])
            VE(lambda e, pt=pt, O=O, rows=rows: e.tensor_copy(
                out=O.pT[:, :, 0:rows], in_=pt[:, 0:256].rearrange("p (c t) -> p c t", t=128)[:, :, 0:rows]),
               [t], [ot(O, "pT")])
        for hf in range(2):
            PG, PQ = {}, {}
            for O, sp in zip(Os, specs):
                rows = sp[4]
                pg, tk_ = psalloc(O)
                tg = tk_ + ("g",)
                PG[O.g] = (pg, tg)
                for c in range(8):
                    TE(lambda e, hf=hf, c=c, pg=pg, O=O, rows=rows: e.matmul(pg[0:rows, :], lhsT=O.h1T[:, c, 0:rows],
                                                                      rhs=wpg[:, c, hf * 512:(hf + 1) * 512],
                                                                      start=(c == 0), stop=(c == 7)),
                       [ot(O, "h1T", 0), ot(O, "h1T", 1), "wpg"], [tg], sig=(c == 7))
            for O, sp in zip(Os, specs):
                rows = sp[4]
                pg, tg = PG[O.g]
                AE(lambda e, hf=hf, pg=pg, O=O, rows=rows: e.activation(out=O.sigb[0:rows, hf * 512:(hf + 1) * 512],
                                                                 in_=pg[0:rows, :], func=AF.Sigmoid),
                   [tg], [ot(O, "sig", hf)])
            for O, sp in zip(Os, specs):
                rows = sp[4]
                pq, tk_ = psalloc(O)
                tq = tk_ + ("q",)
                PQ[O.g] = (pq, tq)
                for c in range(2):
                    TE(lambda e, hf=hf, c=c, pq=pq, O=O, rows=rows: e.matmul(pq[0:rows, :], lhsT=O.pT[:, c, 0:rows],
                                                                      rhs=wpp[:, c, hf * 512:(hf + 1) * 512],
                                                                      start=(c == 0), stop=(c == 1)),
                       [ot(O, "pT"), "wpp"], [tq], sig=(c == 1))
            for O, sp in zip(Os, specs):
                rows = sp[4]
                pq, tq = PQ[O.g]
                VE(lambda e, hf=hf, pq=pq, O=O, rows=rows: e.tensor_tensor(out=O.gg[0:rows, hf * 512:(hf + 1) * 512],
                                                                    in0=O.sigb[0:rows, hf * 512:(hf + 1) * 512],
                                                                    in1=pq[0:rows, :], op=ALU.mult),
                   [tq, ot(O, "sig", hf), ot(O, "g")], [ot(O, "g")])
        for O, sp in zip(Os, specs):
            rows = sp[4]
            GE(lambda e, hf=hf, O=O, rows=rows: e.tensor_tensor(out=O.h1[0:rows, :], in0=O.h1[0:rows, :], in1=O.gg[0:rows, :],
                                                         op=ALU.add), h1t(O) + [ot(O, "g")], h1t(O))
        if after is not None:
            after()
        for O, sp in zip(Os, specs):
            rows = sp[4]
            so = 8 * O.g
            AE(lambda e, hf=hf, O=O, rows=rows, so=so: e.activation(out=O.sigb[0:rows, :], in_=O.h1[0:rows, :], func=AF.Square,
                                                             accum_out=stat[0:rows, so + 4:so + 5]),
               h1t(O) + [ot(O, "sig", 0), ot(O, "sig", 1)], [ot(O, "sig", 0), ot(O, "sig", 1), ot(O, "st4")])
        for O, sp in zip(Os, specs):
            rows = sp[4]
            so = 8 * O.g
            AE(lambda e, hf=hf, rows=rows, so=so: e.activation(out=stat[0:rows, so + 5:so + 6], in_=stat[0:rows, so + 4:so + 5],
                                                        func=AF.Ln, scale=1.0 / D, bias=prm[0:rows, RME_:RME_ + 1]),
               [ot(O, "st4"), "prm"], [ot(O, "st5")])
        for O, sp in zip(Os, specs):
            rows = sp[4]
            so = 8 * O.g
            AE(lambda e, rows=rows, so=so: e.activation(out=stat[0:rows, so + 6:so + 7], in_=stat[0:rows, so + 5:so + 6],
                                                        func=AF.Exp, scale=-0.5), [ot(O, "st5")], [ot(O, "st6")])
        for O, sp in zip(Os, specs):
            rows = sp[4]
            so = 8 * O.g
            VE(lambda e, hf=hf, O=O, rows=rows, so=so: e.scalar_tensor_tensor(out=O.yo[0:rows, :], in0=O.h1[0:rows, :],
                                                                       scalar=stat[0:rows, so + 6:so + 7],
                                                                       in1=gbc[0:rows, :], op0=ALU.mult, op1=ALU.mult),
               h1t(O) + [ot(O, "st6"), "gbc"], [O.yot])
        for O, (xsrc, psrc, ydst, tok0, rows, ymcol) in zip(Os, specs):
            DMA(lambda e, hf=hf, O=O, ydst=ydst, tok0=tok0, rows=rows: e.dma_start(out=ydst[tok0:tok0 + rows, :],
                                                                            in_=O.yo[0:rows, :]),
                [O.yot], [], dsem("d_y", O))

    groups = []
    if do_sample:
        groups.append(([OT[0]], [(xs_d, psm_d, ys_d, 0, NS, SEQ)]))
    ntile = n_st * 2
    for tt in range(0, ntile, 4):
        k_ = min(4, ntile - tt)
        groups.append((OT[:k_], [(x_d, p_d, y_d, (tt + k) * 128, 128, (tt + k) * 128) for k in range(k_)]))
    if groups:
        o_loads(*groups[0])
    for gi, (Os_, sp_) in enumerate(groups):
        nxt = (lambda gi=gi: o_loads(*groups[gi + 1])) if gi + 1 < len(groups) else None
        o_compute(Os_, sp_, nxt)
    S.barrier()

    with nc.allow_low_precision("bf16 matmul operands, fp32 accumulation"):
        with nc.allow_non_contiguous_dma("small strided state DMAs"):
            with nc.Block() as block:
                @block.tensor
                def _(e): S.replay("tensor", e, sems)
                @block.vector
                def _(e): S.replay("vector", e, sems)
                @block.scalar
                def _(e): S.replay("scalar", e, sems)
                @block.gpsimd
                def _(e): S.replay("gpsimd", e, sems)
                @block.sync
                def _(e): S.replay("sync", e, sems)
    es.close()
    return nc


_NC_CACHE = {}


def make_in_maps(inputs):
    f = lambda a: np.ascontiguousarray(np.asarray(a, dtype=np.float32))
    g = {k: f(v) for k, v in inputs.items()}
    rows = [g["norm_g"][0].reshape(8, 128), g["mu_shift"][0].reshape(13, 128), g["conv_w"][0].reshape(12, 128),
            g["w0"][0].reshape(4, 128), g["a0"][0].reshape(4, 128), g["k_k"][0].reshape(4, 128),
            g["k_a"][0].reshape(4, 128), g["r_k"][0].reshape(4, 128), g["ln_w"][0].reshape(4, 128),
            g["ln_b"][0].reshape(4, 128)]
    prm = f(np.concatenate(rows, axis=0).T)
    shared = {"prm": prm, "w_in": g["w_in"][0], "w_up": g["w_up"][0], "a_up": g["a_up"][0], "w_out": g["w_out"][0],
              "w_pg": g["w_pg"][0], "w_pp": g["w_pp"][0], "final_g": g["final_g"].reshape(1, D)}
    maps = []
    for c in range(NCORES):
        sl = slice(NS * c, NS * (c + 1))
        sc = g["state_conv"][0, sl]
        scT = f(sc.reshape(NS, 2, 4, 128).transpose(3, 2, 1, 0))
        ss = g["state_shift"][0, sl]
        ssT = f(ss.reshape(NS, 13, 128).transpose(2, 1, 0))
        m = dict(shared)
        m.update({"x": g["x_prompt"][c], "p": g["p_prompt"][0, c], "xs": g["x_sample"][sl, 0],
                  "psm": g["p_sample"][0, sl, 0], "scT": scT, "ssT": ssT,
                  "swkv": f(g["state_wkv"][0, sl].reshape(NS * 8, 4096))})
        maps.append(m)
    return maps


def assemble(results):
    y = np.stack([r["y"] for r in results], 0).astype(np.float32)
    ys = np.concatenate([r["ys"] for r in results], 0).reshape(NCORES * NS, 1, D).astype(np.float32)
    convp = np.stack([r["convp"].transpose(2, 1, 0).reshape(2, 512) for r in results], 0)[None].astype(np.float32)
    shiftp = np.stack([r["shiftp"].T.reshape(1664) for r in results], 0)[None].astype(np.float32)
    wkvp = np.stack([r["wkvp"].transpose(0, 2, 1) for r in results], 0)[None].astype(np.float32)
    convs = np.concatenate([r["convs"].transpose(3, 2, 1, 0).reshape(NS, 2, 512) for r in results], 0)[None]
    shifts = np.concatenate([r["shifts"].transpose(2, 1, 0).reshape(NS, 1664) for r in results], 0)[None]
    wkvs = np.concatenate([r["wkvs"].reshape(NS, 8, 64, 64) for r in results], 0)[None]
    return (y, ys, np.ascontiguousarray(convp), np.ascontiguousarray(shiftp), np.ascontiguousarray(wkvp),
            np.ascontiguousarray(convs.astype(np.float32)), np.ascontiguousarray(shifts.astype(np.float32)),
            np.ascontiguousarray(wkvs.astype(np.float32)))


def kernel(**inputs):
    if "nc" not in _NC_CACHE:
        _NC_CACHE["nc"] = build_nc()
    nc = _NC_CACHE["nc"]
    maps = make_in_maps(inputs)
    res = run_bass_kernel_spmd(nc, maps, core_ids=list(range(NCORES)))
    return assemble(res.results)
```

```python
import numpy as np
from contextlib import ExitStack
import concourse.bass as bass
import concourse.mybir as mybir
from concourse.bass_utils import run_bass_kernel_spmd

F32 = mybir.dt.float32
BF16 = mybir.dt.bfloat16
AF = mybir.ActivationFunctionType
ALU = mybir.AluOpType
AX = mybir.AxisListType

ENGS = ("tensor", "vector", "scalar", "gpsimd", "sync")
NCORES = 8
D = 1024
SEQ = 2048
NT = 256
NST = SEQ // NT
NS = 16
INW = 4224
CDEC = 0.6065306597126334
RMS_EPS = 1e-6
GN_EPS = 64e-5

G_, MU_, CW_, W0_, A0_, KK_, KA_, RK_, LW_, LB_ = 0, 8, 21, 33, 37, 41, 45, 49, 53, 57
NPRM_IN = 61
OMU_, OMKA_, GNE_, RME_ = 64, 77, 94, 95
NPRM = 96

DMA_SEMS = ["d_w1", "d_w2", "d_w3", "d_w4", "d_w5", "d_c1", "d_c2", "d_c3", "d_c4", "d_x", "d_g", "d_p", "d_y",
            "d_o", "d_s1", "d_s2", "d_l", "d_ws", "d_s3", "d_s4"]


class Sched:
    def __init__(self):
        self.ops = {e: [] for e in ENGS}
        self.count = {e: 0 for e in ENGS}
        self.dcount = {}
        self.know = {e: {} for e in ENGS}
        self.clock = {}
        self.last_write = {}
        self.readers = {}
        self.bank_tokens = {}
        self.bank_pending = {}
        self.all_events = {}
        self.nwaits = 0
        self.pending_nosig = {e: False for e in ENGS}

    def _need(self, eng, ev, waits):
        if ev is None:
            return
        s, v = ev
        if s == "tensor" and eng == "tensor":
            return
        if self.know[eng].get(s, 0) >= v:
            return
        if waits.get(s, 0) < v:
            waits[s] = v

    def _absorb(self, eng, waits):
        items = sorted(waits.items(), key=lambda kv: -len(self.clock.get(kv, ())))
        k = self.know[eng]
        kept = []
        for s, v in items:
            if k.get(s, 0) >= v:
                continue
            kept.append((s, v))
            for s2, v2 in self.clock.get((s, v), {}).items():
                if k.get(s2, 0) < v2:
                    k[s2] = v2
            if k.get(s, 0) < v:
                k[s] = v
        self.nwaits += len(kept)
        return kept

    def retire_bank(self, b):
        evs = []
        for t in self.bank_tokens.get(b, ()):
            if t in self.last_write:
                evs.append(self.last_write.pop(t))
            evs.extend(self.readers.pop(t, ()))
        self.bank_tokens[b] = set()
        self.bank_pending[b] = evs

    def _touch(self, t):
        if isinstance(t, tuple) and t and t[0] == "ps":
            b = t[1]
            if t not in self.bank_tokens.setdefault(b, set()):
                self.bank_tokens[b].add(t)
                if t not in self.last_write and t not in self.readers:
                    self.readers[t] = list(self.bank_pending.get(b, ()))

    def op(self, eng, fn, reads=(), writes=(), dma=None, sig=True):
        waits = {}
        for t in list(reads) + list(writes):
            self._touch(t)
        for r in reads:
            self._need(eng, self.last_write.get(r), waits)
            if isinstance(r, tuple) and r and r[0] == "ps":
                for u in self.bank_tokens.get(r[1], ()):
                    if u != r:
                        self._need(eng, self.last_write.get(u), waits)
                    for ev in self.readers.get(u, ()):
                        if ev[0] != eng:
                            self._need(eng, ev, waits)
        for w in writes:
            if isinstance(w, tuple) and w and w[0] == "ps":
                for u in self.bank_tokens.get(w[1], ()):
                    if u != w:
                        for ev in self.readers.get(u, ()):
                            self._need(eng, ev, waits)
        for w in writes:
            self._need(eng, self.last_write.get(w), waits)
            for ev in self.readers.get(w, ()):
                self._need(eng, ev, waits)
        kept = self._absorb(eng, waits)
        if dma is None and not sig:
            ev = (eng, self.count[eng] + 1)
            inc = None
            self.pending_nosig[eng] = True
        elif dma is None:
            self.count[eng] += 1
            ev = (eng, self.count[eng])
            inc = (eng, 1)
            self.pending_nosig[eng] = False
        else:
            self.dcount[dma] = self.dcount.get(dma, 0) + 16
            ev = (dma, self.dcount[dma])
            inc = (dma, 16)
        ck = dict(self.know[eng])
        ck[ev[0]] = ev[1]
        self.clock[ev] = ck
        self.all_events[ev[0]] = max(self.all_events.get(ev[0], 0), ev[1])
        for w in writes:
            self.last_write[w] = ev
            self.readers[w] = []
        for r in reads:
            if r not in writes:
                self.readers.setdefault(r, []).append(ev)
        self.ops[eng].append((kept, fn, inc))
        return ev

    def barrier(self, engs=ENGS):
        assert not any(self.pending_nosig.values()), "non-signalling op not followed by a signalling one"
        evs = list(self.all_events.items())
        for eng in engs:
            waits = {}
            for ev in evs:
                self._need(eng, ev, waits)
            kept = self._absorb(eng, waits)
            self.ops[eng].append((kept, None, None))

    def replay(self, eng, e, sems):
        for waits, fn, inc in self.ops[eng]:
            for s, v in waits:
                e.wait_ge(sems[s], v)
            if fn is not None:
                if inc is None:
                    fn(e)
                else:
                    fn(e).then_inc(sems[inc[0]], inc[1])


def build_nc(n_st=NST, do_sample=True):
    nc = bass.Bass("TRN2", target_bir_lowering=False)

    def din(name, shape, dt=F32):
        return nc.dram_tensor(name, list(shape), dt, kind="ExternalInput").ap()

    def dout(name, shape, dt=F32):
        return nc.dram_tensor(name, list(shape), dt, kind="ExternalOutput").ap()

    x_d = din("x", [SEQ, D]); p_d = din("p", [SEQ, 256])
    xs_d = din("xs", [NS, D]); psm_d = din("psm", [NS, 256])
    scT_d = din("scT", [128, 4, 2, NS])
    ssT_d = din("ssT", [128, 13, NS])
    swkv_d = din("swkv", [128, 4096])
    prm_d = din("prm", [128, NPRM_IN])
    win_d = din("w_in", [D, INW]); wup_d = din("w_up", [64, 512]); aup_d = din("a_up", [64, 512])
    wout_d = din("w_out", [D, D]); wpg_d = din("w_pg", [D, D]); wpp_d = din("w_pp", [256, D])
    fg_d = din("final_g", [1, D])

    y_d = dout("y", [SEQ, D]); ys_d = dout("ys", [NS, D])
    convp_d = dout("convp", [128, 4, 2])
    shiftp_d = dout("shiftp", [128, 13])
    wkvp_d = dout("wkvp", [8, 64, 64])
    convs_d = dout("convs", [128, 4, 2, NS])
    shifts_d = dout("shifts", [128, 13, NS])
    wkvs_d = dout("wkvs", [128, 4096])
    scr1_d = nc.dram_tensor("scr1", [NS, 8, 6, 64], F32, kind="Internal").ap()
    scr2_d = nc.dram_tensor("scr2", [NS, 8, 64], F32, kind="Internal").ap()
    ym_d = nc.dram_tensor("scr_ym", [128, 8, SEQ + 128], BF16, kind="Internal").ap()

    S = Sched()
    es = ExitStack()

    def sb(name, shape, dt=F32):
        return es.enter_context(nc.sbuf_tensor("s_" + name, list(shape), dt))

    win = sb("win", [128, 8, INW], BF16)
    wua = sb("wua", [128, 512], BF16)
    prm = sb("prm", [128, NPRM])
    ident = sb("ident", [128, 128], BF16)
    identf = sb("identf", [128, 128])
    blk = sb("blk", [128, 128])
    blkb = sb("blkb", [128, 128], BF16)
    blk64 = sb("blk64", [128, 128], BF16)
    blkrk = sb("blkrk", [128, 4, 128], BF16)
    maskA = sb("maskA", [128, 256], BF16)
    maskN = sb("maskN", [128, 128], BF16)
    scanm = sb("scanm", [128, NT])
    zlast = sb("zlast", [128, 13])
    uext = sb("uext", [128, 4, NT + 2])
    T32 = sb("T32", [128, 4, 128])
    T16 = sb("T16", [128, 4, 128], BF16)
    pcs = sb("pcs", [128, 4, 4])
    xnTf = [sb("xnTa", [128, 8 * (NT + 2)], BF16), sb("xnTb", [128, 8 * (NT + 2)], BF16)]
    xnT2 = [t_[:, :].rearrange("p (c n) -> p c n", c=8) for t_ in xnTf]
    CUR = {"x": 0}
    ymixT = sb("ymixT", [128, 8, NT], BF16)
    stat = sb("stat", [128, 32])
    thad = sb("thad", [128, NT], BF16)

    class TH:
        pass

    def mk_thread(g, ring, ybank):
        B = TH()
        B.g = g
        B.arena = sb(f"arena{g}", [128, 20, NT])
        B.xb = sb(f"xb{g}", [128, D], BF16)
        B.KR = sb(f"KR{g}", [128, 2, 2, NT], BF16)
        B.KB = sb(f"KB{g}", [128, 2, 2, NT], BF16)
        B.VT = sb(f"VT{g}", [128, 2, NT], BF16)
        B.bv = sb(f"bv{g}", [128, 2, NT])
        B.TK = sb(f"TK{g}", [128, 2, 4, 256], BF16)
        B.SC = sb(f"SC{g}", [128, 2, 4, 512], BF16)
        B.D7 = sb(f"D7{g}", [128, 7, 4, 128], BF16)
        B.dA = [B.D7[:, 0], B.D7[:, 1]]
        B.dB = [B.D7[:, 2], B.D7[:, 3]]
        B.dP = [B.D7[:, 4], B.D7[:, 5]]
        B.NB = B.D7[:, 6]
        B.W1 = sb(f"W1{g}", [128, 4, 64], BF16)
        B.U1b = sb(f"U1b{g}", [128, 2, 256], BF16)
        B.KPT = sb(f"KPT{g}", [128, 2, 2, 128], BF16)
        B.Y1T = sb(f"Y1T{g}", [128, 2, 2, 128])
        B.H1 = sb(f"H1{g}", [128, 8, 128])
        B.U0b = sb(f"U0b{g}", [128, 256], BF16)
        B.Tsum = sb(f"Tsum{g}", [128, 2, 128])
        B.ring = ring
        B.rs = {"ri": 0}
        B.ybank = ybank
        B.pbank = g
        return B

    psb = [es.enter_context(nc.psum_tensor(f"psb{i}", [128, 512], F32)) for i in range(6)]
    ptbs = [es.enter_context(nc.psum_tensor(f"ptb16_{i}", [128, 1024], BF16)) for i in range(2)]
    TH0 = mk_thread(0, [0, 1], 2)
    TH1 = mk_thread(1, [3, 4], 5)
    THS = [TH0, TH1]
    import copy as _copy
    CTH = []
    for B_ in THS:
        B_.ring4 = list(B_.ring[0:2]) + [B_.ybank, 6 + B_.g]
        B_.rs4 = {"ri": 0}
        Bc = _copy.copy(B_)
        Bc.ring = B_.ring4
        Bc.rs = B_.rs4
        CTH.append(Bc)
        Bw = _copy.copy(B_)
        Bw.ring = list(B_.ring[0:2]) + [B_.ybank, 6 + B_.g]
        Bw.rs = {"ri": 0}
        B_.wide = Bw
        Bn = _copy.copy(B_)
        Bn.ring = [B_.ring[0]]
        Bn.rs = {"ri": 0}
        B_.chain = Bn
        B_.ring3 = list(B_.ring[0:2]) + [6 + B_.g]
    WC = _copy.copy(TH0)
    WC.ring = [TH0.ring[1]]
    WC.rs = {"ri": 0}

    sem_names = list(ENGS) + DMA_SEMS + ["d_x1", "d_g1", "d_p1", "d_y1", "d_m0", "d_m1", "d_ym", "d_g2", "d_g3", "d_p2", "d_p3", "d_y2", "d_y3", "d_m2", "d_m3", "d_l2", "d_ws2", "d_s1b"] + [f"d_stg{i}" for i in range(6)]
    sems = {n: es.enter_context(nc.semaphore(n)) for n in sem_names}

    def TE(fn, r, w, sig=True): return S.op("tensor", fn, r, w, sig=sig)
    def VE(fn, r, w): return S.op("vector", fn, r, w)
    def AE(fn, r, w): return S.op("scalar", fn, r, w)
    def GE(fn, r, w): return S.op("gpsimd", fn, r, w)
    def DMA(fn, r, w, sem, q="sync"): return S.op(q, fn, r, w, dma=sem)

    gen = {"g": 0}

    oring = {"ri": 0}

    ptf32 = [pt_[:, :].bitcast(F32) for pt_ in ptbs]

    def psalloc(B):
        if getattr(B, "gring", False):
            b = oring["ri"] % 6
            oring["ri"] += 1
        else:
            b = B.ring[B.rs["ri"] % len(B.ring)]
            B.rs["ri"] += 1
        gen["g"] += 1
        S.retire_bank(b)
        return (psb[b] if b < 6 else ptf32[b - 6]), ("ps", b, gen["g"])

    def psy(B):
        gen["g"] += 1
        S.retire_bank(B.ybank)
        return psb[B.ybank], ("ps", B.ybank, gen["g"])

    def ptalloc(B):
        gen["g"] += 1
        vb = 6 + B.pbank
        S.retire_bank(vb)
        return ptbs[B.pbank], ("ps", vb, gen["g"])

    def pcol(c):
        return prm[:, c:c + 1]

    def ar(B, k, n=NT):
        return B.arena[:, k, 0:n]

    def at(B, k):
        return ("ar", B.g, k)

    def arb(B, k, n=NT):
        return B.arena[:, k, :].bitcast(BF16)[:, 0:n]

    def tk(B, *name):
        return (B.g,) + name

    cast_rr = {"i": 0}

    def load_cast(chunks, slots, tag):
        for idx, ch in enumerate(chunks):
            dst, src, dtok = ch[0], ch[1], ch[2]
            scl = ch[3] if len(ch) > 3 else None
            k = idx % len(slots)
            sap, stok = slots[k]
            W = dst.shape[-1]
            DMA(lambda e, sap=sap, src=src, W=W: e.dma_start(out=sap[:, 0:W], in_=src), [], [stok], f"d_stg{k}")
            if scl is not None:
                if cast_rr["i"] % 2 == 0:
                    AE(lambda e, dst=dst, sap=sap, W=W, scl=scl: e.mul(out=dst, in_=sap[:, 0:W], mul=scl),
                       [stok, "prm"], [dtok])
                else:
                    VE(lambda e, dst=dst, sap=sap, W=W, scl=scl: e.tensor_scalar(out=dst, in0=sap[:, 0:W], scalar1=scl,
                                                                                 scalar2=None, op0=ALU.mult),
                       [stok, "prm"], [dtok])
            elif cast_rr["i"] % 2 == 0:
                AE(lambda e, dst=dst, sap=sap, W=W: e.copy(out=dst, in_=sap[:, 0:W]), [stok], [dtok])
            else:
                VE(lambda e, dst=dst, sap=sap, W=W: e.tensor_copy(out=dst, in_=sap[:, 0:W]), [stok], [dtok])
            cast_rr["i"] += 1

    DMA(lambda e: e.dma_start(out=prm[:, 0:NPRM_IN], in_=prm_d), [], ["prm"], "d_c1")

    tmpc = TH1.arena[:, 15, :]
    GE(lambda e: e.memset(identf[:], 1.0), [], ["identf"])
    GE(lambda e: e.affine_select(out=identf[:], in_=identf[:], pattern=[[-1, 128]], compare_op=ALU.is_equal,
                                 fill=0.0, base=0, channel_multiplier=1), ["identf"], ["identf"])
    VE(lambda e: e.tensor_copy(out=ident[:], in_=identf[:]), ["identf"], ["ident"])
    GE(lambda e: e.memset(blk[:], 0.0), [], ["blk"])
    GE(lambda e: e.memset(blk[0:64, 0:64], 1.0), ["blk"], ["blk"])
    GE(lambda e: e.memset(blk[64:128, 64:128], 1.0), ["blk"], ["blk"])
    VE(lambda e: e.tensor_scalar(out=blk64[:], in0=blk[:], scalar1=1.0 / 64, scalar2=None, op0=ALU.mult),
       ["blk"], ["blk64"])
    VE(lambda e: e.tensor_copy(out=blkb[:], in_=blk[:]), ["blk"], ["blkb"])
    GE(lambda e: e.memset(prm[:, GNE_:GNE_ + 1], GN_EPS), ["prm"], ["prm"])
    GE(lambda e: e.memset(prm[:, RME_:RME_ + 1], RMS_EPS), ["prm"], ["prm"])
    for j in range(4):
        VE(lambda e, j=j: e.tensor_scalar(out=blkrk[:, j, :], in0=blk[:], scalar1=pcol(RK_ + j), scalar2=None,
                                          op0=ALU.mult), ["blk", "prm"], ["blkrk"])
    GE(lambda e: e.memset(tmpc[:], 1.0), [], ["tmpc"])
    GE(lambda e: e.affine_select(out=tmpc[:, 0:128], in_=tmpc[:, 0:128], pattern=[[1, 128]], compare_op=ALU.is_ge,
                                 fill=0.0, base=-1, channel_multiplier=-1), ["tmpc"], ["tmpc"])
    GE(lambda e: e.affine_select(out=tmpc[:, 128:256], in_=tmpc[:, 128:256], pattern=[[1, 128]],
                                 compare_op=ALU.is_ge, fill=0.0, base=0, channel_multiplier=-1),
       ["tmpc"], ["tmpc"])
    GE(lambda e: e.memset(tmpc[0:64, 64:128], 0.0), ["tmpc"], ["tmpc"])
    GE(lambda e: e.memset(tmpc[0:64, 192:256], 0.0), ["tmpc"], ["tmpc"])
    VE(lambda e: e.tensor_copy(out=maskA[:], in_=tmpc[:]), ["tmpc"], ["maskA"])
    GE(lambda e: e.memset(tmpc[:, 0:128], 1.0), ["tmpc"], ["tmpc"])
    GE(lambda e: e.affine_select(out=tmpc[:, 0:128], in_=tmpc[:, 0:128], pattern=[[-1, 128]],
                                 compare_op=ALU.is_ge, fill=0.0, base=-1, channel_multiplier=1),
       ["tmpc"], ["tmpc"])
    GE(lambda e: e.memset(tmpc[64:128, 0:64], 0.0), ["tmpc"], ["tmpc"])
    VE(lambda e: e.tensor_copy(out=maskN[:], in_=tmpc[:, 0:128]), ["tmpc"], ["maskN"])
    GE(lambda e: e.memset(scanm[:], 1.0), [], ["scanm"])
    GE(lambda e: e.memset(scanm[:].rearrange("p (c t) -> p c t", t=64)[:, :, 0:1], 0.0), ["scanm"], ["scanm"])
    VE(lambda e: e.tensor_scalar(out=prm[:, OMU_:OMU_ + 13], in0=prm[:, MU_:MU_ + 13], scalar1=-1.0, scalar2=1.0,
                                 op0=ALU.mult, op1=ALU.add), ["prm"], ["prm"])
    VE(lambda e: e.tensor_scalar(out=prm[:, OMKA_:OMKA_ + 4], in0=prm[:, KA_:KA_ + 4], scalar1=-1.0, scalar2=1.0,
                                 op0=ALU.mult, op1=ALU.add), ["prm"], ["prm"])
    GE(lambda e: e.memset(zlast[:], 0.0), [], [("zl", q) for q in range(13)])
    GE(lambda e: e.memset(uext[:], 0.0), [], [("u", j) for j in range(4)])
    GE(lambda e: e.memset(T32[:], 0.0), [], [("T", j) for j in range(4)])
    GE(lambda e: e.memset(T16[:], 0.0), [], [("T16", j) for j in range(4)])
    DMA(lambda e: e.dma_start(out=wua[0:64, :], in_=wup_d), [], ["wua"], "d_w2", q="gpsimd")
    DMA(lambda e: e.dma_start(out=wua[64:128, :], in_=aup_d), [], ["wua"], "d_w2", q="gpsimd")
    S.barrier(("gpsimd", "vector", "scalar"))

    def p1_gen(Bs, xsrc, tok0s, rows, col0s, par, d7stage=False, carry=None):
        if d7stage:
            xts = {B.g: B.D7[:, 0:4].rearrange("p a b c -> p (a b c)").bitcast(F32) for B in Bs}
            xtt = {B.g: [tk(B, "dA0"), tk(B, "dA1"), tk(B, "dB0"), tk(B, "dB1")] for B in Bs}
        else:
            xts = {B.g: B.arena[:, 12:16, :].rearrange("p a b -> p (a b)") for B in Bs}
            xtt = {B.g: [at(B, 12), at(B, 13), at(B, 14), at(B, 15)] for B in Bs}
        xdst = xnT2[par]
        for B, tok0 in zip(Bs, tok0s):
            DMA(lambda e, B=B, tok0=tok0: e.dma_start(out=xts[B.g][0:rows, :], in_=xsrc[tok0:tok0 + rows, :]),
                [], xtt[B.g], f"d_x{B.g}" if B.g else "d_x")
        yield
        for B in Bs:
            so = 8 * B.g
            AE(lambda e, B=B, so=so: e.activation(out=B.xb[0:rows, :], in_=xts[B.g][0:rows, :], func=AF.Square,
                                                  accum_out=stat[0:rows, so:so + 1]),
               xtt[B.g], [tk(B, "xb"), tk(B, "st0")])
        yield
        for B in Bs:
            so = 8 * B.g
            AE(lambda e, so=so: e.activation(out=stat[0:rows, so + 1:so + 2], in_=stat[0:rows, so:so + 1],
                                             func=AF.Ln, scale=1.0 / D, bias=prm[0:rows, RME_:RME_ + 1]),
               [tk(B, "st0"), "prm"], [tk(B, "st1")])
        for B in Bs:
            so = 8 * B.g
            AE(lambda e, so=so: e.activation(out=stat[0:rows, so + 2:so + 3], in_=stat[0:rows, so + 1:so + 2],
                                             func=AF.Exp, scale=-0.5), [tk(B, "st1")], [tk(B, "st2")])
        yield
        for B in Bs:
            so = 8 * B.g
            VE(lambda e, B=B, so=so: e.tensor_scalar(out=B.xb[0:rows, :], in0=xts[B.g][0:rows, :],
                                                     scalar1=stat[0:rows, so + 2:so + 3], scalar2=None, op0=ALU.mult),
               xtt[B.g] + [tk(B, "st2"), tk(B, "xb")], [tk(B, "xb")])
        yield
        pts = {}
        for B in Bs:
            pt, tk_ = ptalloc(B)
            t = tk_ + ("x",)
            pts[B.g] = (pt, t)
            for c in range(8):
                TE(lambda e, c=c, pt=pt, B=B: e.transpose(pt[:, c * 128:c * 128 + rows],
                                                          B.xb[0:rows, c * 128:(c + 1) * 128],
                                                          ident[0:rows, 0:rows]), [tk(B, "xb"), "ident"], [t], sig=(c == 7))
        yield
        for B, col0 in zip(Bs, col0s):
            pt, t = pts[B.g]
            AE(lambda e, pt=pt, col0=col0: e.copy(out=xdst[:, :, 1 + col0:1 + col0 + rows],
                                                  in_=pt[:, :].rearrange("p (c t) -> p c t", t=128)[:, :, 0:rows]),
               [t], [("xnT", par, B.g)])
            yield
        if carry == "zero":
            GE(lambda e: e.memset(xdst[:, :, 0:1], 0.0), [], [("xnT", par, "c")])
        elif carry == "prev":
            GE(lambda e: e.tensor_copy(out=xdst[:, :, 0:1], in_=xnT2[1 - par][:, :, NT:NT + 1]),
               [("xnT", 1 - par, 1)], [("xnT", par, "c")])
        yield

    def load_norm_T(Bs, xsrc, tok0s, rows, col0s, par=0, carry=None):
        for _ in p1_gen(Bs, xsrc, tok0s, rows, col0s, par, carry=carry):
            pass

    stg = []
    for g_ in range(2):
        fl = THS[g_].arena[:].rearrange("p a b -> p (a b)")
        for k_ in range(3):
            stg.append((fl[:, k_ * 1056:(k_ + 1) * 1056], ("stg", g_, k_)))
    chunks = []
    for cb in range(0, INW, 1056):
        for c in range(8):
            chunks.append((win[:, c, cb:cb + 1056], win_d[c * 128:(c + 1) * 128, cb:cb + 1056], "win",
                           pcol(G_ + c)))
    if do_sample:
        for _ in p1_gen([TH0], xs_d, [0], NS, [0], 0, d7stage=True):
            pass
    load_cast(chunks, stg, "win")
    S.barrier()

    def projx(B, m, n, par=None, shift=False):
        if par is None:
            par = CUR["x"]
        xa = xnT2[par]
        xnt = [("xnT", par, 0), ("xnT", par, 1)]
        c0, nn = (0, n + 1) if shift else (1, n)
        if shift:
            xnt = xnt + [("xnT", par, "c")]
        pz, tok = psalloc(B)
        t = tok + ("z",)
        for c in range(8):
            TE(lambda e, c=c, pz=pz, xa=xa: e.matmul(pz[:, 0:nn], lhsT=win[:, c, m * 128:(m + 1) * 128],
                                                     rhs=xa[:, c, c0:c0 + nn], start=(c == 0), stop=(c == 7)),
               ["win"] + xnt, [t], sig=(c == 7))
        return pz, t

    def conv_gen(Bs, jf, n, sample, scT=None, convo=None, slots=(0, 1, 2)):
        zc, cv, sl = slots
        P = {}
        for B in Bs:
            P[B.g] = projx(B, 4 + jf(B), n)
        yield
        for B in Bs:
            pz, t = P[B.g]
            AE(lambda e, pz=pz, B=B: e.copy(out=ar(B, zc, n), in_=pz[:, 0:n]), [t], [at(B, zc)])
        yield
        for B in Bs:
            P[B.g] = projx(B, 8 + jf(B), n)
        yield
        if not sample:
            for B in Bs:
                pz, t = P[B.g]
                j = jf(B)
                VE(lambda e, pz=pz, j=j, B=B: e.tensor_tensor(out=uext[:, j, 2:2 + n], in0=ar(B, zc, n), in1=pz[:, 0:n],
                                                              op=ALU.mult), [t, at(B, zc)], [("u", j)])
            yield
            for B in Bs:
                j = jf(B)
                VE(lambda e, j=j, B=B: e.tensor_scalar(out=ar(B, cv, n), in0=uext[:, j, 0:n], scalar1=pcol(CW_ + j),
                                                       scalar2=None, op0=ALU.mult), [("u", j), "prm"], [at(B, cv)])
            yield
            for tap in (1, 2):
                for B in Bs:
                    j = jf(B)
                    VE(lambda e, j=j, B=B, tap=tap: e.scalar_tensor_tensor(
                        out=ar(B, cv, n), in0=uext[:, j, tap:n + tap], scalar=pcol(CW_ + 4 * tap + j),
                        in1=ar(B, cv, n), op0=ALU.mult, op1=ALU.add), [("u", j), "prm", at(B, cv)], [at(B, cv)])
                yield
            for B in Bs:
                j = jf(B)
                GE(lambda e, j=j: e.tensor_copy(out=uext[:, j, 0:2], in_=uext[:, j, n:n + 2]), [("u", j)], [("u", j)])
            yield
        else:
            for B in Bs:
                pz, t = P[B.g]
                j = jf(B)
                VE(lambda e, pz=pz, j=j, B=B: e.tensor_tensor(out=convo[:, j, 1, :], in0=ar(B, zc, n), in1=pz[:, 0:n],
                                                              op=ALU.mult), [t, at(B, zc)], [("cvo", j)])
                GE(lambda e, j=j: e.tensor_copy(out=convo[:, j, 0, :], in_=scT[:, j, 1, :]), ["scT"], [("cvo0", j)])
                VE(lambda e, j=j, B=B: e.tensor_scalar(out=ar(B, cv, n), in0=scT[:, j, 0, :], scalar1=pcol(CW_ + j),
                                                       scalar2=None, op0=ALU.mult), ["scT", "prm"], [at(B, cv)])
                VE(lambda e, j=j, B=B: e.scalar_tensor_tensor(out=ar(B, cv, n), in0=scT[:, j, 1, :],
                                                              scalar=pcol(CW_ + 4 + j), in1=ar(B, cv, n),
                                                              op0=ALU.mult, op1=ALU.add),
                   ["scT", "prm", at(B, cv)], [at(B, cv)])
                VE(lambda e, j=j, B=B: e.scalar_tensor_tensor(out=ar(B, cv, n), in0=convo[:, j, 1, :],
                                                              scalar=pcol(CW_ + 8 + j), in1=ar(B, cv, n),
                                                              op0=ALU.mult, op1=ALU.add),
                   [("cvo", j), "prm", at(B, cv)], [at(B, cv)])
        for B in Bs:
            P[B.g] = projx(B, jf(B), n)
        yield
        for B in Bs:
            pz, t = P[B.g]
            VE(lambda e, pz=pz, B=B: e.tensor_tensor(out=ar(B, cv, n), in0=ar(B, cv, n), in1=pz[:, 0:n], op=ALU.mult),
               [t, at(B, cv)], [at(B, cv)])
        yield
        for B in Bs:
            P[B.g] = projx(B, 12 + jf(B), n)
        yield
        for B in Bs:
            pz, t = P[B.g]
            AE(lambda e, pz=pz, B=B: e.activation(out=ar(B, sl, n), in_=pz[:, 0:n], func=AF.Sigmoid), [t], [at(B, sl)])
        yield
        for B in Bs:
            pz, t = P[B.g]
            VE(lambda e, pz=pz, B=B: e.tensor_tensor(out=ar(B, cv, n), in0=ar(B, cv, n), in1=pz[:, 0:n], op=ALU.mult),
               [t, at(B, cv)], [at(B, cv)])
        yield
        for B in Bs:
            j = jf(B)
            GE(lambda e, j=j, B=B: e.tensor_tensor(out=ymixT[:, j, 0:n], in0=ar(B, cv, n), in1=ar(B, sl, n),
                                                   op=ALU.mult), [at(B, cv), at(B, sl)], [("ym", j)])
        yield

    def conv_branch(Bs, jf, n, sample, scT=None, convo=None):
        for _ in conv_gen(Bs, jf, n, sample, scT, convo):
            pass

    FILL = {"gen": None, "ok": False, "busy": False}

    def fill():
        if FILL["ok"] and FILL["gen"] is not None and not FILL["busy"]:
            FILL["busy"] = True
            try:
                next(FILL["gen"])
                if FILL.get("k2"):
                    next(FILL["gen"])
            except StopIteration:
                FILL["gen"] = None
            FILL["busy"] = False

    def flush_fill():
        if FILL["gen"] is not None:
            FILL["busy"] = True
            for _ in FILL["gen"]:
                pass
            FILL["busy"] = False
            FILL["gen"] = None

    def ck(C):
        return (C.g, getattr(C, "sb", 0))

    def shift_chunk(Cs, qf, n, dkf, sample, ssT=None, zso=None, par=None, save_last=False):
        P = {}
        o1 = 0 if sample else 1
        for C in Cs:
            P[ck(C)] = projx(C, 16 + qf(C), n, par, shift=not sample)
        for C in Cs:
            pz, t = P[ck(C)]
            q, dk = qf(C), dkf(C)
            AE(lambda e, pz=pz, C=C, q=q, dk=dk: e.mul(out=ar(C, dk, n), in_=pz[:, o1:o1 + n], mul=pcol(OMU_ + q)),
               [t, "prm"], [at(C, dk)])
        if not sample:
            for C in Cs:
                pz, t = P[ck(C)]
                q, dk = qf(C), dkf(C)
                VE(lambda e, pz=pz, C=C, q=q, dk=dk: e.scalar_tensor_tensor(
                    out=ar(C, dk, n), in0=pz[:, 0:n], scalar=pcol(MU_ + q), in1=ar(C, dk, n),
                    op0=ALU.mult, op1=ALU.add), [t, "prm", at(C, dk)], [at(C, dk)])
            if save_last:
                for C in Cs:
                    pz, t = P[ck(C)]
                    q, dk = qf(C), dkf(C)
                    AE(lambda e, pz=pz, q=q: e.copy(out=zlast[:, q:q + 1], in_=pz[:, n:n + 1]), [t, at(C, dk)],
                       [("zl", q)])
            fill()
            fill()
        else:
            for C in Cs:
                pz, t = P[ck(C)]
                q, dk = qf(C), dkf(C)
                VE(lambda e, C=C, q=q, dk=dk: e.scalar_tensor_tensor(out=ar(C, dk, n), in0=ssT[:, q, :],
                                                                     scalar=pcol(MU_ + q), in1=ar(C, dk, n),
                                                                     op0=ALU.mult, op1=ALU.add),
                   ["ssT", "prm", at(C, dk)], [at(C, dk)])
                AE(lambda e, pz=pz, q=q: e.copy(out=zso[:, q, :], in_=pz[:, 0:n]), [t], [("zso", q)])

    WD = 19

    def thad_ops(B, n):
        AE(lambda e: e.activation(out=thad[0:64, 0:n], in_=ar(B, WD, n)[0:64, :], func=AF.Tanh), [at(B, WD)], ["thad"])
        GE(lambda e: e.tensor_copy(out=thad[64:128, 0:n], in_=ar(B, WD, n)[64:128, :]), [at(B, WD), "thad"], ["thad"])

    def wdad_chunk(B, n, sample, ssT=None, zso=None, par=None, save_last=False, defer=False):
        shift_chunk([B], lambda C: 12, n, lambda C: WD, sample, ssT, zso, par, save_last)
        if not defer:
            thad_ops(B, n)

    ZR, ZK, ZV, SG, CSG, PIN, AA, S0, S1 = 0, 1, 2, 3, 4, 5, 6, 7, 8
    CEX = PEX = SG
    PINV = CSG
    LASTF = {"v": False}

    def sl(C, k):
        return getattr(C, "sb", 0) + k

    def pair_pre(Cs, jf, n, sample, bvf, ssT=None, zso=None):
        FILL["ok"] = True
        sv = LASTF["v"] and not sample
        shift_chunk(Cs, lambda C: 4 + jf(C), n, lambda C: sl(C, ZK), sample, ssT, zso, save_last=sv)
        P = {}

        def A(C, k):
            return ar(C, sl(C, k), n)

        def Ab(C, k):
            return arb(C, sl(C, k), n)

        def T(C, k):
            return at(C, sl(C, k))

        def each(f):
            for C in Cs:
                f(C, jf(C))
            fill()

        def mm1(C, j, lhs, rhs, rtok, nm):
            pz, t = psalloc(C)
            t = t + (nm,)
            TE(lambda e, pz=pz: e.matmul(pz[:, 0:n], lhsT=lhs, rhs=rhs, start=True, stop=True), rtok, [t])
            P[ck(C)] = (pz, t)

        each(lambda C, j: GE(lambda e: e.tensor_scalar(out=A(C, S0), in0=A(C, ZK), scalar1=pcol(KK_ + j),
                                                       scalar2=0.0, op0=ALU.mult, op1=ALU.add),
                             [T(C, ZK), "prm"], [T(C, S0)]))
        each(lambda C, j: AE(lambda e: e.activation(out=Ab(C, S1), in_=A(C, ZK), func=AF.Square,
                                                    scale=pcol(KK_ + j)), [T(C, ZK), "prm"], [T(C, S1)]))
        each(lambda C, j: mm1(C, j, blkb[:], Ab(C, S1), ["blkb", T(C, S1)], "n"))
        each(lambda C, j: VE(lambda e, pz=P[ck(C)][0]: e.tensor_scalar(out=A(C, S1), in0=pz[:, 0:n], scalar1=1e-24,
                                                                       scalar2=None, op0=ALU.max),
                             [P[ck(C)][1], T(C, S1)], [T(C, S1)]))
        shift_chunk(Cs, lambda C: jf(C), n, lambda C: sl(C, ZR), sample, ssT, zso, save_last=sv)
        shift_chunk(Cs, lambda C: 8 + jf(C), n, lambda C: sl(C, ZV), sample, ssT, zso, save_last=sv)
        each(lambda C, j: mm1(C, j, wua[0:64, j * 128:(j + 1) * 128], thad[0:64, 0:n], ["wua", "thad"], "w"))
        each(lambda C, j: AE(lambda e, pz=P[ck(C)][0]: e.activation(out=A(C, SG), in_=pz[:, 0:n], func=AF.Sigmoid,
                                                                    bias=pcol(W0_ + j)), [P[ck(C)][1], "prm"],
                             [T(C, SG)]))
        each(lambda C, j: mm1(C, j, wua[64:128, j * 128:(j + 1) * 128], thad[64:128, 0:n], ["wua", "thad"], "a"))
        each(lambda C, j: AE(lambda e, pz=P[ck(C)][0]: e.activation(out=A(C, AA), in_=pz[:, 0:n], func=AF.Sigmoid,
                                                                    bias=pcol(A0_ + j)), [P[ck(C)][1], "prm"],
                             [T(C, AA)]))
        def bvd(C):
            return bvf(C)[0]

        def bvb(C):
            return bvf(C)[0].bitcast(BF16)[:, 0:n]
        each(lambda C, j: VE(lambda e: e.scalar_tensor_tensor(out=bvd(C), in0=A(C, ZK), scalar=pcol(KA_ + j),
                                                              in1=A(C, AA), op0=ALU.mult, op1=ALU.mult),
                             [T(C, ZK), T(C, AA), "prm"], [bvf(C)[1]]))
        each(lambda C, j: VE(lambda e: e.scalar_tensor_tensor(out=A(C, ZK), in0=A(C, ZK), scalar=pcol(OMKA_ + j),
                                                              in1=bvd(C), op0=ALU.mult, op1=ALU.add),
                             [T(C, ZK), bvf(C)[1], "prm"], [T(C, ZK)]))
        each(lambda C, j: GE(lambda e: e.tensor_tensor(out=bvb(C), in0=A(C, ZR), in1=A(C, ZK),
                                                       op=ALU.mult), [T(C, ZR), T(C, ZK), bvf(C)[1]], [bvf(C)[1]]))
        each(lambda C, j: mm1(C, j, blkrk[:, j, :], bvb(C), ["blkrk", bvf(C)[1]], "b"))
        each(lambda C, j: VE(lambda e, pz=P[ck(C)][0], d=bvf(C)[0]: e.tensor_tensor(out=d, in0=A(C, ZV),
                                                                                    in1=pz[:, 0:n], op=ALU.mult),
                             [P[ck(C)][1], T(C, ZV), bvf(C)[1]], [bvf(C)[1]]))
        if not sample:
            each(lambda C, j: VE(lambda e: e.tensor_tensor_scan(out=A(C, CSG), data0=scanm[:, 0:n],
                                                                data1=A(C, SG), initial=0.0, op0=ALU.mult,
                                                                op1=ALU.add), ["scanm", T(C, SG)], [T(C, CSG)]))
            each(lambda C, j: GE(lambda e: e.tensor_tensor(out=A(C, CEX), in0=A(C, CSG), in1=A(C, SG),
                                                           op=ALU.subtract), [T(C, CSG), T(C, SG)], [T(C, CEX)]))
        if FILL.get("flush_at_ln"):
            flush_fill()
        FILL["ok"] = False
        each(lambda C, j: AE(lambda e: e.activation(out=A(C, S1), in_=A(C, S1), func=AF.Ln),
                             [T(C, S1)], [T(C, S1)]))
        each(lambda C, j: AE(lambda e: e.activation(out=A(C, S1), in_=A(C, S1), func=AF.Exp, scale=-0.5),
                             [T(C, S1)], [T(C, S1)]))
        if not sample:
            each(lambda C, j: AE(lambda e: e.activation(out=A(C, PIN), in_=A(C, CSG), func=AF.Exp,
                                                        scale=-CDEC), [T(C, CSG)], [T(C, PIN)]))
            each(lambda C, j: AE(lambda e: e.activation(out=A(C, PEX), in_=A(C, CEX), func=AF.Exp,
                                                        scale=-CDEC), [T(C, CEX)], [T(C, PEX)]))
            each(lambda C, j: AE(lambda e: e.activation(out=A(C, PINV), in_=A(C, CSG), func=AF.Exp,
                                                        scale=CDEC), [T(C, CSG), T(C, PIN)], [T(C, PINV)]))
            each(lambda C, j: GE(lambda e: e.tensor_copy(
                out=pcs[:, j, :], in_=A(C, PIN).rearrange("p (c t) -> p c t", t=64)[:, :, 63]),
                [T(C, PIN)], [("pcs", j)]))
        else:
            each(lambda C, j: AE(lambda e: e.activation(out=A(C, PIN), in_=A(C, SG), func=AF.Exp,
                                                        scale=-CDEC), [T(C, SG)], [T(C, PIN)]))
        each(lambda C, j: VE(lambda e: e.tensor_tensor(out=A(C, S0), in0=A(C, S0), in1=A(C, S1),
                                                       op=ALU.mult), [T(C, S0), T(C, S1)], [T(C, S0)]))
        each(lambda C, j: GE(lambda e: e.tensor_tensor(out=A(C, S1), in0=A(C, S0), in1=A(C, AA),
                                                       op=ALU.mult), [T(C, S0), T(C, AA), T(C, S1)], [T(C, S1)]))
        if not sample:
            each(lambda C, j: VE(lambda e: e.tensor_tensor(out=C.KR[:, C.ppi, 1, 0:n], in0=A(C, ZR), in1=A(C, PIN),
                                                           op=ALU.mult), [T(C, ZR), T(C, PIN)],
                                 [tk(C, "KR1", C.ppi)]))
            each(lambda C, j: VE(lambda e: e.tensor_tensor(out=C.KR[:, C.ppi, 0, 0:n], in0=A(C, S0),
                                                           in1=A(C, PEX), op=ALU.mult),
                                 [T(C, S0), T(C, PEX)], [tk(C, "KR0", C.ppi)]))
            each(lambda C, j: VE(lambda e: e.tensor_tensor(out=C.KB[:, C.ppi, 0, 0:n], in0=A(C, ZK),
                                                           in1=A(C, PINV), op=ALU.mult),
                                 [T(C, ZK), T(C, PINV)], [tk(C, "KB0", C.ppi)]))
            each(lambda C, j: VE(lambda e: e.scalar_tensor_tensor(out=C.KB[:, C.ppi, 1, 0:n], in0=A(C, S1),
                                                                  scalar=-1.0, in1=A(C, PINV), op0=ALU.mult,
                                                                  op1=ALU.mult), [T(C, S1), T(C, PINV)],
                                 [tk(C, "KB1", C.ppi)]))
            each(lambda C, j: GE(lambda e: e.tensor_copy(out=C.VT[:, C.ppi, 0:n], in_=A(C, ZV)), [T(C, ZV)],
                                 [tk(C, "VT", C.ppi)]))
        return ZR, PIN, ZK, ZV, S0, S1

    def post_pair(Bs, jf, n, ysf, bvf):
        P = {}
        Y = {}

        def so(B):
            return getattr(B, "soff", 0)

        def pk(B):
            return (B.g, getattr(B, "soff", 0))
        for B in Bs:
            ysrc, ytok = ysf(B)
            if ytok[0] == "ps":
                VE(lambda e, src=ysrc, B=B: e.tensor_copy(out=ar(B, 4 + so(B), n), in_=src), [ytok], [at(B, 4 + so(B))])
                ysrc, ytok = ar(B, 4 + so(B), n), at(B, 4 + so(B))
            Y[pk(B)] = (ysrc, ytok)

        def mm1(B, lhs, rhs, rtok, nm):
            pz, t = psalloc(B)
            t = t + (nm,)
            TE(lambda e, pz=pz: e.matmul(pz[:, 0:n], lhsT=lhs, rhs=rhs, start=True, stop=True), rtok, [t])
            P[pk(B)] = (pz, t)

        for B in Bs:
            AE(lambda e, B=B, ys=Y[pk(B)][0]: e.copy(out=arb(B, 6 + so(B), n), in_=ys), [Y[pk(B)][1]],
               [at(B, 6 + so(B))])
        for B in Bs:
            mm1(B, blk64[:], arb(B, 6 + so(B), n), ["blk64", at(B, 6 + so(B))], "m")
        for B in Bs:
            VE(lambda e, pz=P[pk(B)][0], ys=Y[pk(B)][0], B=B: e.tensor_tensor(out=ar(B, 5 + so(B), n), in0=ys, in1=pz[:, 0:n],
                                                                          op=ALU.subtract),
               [P[pk(B)][1], Y[pk(B)][1]], [at(B, 5 + so(B))])
        for B in Bs:
            AE(lambda e, B=B: e.activation(out=arb(B, 6 + so(B), n), in_=ar(B, 5 + so(B), n), func=AF.Square),
               [at(B, 5 + so(B)), at(B, 6 + so(B))], [at(B, 6 + so(B))])
        for B in Bs:
            mm1(B, blk64[:], arb(B, 6 + so(B), n), ["blk64", at(B, 6 + so(B))], "v")
        for B in Bs:
            AE(lambda e, pz=P[pk(B)][0], B=B: e.activation(out=ar(B, 6 + so(B), n), in_=pz[:, 0:n], func=AF.Ln,
                                                         bias=pcol(GNE_)), [P[pk(B)][1], "prm", at(B, 6 + so(B))], [at(B, 6 + so(B))])
        G = {}
        for B in Bs:
            G[pk(B)] = projx(B, 29 + jf(B), n)
        for B in Bs:
            AE(lambda e, B=B: e.activation(out=ar(B, 6 + so(B), n), in_=ar(B, 6 + so(B), n), func=AF.Exp, scale=-0.5),
               [at(B, 6 + so(B))], [at(B, 6 + so(B))])
        for B in Bs:
            AE(lambda e, pz=G[pk(B)][0], B=B: e.activation(out=ar(B, 4 + so(B), n), in_=pz[:, 0:n], func=AF.Silu),
               [G[pk(B)][1], at(B, 4 + so(B)), Y[pk(B)][1]], [at(B, 4 + so(B))])
        for B in Bs:
            VE(lambda e, B=B: e.tensor_tensor(out=ar(B, 7 + so(B), n), in0=ar(B, 5 + so(B), n), in1=ar(B, 6 + so(B), n), op=ALU.mult),
               [at(B, 5 + so(B)), at(B, 6 + so(B))], [at(B, 7 + so(B))])
        for B in Bs:
            j = jf(B)
            GE(lambda e, B=B, j=j: e.tensor_scalar(out=ar(B, 7 + so(B), n), in0=ar(B, 7 + so(B), n), scalar1=pcol(LW_ + j),
                                                   scalar2=pcol(LB_ + j), op0=ALU.mult, op1=ALU.add),
               [at(B, 7 + so(B)), "prm"], [at(B, 7 + so(B))])
        for B in Bs:
            GE(lambda e, B=B, bv_=bvf(B)[0]: e.tensor_tensor(out=ar(B, 7 + so(B), n), in0=ar(B, 7 + so(B), n), in1=bv_, op=ALU.add),
               [at(B, 7 + so(B)), bvf(B)[1]], [at(B, 7 + so(B))])
        for B in Bs:
            j = jf(B)
            VE(lambda e, B=B, j=j: e.tensor_tensor(out=ymixT[:, 4 + j, 0:n], in0=ar(B, 7 + so(B), n), in1=ar(B, 4 + so(B), n),
                                                   op=ALU.mult), [at(B, 7 + so(B)), at(B, 4 + so(B))], [("ym", 4 + j)])

    def sample_group():
        n = NS
        B = TH0
        A1 = TH1.arena
        SQs = [A1[:, 0:3, :].rearrange("p a b -> p (a b)").rearrange("p (q c) -> p q c", c=128),
               A1[:, 11:14, :].rearrange("p a b -> p (a b)").rearrange("p (q c) -> p q c", c=128)]
        yq = A1[:, 3:5, :].rearrange("p a b -> p (a b)")
        VQ = A1[:, 5:7, :].rearrange("p a b -> p (a b)")[:, 0:384].rearrange("p (q c) -> p q c", c=64)
        scT = A1[:, 7, 0:128].rearrange("p (j t b) -> p j t b", j=4, t=2)
        convo = A1[:, 7, 128:256].rearrange("p (j t b) -> p j t b", j=4, t=2)
        ssT = A1[:, 8, 0:208].rearrange("p (q b) -> p q b", b=NS)
        zso = A1[:, 9, 0:208].rearrange("p (q b) -> p q b", b=NS)
        bvS = A1[:, 10, 0:64].rearrange("p (j b) -> p j b", b=NS)
        sa = A1[:, 10, 64:128]
        yv = A1[:, 10, 128:192]
        DMA(lambda e: e.dma_start(out=scT, in_=scT_d), [], ["scT"], "d_c3")
        DMA(lambda e: e.dma_start(out=ssT, in_=ssT_d), [], ["ssT"], "d_c4")
        wdad_chunk(B, n, True, ssT, zso)
        S.barrier()
        SBs = []
        for k_ in range(4):
            Bk = TH()
            Bk.g = 10 + k_
            Bk.arena = TH0.arena[:, :, 64 * k_:64 * k_ + 64]
            Bk.ring = [[0, 1, 3, 4][k_]]
            Bk.rs = {"ri": 0}
            SBs.append(Bk)
        sj = lambda Bk: Bk.g - 10
        conv_branch(SBs, sj, n, True, scT, convo)
        DMA(lambda e: e.dma_start(out=convs_d, in_=convo),
            [("cvo", j) for j in range(4)] + [("cvo0", j) for j in range(4)], [], "d_o")
        slots = pair_pre(SBs, sj, n, True, lambda Bk: (bvS[:, sj(Bk), :], ("bvS", sj(Bk))), ssT, zso)
        for j in range(4):
            B = SBs[j]
            SQ = SQs[j % 2]
            for part, qs in ((0, slots[0:4]), (1, slots[4:6])):
                pzt, tk_ = psalloc(B)
                tt = tk_ + ("sq",)
                for qi, sl in enumerate(qs):
                    TE(lambda e, qi=qi, sl=sl, pzt=pzt, B=B: e.transpose(pzt[0:NS, qi * 128:(qi + 1) * 128],
                                                                          ar(B, sl, n), identf[:, :]),
                       [at(B, sl), "identf"], [tt])
                nq = len(qs)
                AE(lambda e, pzt=pzt, part=part, nq=nq, SQ=SQ: e.copy(
                    out=SQ[0:NS, part * 4:part * 4 + nq, :],
                    in_=pzt[0:NS, 0:nq * 128].rearrange("p (q c) -> p q c", c=128)), [tt], [("SQ", j % 2, part)])
            for h in range(2):
                DMA(lambda e, j=j, h=h, SQ=SQ: e.dma_start(out=scr1_d[:, 2 * j + h, :, :],
                                                           in_=SQ[0:NS, :, h * 64:(h + 1) * 64]),
                    [("SQ", j % 2, 0), ("SQ", j % 2, 1)], [("scr1", j, h)], ["d_s1", "d_s1b"][j % 2])
        DMA(lambda e: e.dma_start(out=shifts_d, in_=zso), [("zso", q) for q in range(13)], [], "d_o")
        DMA(lambda e: e.dma_start(out=VQ, in_=scr1_d.rearrange("b h q k -> (b h) q k")),
            [("scr1", j, h) for j in range(4) for h in range(2)], ["VQ"], "d_s2")
        B = TH0
        S.barrier()
        def bk(q):
            return VQ[:, q, :].unsqueeze(1).broadcast_to([128, 16, 64])

        def state_load(qt):
            sl0 = 8 * (qt % 2)
            S3 = B.arena[:, sl0:sl0 + 4, :].rearrange("p a b -> p (a b)").rearrange("p (v k) -> p v k", k=64)
            s3t = [at(B, k) for k in range(sl0, sl0 + 4)]
            DMA(lambda e: e.dma_start(out=S3, in_=swkv_d[:, qt * 1024:(qt + 1) * 1024]
                                      .rearrange("p (v k) -> p v k", k=64)), [], s3t, ["d_l", "d_l2"][qt % 2])

        def q_views(qt):
            sl0 = 8 * (qt % 2)
            S3 = B.arena[:, sl0:sl0 + 4, :].rearrange("p a b -> p (a b)").rearrange("p (v k) -> p v k", k=64)
            TM = B.arena[:, sl0 + 4:sl0 + 8, :].rearrange("p a b -> p (a b)").rearrange("p (v k) -> p v k", k=64)
            s3t = [at(B, k) for k in range(sl0, sl0 + 4)]
            tmt = [at(B, k) for k in range(sl0 + 4, sl0 + 8)]
            return S3, TM, s3t, tmt

        def state_a(qt):
            v0 = qt * 16
            S3, TM, s3t, tmt = q_views(qt)

            def bvv(ap2):
                return ap2[:, v0:v0 + 16].unsqueeze(2).broadcast_to([128, 16, 64])
            VE(lambda e: e.tensor_tensor(out=TM, in0=S3, in1=bk(4), op=ALU.mult), s3t + ["VQ"], tmt)
            VE(lambda e: e.tensor_reduce(out=sa[:, v0:v0 + 16], in_=TM, axis=AX.X, op=ALU.add, negate=True),
               tmt, [("sa", qt)])
            GE(lambda e: e.tensor_tensor(out=S3, in0=S3, in1=bk(1), op=ALU.mult), s3t + ["VQ"], s3t)
            VE(lambda e: e.tensor_tensor(out=TM, in0=bvv(sa), in1=bk(5), op=ALU.mult), [("sa", qt), "VQ"] + tmt, tmt)
            GE(lambda e: e.tensor_tensor(out=S3, in0=S3, in1=TM, op=ALU.add), s3t + tmt, s3t)

        def state_b(qt):
            v0 = qt * 16
            S3, TM, s3t, tmt = q_views(qt)
            wsm = ["d_ws", "d_ws2"][qt % 2]

            def bvv(ap2):
                return ap2[:, v0:v0 + 16].unsqueeze(2).broadcast_to([128, 16, 64])
            VE(lambda e: e.tensor_tensor(out=TM, in0=bvv(VQ[:, 3, :]), in1=bk(2), op=ALU.mult), ["VQ"] + tmt, tmt)
            GE(lambda e: e.tensor_tensor(out=S3, in0=S3, in1=TM, op=ALU.add), s3t + tmt, s3t)
            DMA(lambda e: e.dma_start(out=wkvs_d[:, qt * 1024:(qt + 1) * 1024]
                                      .rearrange("p (v k) -> p v k", k=64), in_=S3), s3t, [], wsm, q="scalar")
            VE(lambda e: e.tensor_tensor(out=TM, in0=S3, in1=bk(0), op=ALU.mult), s3t + ["VQ"] + tmt, tmt)
            VE(lambda e: e.tensor_reduce(out=yv[:, v0:v0 + 16], in_=TM, axis=AX.X, op=ALU.add), tmt, [("yv", qt)])

        p1g = p1_gen(THS, x_d, [0, 128], 128, [0, 128], 1, d7stage=True, carry="zero") if n_st > 0 else iter(())

        def p1step(k=2):
            for _ in range(k):
                next(p1g, None)
        state_load(0)
        state_load(1)
        state_a(0)
        p1step()
        state_a(1)
        p1step()
        state_b(0)
        state_load(2)
        p1step()
        state_b(1)
        state_load(3)
        p1step()
        state_a(2)
        p1step()
        state_a(3)
        p1step()
        state_b(2)
        state_b(3)
        for _ in p1g:
            pass
        DMA(lambda e: e.dma_start(out=scr2_d.rearrange("b h v -> (b h) v"), in_=yv), [("yv", q_) for q_ in range(4)],
            ["scr2"], "d_s3")
        DMA(lambda e: e.dma_start(out=yq[0:NS, :], in_=scr2_d.rearrange("b h v -> b (h v)")), ["scr2"], ["yq"], "d_s4")
        S.barrier()
        YP = {}
        for Bk in SBs:
            j = sj(Bk)
            pzt, tk_ = psalloc(Bk)
            tt = tk_ + ("yt",)
            YP[Bk.g] = (pzt, tt)
            TE(lambda e, j=j, pzt=pzt: e.transpose(pzt[:, 0:NS], yq[0:NS, j * 128:(j + 1) * 128], identf[0:NS, 0:NS]),
               ["yq", "identf"], [tt])
        post_pair(SBs, sj, n, lambda Bk: (YP[Bk.g][0][:, 0:NS], YP[Bk.g][1]),
                  lambda Bk: (bvS[:, sj(Bk), :], ("bvS", sj(Bk))))
        DMA(lambda e: e.dma_start(out=ym_d[:, :, SEQ:SEQ + NS], in_=ymixT[:, :, 0:NS]),
            [("ym", k) for k in range(8)], [], "d_ym")

    def bc4(ap2):
        return ap2.unsqueeze(1).broadcast_to([128, 4, 128])

    def halves_wkv(Bs, front=None):
        n = NT
        jb = lambda B, pp: 2 * B.g + pp
        PC4 = []
        for pp in range(2):
            for B in Bs:
                C = _copy.copy(B)
                C.ring, C.rs = B.ring4, B.rs4
                C.sb = 9 * pp
                C.ppi = pp
                PC4.append(C)
        FILL["k2"] = True
        pair_pre(PC4, lambda C: jb(C, C.ppi), n, False, lambda C: (C.bv[:, C.ppi, 0:n], tk(C, "bv", C.ppi)))
        FILL["ok"] = True
        flush_fill()
        FILL["ok"] = False
        FILL["k2"] = False

        def tkt(B, i):
            return [tk(B, "TK", i, 0), tk(B, "TK", i, 1)]

        def tk_round(i, rnd, qs):
            PT = {}
            for B in Bs:
                pt, tk_ = ptalloc(B)
                t = tk_ + ("tk",)
                PT[B.g] = (pt, t)
                for qi, q in enumerate(qs):
                    for pp in range(2):
                        if q == 0:
                            src, nm = B.KB[:, pp, 0, i * 128:(i + 1) * 128], "KB0"
                        elif q == 1:
                            src, nm = B.KB[:, pp, 1, i * 128:(i + 1) * 128], "KB1"
                        elif q == 2:
                            src, nm = B.VT[:, pp, i * 128:(i + 1) * 128], "VT"
                        else:
                            src, nm = B.KR[:, pp, 0, i * 128:(i + 1) * 128], "KR0"
                        TE(lambda e, src=src, pt=pt, qi=qi, pp=pp: e.transpose(
                            pt[:, qi * 256 + pp * 128:qi * 256 + (pp + 1) * 128], src, ident[:, :]),
                           [tk(B, nm, pp), "ident"], [t], sig=(qi == len(qs) - 1 and pp == 1))
            for B in Bs:
                pt, t = PT[B.g]
                AE(lambda e, pt=pt, B=B: e.copy(
                    out=B.TK[:, i, :, :].rearrange("p q c -> p (q c)"), in_=pt[:, :]),
                   [t], [tk(B, "TK", i, 0), tk(B, "TK", i, 1)])

        def score_head(i, hh):
            tsl = slice(i * 128, (i + 1) * 128)
            pp, h = hh // 2, hh % 2
            hs = slice(64 * h, 64 * h + 64)
            P = {}
            for B in Bs:
                ps, tk_ = psalloc(B.wide)
                t = tk_ + ("s",)
                P[B.g] = (ps, t)
                TE(lambda e, ps=ps, B=B: e.matmul(ps[:, 0:256], lhsT=B.KB[hs, pp, 0, tsl], rhs=B.KR[hs, pp, :, tsl],
                                                  start=True, stop=True),
                   [tk(B, "KB0", pp), tk(B, "KR0", pp), tk(B, "KR1", pp)], [t])
                TE(lambda e, ps=ps, B=B: e.matmul(ps[:, 256:512], lhsT=B.KB[hs, pp, 1, tsl],
                                                  rhs=B.KR[hs, pp, :, tsl], start=True, stop=True),
                   [tk(B, "KB1", pp), tk(B, "KR0", pp), tk(B, "KR1", pp)], [t])
            if hh % 2 == 0:
                for B in Bs:
                    ps, t = P[B.g]
                    VE(lambda e, ps=ps, B=B: e.tensor_tensor(
                        out=B.SC[:, i, hh, :].rearrange("p (a c) -> p a c", a=2),
                        in0=ps[:, :].rearrange("p (a c) -> p a c", a=2),
                        in1=maskA[:, :].unsqueeze(1).broadcast_to([128, 2, 256]), op=ALU.mult),
                       [t, "maskA"], [tk(B, "SC", i, hh)])
            else:
                for B in Bs:
                    ps, t = P[B.g]
                    AE(lambda e, ps=ps, B=B: e.copy(out=B.SC[:, i, hh, :], in_=ps[:, :]), [t], [tk(B, "SC", i, hh)])
                for B in Bs:
                    GE(lambda e, B=B: e.tensor_tensor(
                        out=B.SC[:, i, hh, :].rearrange("p (a c) -> p a c", a=2),
                        in0=B.SC[:, i, hh, :].rearrange("p (a c) -> p a c", a=2),
                        in1=maskA[:, :].unsqueeze(1).broadcast_to([128, 2, 256]), op=ALU.mult),
                       [tk(B, "SC", i, hh), "maskA"], [tk(B, "SC", i, hh)])

        def score_N(i, h):
            tsl = slice(i * 128, (i + 1) * 128)
            hs = slice(64 * h, 64 * h + 64)
            P = {}
            for B in Bs:
                ps, tk_ = psalloc(B.wide)
                t = tk_ + ("n",)
                P[B.g] = (ps, t)
                for pp in range(2):
                    TE(lambda e, ps=ps, pp=pp, B=B: e.matmul(ps[:, pp * 128:(pp + 1) * 128],
                                                             lhsT=B.KR[hs, pp, 0, tsl], rhs=B.KB[hs, pp, 1, tsl],
                                                             start=True, stop=True),
                       [tk(B, "KR0", pp), tk(B, "KB1", pp)], [t])
            for pp in range(2):
                for B in Bs:
                    ps, t = P[B.g]
                    VE(lambda e, ps=ps, pp=pp, B=B: e.tensor_tensor(out=B.NB[:, pp * 2 + h, :],
                                                                    in0=ps[:, pp * 128:(pp + 1) * 128],
                                                                    in1=maskN[:, :], op=ALU.mult),
                       [t, "maskN"], [tk(B, "NB", pp * 2 + h)])

        def sct(B, i):
            return [tk(B, "SC", i, hh) for hh in range(4)]

        def nbt(B):
            return [tk(B, "NB", hh) for hh in range(4)]

        def dbl_level(i, lv):
            cur, nxt = (lv - 1) % 2, lv % 2

            def Ap(B, hh):
                return B.SC[:, i, hh, 256:384] if lv == 1 else B.dA[cur][:, hh, :]

            def Bp(B, hh):
                return B.NB[:, hh, :] if lv == 1 else B.dB[cur][:, hh, :]

            def abt(B):
                return (sct(B, i) + nbt(B)) if lv == 1 else [tk(B, f"dA{cur}"), tk(B, f"dB{cur}")]
            PB, PA, PP_ = {}, {}, {}
            for B in Bs:
                ps, tk_ = psalloc(B.wide)
                t = tk_ + ("B",)
                PB[B.g] = (ps, t)
                for hh in range(4):
                    TE(lambda e, ps=ps, hh=hh, a=Ap(B, hh), b=Bp(B, hh): e.matmul(
                        ps[:, hh * 128:(hh + 1) * 128], lhsT=a, rhs=b, start=True, stop=True), abt(B), [t],
                       sig=(hh == 3))
            for B in Bs:
                ps, t = PB[B.g]
                AE(lambda e, ps=ps, B=B: e.copy(out=B.dB[nxt][:, :, :].rearrange("p a c -> p (a c)"), in_=ps[:, :]),
                   [t], [tk(B, f"dB{nxt}")])
            if lv <= 4:
                for B in Bs:
                    ps, tk_ = psalloc(B.wide)
                    t = tk_ + ("A",)
                    PA[B.g] = (ps, t)
                    for hh in range(4):
                        TE(lambda e, ps=ps, hh=hh, a=Ap(B, hh), b=Bp(B, hh): e.matmul(
                            ps[:, hh * 128:(hh + 1) * 128], lhsT=b, rhs=a, start=True, stop=True), abt(B), [t],
                           sig=(hh == 3))
                for B in Bs:
                    ps, t = PA[B.g]
                    AE(lambda e, ps=ps, B=B: e.copy(out=B.dA[nxt][:, :, :].rearrange("p a c -> p (a c)"),
                                                    in_=ps[:, :]), [t], [tk(B, f"dA{nxt}")])
            for B in Bs:
                ps, tk_ = psalloc(B.wide)
                t = tk_ + ("P",)
                PP_[B.g] = (ps, t)
                for hh in range(4):
                    TE(lambda e, ps=ps, hh=hh, B=B: e.matmul(ps[:, hh * 128:(hh + 1) * 128],
                                                             lhsT=B.dB[nxt][:, hh, :], rhs=B.dP[cur][:, hh, :],
                                                             start=True, stop=True),
                       [tk(B, f"dB{nxt}"), tk(B, f"dP{cur}")], [t], sig=(hh == 3))
            for B in Bs:
                ps, t = PP_[B.g]
                VE(lambda e, ps=ps, B=B: e.tensor_tensor(
                    out=B.dP[nxt][:, :, :].rearrange("p a c -> p (a c)"), in0=ps[:, :],
                    in1=B.dP[cur][:, :, :].rearrange("p a c -> p (a c)"), op=ALU.add),
                   [t, tk(B, f"dP{cur}")], [tk(B, f"dP{nxt}")])

        def mtt(B):
            return [tk(B, "dP1")]

        def prec(i):
            P = {}
            for B in Bs:
                ps, tk_ = psalloc(B.wide)
                t = tk_ + ("w1",)
                P[B.g] = (ps, t)
                for hh in range(4):
                    TE(lambda e, ps=ps, hh=hh, B=B: e.matmul(ps[:, hh * 64:(hh + 1) * 64], lhsT=B.SC[:, i, hh, 0:128],
                                                             rhs=B.TK[:, i, 2, hh * 64:(hh + 1) * 64],
                                                             start=True, stop=True),
                       [tk(B, "SC", i, hh)] + tkt(B, i), [t])
            for B in Bs:
                ps, t = P[B.g]
                AE(lambda e, ps=ps, B=B: e.copy(out=B.W1[:, :, :].rearrange("p a c -> p (a c)"), in_=ps[:, 0:256]),
                   [t], [tk(B, "W1")])
            for B in Bs:
                ps, tk_ = psalloc(B.wide)
                t = tk_ + ("u1",)
                P[B.g] = (ps, t)
                for hh in range(4):
                    TE(lambda e, ps=ps, hh=hh, B=B: e.matmul(ps[:, hh * 64:(hh + 1) * 64], lhsT=B.dP[1][:, hh, :],
                                                             rhs=B.W1[:, hh, :], start=True, stop=True),
                       mtt(B) + [tk(B, "W1")], [t])
            for B in Bs:
                ps, t = P[B.g]
                AE(lambda e, ps=ps, B=B: e.copy(out=B.U1b[:, i, :], in_=ps[:, 0:256]), [t], [tk(B, "U1b", i)])
            for B in Bs:
                ps, tk_ = psalloc(B.wide)
                t = tk_ + ("kp",)
                P[B.g] = (ps, t)
                for hh in range(4):
                    pp, h = hh // 2, hh % 2
                    TE(lambda e, ps=ps, hh=hh, pp=pp, h=h, B=B: e.matmul(
                        ps[64 * h:64 * h + 64, pp * 128:(pp + 1) * 128], lhsT=B.TK[:, i, 3, hh * 64:(hh + 1) * 64],
                        rhs=B.dP[1][:, hh, :], start=True, stop=True), mtt(B) + tkt(B, i), [t])
            for B in Bs:
                ps, t = P[B.g]
                VE(lambda e, ps=ps, B=B: e.tensor_copy(out=B.KPT[:, i, :, :].rearrange("p a c -> p (a c)"),
                                                       in_=ps[:, 0:256]), [t], [tk(B, "KPT", i)])
            for B in Bs:
                ps, tk_ = psalloc(B.wide)
                t = tk_ + ("y1",)
                P[B.g] = (ps, t)
                for hh in range(4):
                    pp, h = hh // 2, hh % 2
                    TE(lambda e, ps=ps, hh=hh, pp=pp, h=h, B=B: e.matmul(
                        ps[64 * h:64 * h + 64, pp * 128:(pp + 1) * 128], lhsT=B.TK[:, i, 2, hh * 64:(hh + 1) * 64],
                        rhs=B.SC[:, i, hh, 128:256], start=True, stop=False), tkt(B, i) + [tk(B, "SC", i, hh)], [t])
                    TE(lambda e, ps=ps, hh=hh, pp=pp, h=h, B=B: e.matmul(
                        ps[64 * h:64 * h + 64, pp * 128:(pp + 1) * 128], lhsT=B.U1b[:, i, hh * 64:(hh + 1) * 64],
                        rhs=B.SC[:, i, hh, 384:512], start=False, stop=True),
                       [tk(B, "U1b", i), tk(B, "SC", i, hh)], [t])
            for B in Bs:
                ps, t = P[B.g]
                AE(lambda e, ps=ps, B=B: e.copy(out=B.Y1T[:, i, :, :].rearrange("p a c -> p (a c)"), in_=ps[:, 0:256]),
                   [t], [tk(B, "Y1T", i)])

        def h1_part(i, ee):
            es_ = slice(64 * ee, 64 * ee + 64)
            s0_ = i * 4 + ee * 2
            P = {}
            for B in Bs:
                ps, tk_ = psalloc(B.wide)
                t = tk_ + ("h1",)
                P[B.g] = (ps, t)
                for pp in range(2):
                    cs = slice(pp * 128, (pp + 1) * 128)
                    TE(lambda e, ps=ps, cs=cs, B=B: e.matmul(ps[:, cs], lhsT=B.TK[es_, i, 0, cs],
                                                             rhs=B.TK[es_, i, 2, cs], start=True, stop=False),
                       tkt(B, i), [t])
                    TE(lambda e, ps=ps, cs=cs, B=B: e.matmul(ps[:, cs], lhsT=B.TK[es_, i, 1, cs],
                                                             rhs=B.U1b[es_, i, cs], start=False, stop=True),
                       tkt(B, i) + [tk(B, "U1b", i)], [t])
            for B in Bs:
                ps, t = P[B.g]
                AE(lambda e, ps=ps, B=B: e.copy(out=B.H1[:, s0_:s0_ + 2, :].rearrange("p a c -> p (a c)"),
                                                in_=ps[:, 0:256]), [t],
                   [tk(B, "H1", s0_), tk(B, "H1", s0_ + 1)])

        def Tt(B):
            return [("T", 2 * B.g), ("T", 2 * B.g + 1)]

        def chain_chunk(i, ee, PY):
            es_ = slice(64 * ee, 64 * ee + 64)
            ch = i * 2 + ee
            PU, PD = {}, {}
            fill()
            for B in Bs:
                psU, tk_ = psalloc(B.chain)
                tU = tk_ + ("U",)
                PU[B.g] = (psU, tU)
                for pp in range(2):
                    TE(lambda e, psU=psU, pp=pp, B=B: e.matmul(psU[:, pp * 128:(pp + 1) * 128],
                                                               lhsT=B.KPT[:, i, pp, :], rhs=T16[:, 2 * B.g + pp, :],
                                                               start=True, stop=True),
                       [tk(B, "KPT", i), ("T16", 2 * B.g + pp)], [tU])
            fill()
            for B in Bs:
                psU, tU = PU[B.g]
                VE(lambda e, psU=psU, B=B: e.tensor_copy(out=B.U0b[es_, :], in_=psU[es_, 0:256]), [tU],
                   [tk(B, "U0b")])
            fill()
            for pp in range(2):
                slot = i * 4 + ee * 2 + pp
                for B in Bs:
                    GE(lambda e, pp=pp, B=B, slot=slot: e.tensor_tensor(out=B.Tsum[:, pp, :],
                                                                        in0=T32[:, 2 * B.g + pp, :],
                                                                        in1=B.H1[:, slot, :], op=ALU.add),
                       [("T", 2 * B.g + pp), tk(B, "H1", slot)], [tk(B, "Ts", pp)])
            fill()
            for B in Bs:
                psD, tk_ = psalloc(B.chain)
                tD = tk_ + ("D",)
                PD[B.g] = (psD, tD)
                for pp in range(2):
                    cs = slice(pp * 128, (pp + 1) * 128)
                    TE(lambda e, psD=psD, cs=cs, B=B: e.matmul(psD[:, cs], lhsT=B.TK[es_, i, 1, cs],
                                                               rhs=B.U0b[es_, cs], start=True, stop=True),
                       tkt(B, i) + [tk(B, "U0b")], [tD])
            fill()
            for B in Bs:
                psY, tY = PY[B.g]
                for pp in range(2):
                    ycs = slice(pp * 128 + 64 * ee, pp * 128 + 64 * ee + 64)
                    TE(lambda e, psY=psY, pp=pp, ycs=ycs, B=B: e.matmul(
                        psY[:, ycs], lhsT=T16[:, 2 * B.g + pp, :],
                        rhs=B.KR[:, pp, 1, i * 128 + 64 * ee:i * 128 + 64 * ee + 64], start=True, stop=False),
                       [("T16", 2 * B.g + pp), tk(B, "KR1", pp)], [tY])
                    for h in range(2):
                        hh = pp * 2 + h
                        TE(lambda e, psY=psY, ycs=ycs, h=h, hh=hh, B=B: e.matmul(
                            psY[64 * h:64 * h + 64, ycs], lhsT=B.U0b[es_, hh * 64:(hh + 1) * 64],
                            rhs=B.SC[es_, i, hh, 384 + 64 * ee:384 + 64 * ee + 64], start=False, stop=True),
                           [tk(B, "U0b"), tk(B, "SC", i, hh)], [tY])
            fill()
            for h in range(2):
                hs = slice(64 * h, 64 * h + 64)
                for B in Bs:
                    psD, tD = PD[B.g]
                    VE(lambda e, psD=psD, B=B, hs=hs: e.tensor_tensor(
                        out=B.Tsum[hs, :, hs], in0=psD[hs, 0:256].rearrange("p (a c) -> p a c", a=2)[:, :, hs],
                        in1=B.Tsum[hs, :, hs], op=ALU.add), [tD, tk(B, "Ts", 0), tk(B, "Ts", 1)],
                       [tk(B, "Ts", 0), tk(B, "Ts", 1)])
                for B in Bs:
                    VE(lambda e, B=B, hs=hs: e.tensor_tensor(
                        out=T16[hs, 2 * B.g:2 * B.g + 2, hs], in0=B.Tsum[hs, :, hs],
                        in1=pcs[hs, 2 * B.g:2 * B.g + 2, ch:ch + 1].broadcast_to([64, 2, 64]), op=ALU.mult),
                       [tk(B, "Ts", 0), tk(B, "Ts", 1), ("pcs", 2 * B.g), ("pcs", 2 * B.g + 1)],
                       [("T16", 2 * B.g), ("T16", 2 * B.g + 1)])
                for B in Bs:
                    GE(lambda e, B=B, hs=hs: e.tensor_tensor(
                        out=T32[hs, 2 * B.g:2 * B.g + 2, hs], in0=B.Tsum[hs, :, hs],
                        in1=pcs[hs, 2 * B.g:2 * B.g + 2, ch:ch + 1].broadcast_to([64, 2, 64]), op=ALU.mult),
                       [tk(B, "Ts", 0), tk(B, "Ts", 1), ("pcs", 2 * B.g), ("pcs", 2 * B.g + 1)], Tt(B))

        def chain_tile(i):
            PY = {}
            for B in Bs:
                psY, tk_ = psy(B)
                PY[B.g] = (psY, tk_ + ("Y",))
            for ee in range(2):
                chain_chunk(i, ee, PY)
            for B in Bs:
                psY, tY = PY[B.g]
                VE(lambda e, psY=psY, B=B: e.tensor_tensor(
                    out=B.arena[:, 12:14, i * 128:(i + 1) * 128], in0=psY[:, 0:256].rearrange("p (a c) -> p a c", a=2),
                    in1=B.Y1T[:, i, :, :], op=ALU.add), [tY, tk(B, "Y1T", i)], [at(B, 12), at(B, 13)])

        for i in range(2):
            tk_round(i, 0, (0, 1, 2, 3))
        for i in range(2):
            for hh in range(4):
                score_head(i, hh)
            for h in range(2):
                score_N(i, h)
            for B in Bs:
                VE(lambda e, B=B, i=i: e.tensor_tensor(out=B.dP[0][:, :, :], in0=B.SC[:, i, :, 256:384],
                                                       in1=bc4(ident[:, :]), op=ALU.add),
                   sct(B, i) + ["ident"], [tk(B, "dP0")])
            for lv in range(1, 6):
                dbl_level(i, lv)
            prec(i)
            for ee in range(2):
                h1_part(i, ee)
        if front is not None:
            FILL["gen"] = front
        FILL["ok"] = True
        for i in range(2):
            chain_tile(i)
        flush_fill()
        FILL["ok"] = False
        PB4 = []
        for pp in range(2):
            for B in Bs:
                Bq = _copy.copy(B.wide)
                Bq.soff = 4 * pp
                Bq.ppi = pp
                PB4.append(Bq)
        post_pair(PB4, lambda B: jb(B, B.ppi), n, lambda B: (ar(B, 12 + B.ppi, n), at(B, 12 + B.ppi)),
                  lambda B: (B.bv[:, B.ppi, 0:n], tk(B, "bv", B.ppi)))

    if do_sample:
        sample_group()
        S.barrier()
    xpar = lambda st: (st + 1) % 2

    def front_gen(st1):
        yield from p1_gen(THS, x_d, [st1 * NT, st1 * NT + 128], 128, [0, 128], xpar(st1), d7stage=True, carry="prev")
        wdad_chunk(WC, NT, False, par=xpar(st1), save_last=(st1 == n_st - 1), defer=True)
        yield

    if n_st > 0:
        if not do_sample:
            load_norm_T(THS, x_d, [0, 128], 128, [0, 128], xpar(0), carry="zero")
        wdad_chunk(TH0, NT, False, par=xpar(0), save_last=(n_st == 1))
    for st in range(n_st):
        CUR["x"] = xpar(st)
        LASTF["v"] = (st == n_st - 1)
        if st > 0:
            thad_ops(TH0, NT)

        def conv_all():
            for jj in range(2):
                yield from conv_gen(CTH, lambda B, jj=jj: 2 * B.g + jj, NT, False, slots=(18, 18, 19))
        FILL["gen"] = conv_all()
        halves_wkv(THS, front_gen(st + 1) if st + 1 < n_st else None)
        DMA(lambda e, st=st: e.dma_start(out=ym_d[:, :, st * NT:(st + 1) * NT], in_=ymixT[:, :, :]),
            [("ym", k) for k in range(8)], [], "d_ym")
    DMA(lambda e: e.dma_start(out=convp_d, in_=uext[:, :, 0:2]), [("u", j) for j in range(4)], [], "d_o")
    DMA(lambda e: e.dma_start(out=shiftp_d, in_=zlast[:, :]), [("zl", q) for q in range(13)], [], "d_o")
    for j in range(4):
        for h in range(2):
            hs = slice(64 * h, 64 * h + 64)
            DMA(lambda e, j=j, h=h, hs=hs: e.dma_start(out=wkvp_d[2 * j + h, :, :], in_=T32[hs, j, hs]),
                [("T", j)], [], "d_o")
    S.barrier()

    wflat = win[:].rearrange("p c d -> p (c d)")
    wout = wflat[:, 0:8192].rearrange("p (c d) -> p c d", c=8)
    wpg = wflat[:, 8192:16384].rearrange("p (c d) -> p c d", c=8)
    wpp = wflat[:, 16384:18432].rearrange("p (c d) -> p c d", c=2)
    gbc = wflat[:, 29696:31744].bitcast(F32)
    stgB = [(xnTf[0][:, 0:2048].bitcast(F32), "stgB0"),
            (ymixT[:].rearrange("p a b -> p (a b)").bitcast(F32), "stgB1"),
            (uext[:].rearrange("p a b -> p (a b)")[:, 0:1024], "stgB2")]
    chunksB = []
    for c in range(8):
        chunksB.append((wout[:, c, :], wout_d[c * 128:(c + 1) * 128, :], "wout"))
    for c in range(8):
        chunksB.append((wpg[:, c, :], wpg_d[c * 128:(c + 1) * 128, :], "wpg"))
    for c in range(2):
        chunksB.append((wpp[:, c, :], wpp_d[c * 128:(c + 1) * 128, :], "wpp"))
    load_cast(chunksB, stgB, "wB")
    DMA(lambda e: e.dma_start(out=gbc, in_=fg_d.partition_broadcast(128)), [], ["gbc"], "d_c2")

    class TB:
        pass

    OT = []
    for t_ in range(4):
        O = TB()
        O.g = t_
        O.pbank = t_ % 2
        if t_ < 2:
            base = 18432 + t_ * 5632
            O.h1b = wflat[:, base:base + 1024]
            O.h1T = wflat[:, base + 1024:base + 2048].rearrange("p (c t) -> p c t", c=8)
            O.sigb = wflat[:, base + 2048:base + 3072]
            O.ptb = wflat[:, base + 3072:base + 3328]
            O.pT = wflat[:, base + 3328:base + 3584].rearrange("p (c t) -> p c t", c=2)
            O.ymi = wflat[:, base + 3584:base + 4608].rearrange("p (c t) -> p c t", c=8)
            A = THS[t_].arena
            O.h1 = A[:, 0:4, :].rearrange("p a b -> p (a b)")
            O.gg = A[:, 4:8, :].rearrange("p a b -> p (a b)")
            O.pt32 = A[:, 8, :]
            O.ring = [t_, 4 + t_]
        else:
            Bx = THS[t_ - 2]
            scf = Bx.SC[:].rearrange("p a b c -> p (a b c)")
            tkf = Bx.TK[:].rearrange("p a b c -> p (a b c)")
            O.h1b = scf[:, 0:1024]
            O.h1T = scf[:, 1024:2048].rearrange("p (c t) -> p c t", c=8)
            O.sigb = scf[:, 2048:3072]
            O.ymi = scf[:, 3072:4096].rearrange("p (c t) -> p c t", c=8)
            O.ptb = tkf[:, 0:256]
            O.pT = tkf[:, 256:512].rearrange("p (c t) -> p c t", c=2)
            O.h1 = Bx.H1[:].rearrange("p a b -> p (a b)")
            O.gg = Bx.arena[:, 9:13, :].rearrange("p a b -> p (a b)")
            O.pt32 = Bx.arena[:, 13, :]
            O.ring = [t_]
        if t_ < 2:
            O.yo = THS[t_].D7[:].rearrange("p a b c -> p (a b c)").bitcast(F32)[:, 0:1024]
            O.yot = ("o", t_, "yo")
        elif t_ == 2:
            O.yo = xnTf[1][:, 0:2048].bitcast(F32)
            O.yot = ("o", t_, "yo")
        else:
            O.yo = uext[:].rearrange("p a b -> p (a b)")[:, 0:1024]
            O.yot = "stgB2"
        O.rs = {"ri": 0}
        O.gring = True
        OT.append(O)

    def ot(O, *nm):
        return ("o", O.g) + nm

    def dsem(p, O):
        if p == "d_m":
            return f"d_m{O.g}"
        return f"{p}{O.g}" if O.g else p

    def o_loads(Os, specs):
        for O, (xsrc, psrc, ydst, tok0, rows, ymcol) in zip(Os, specs):
            DMA(lambda e, O=O, xsrc=xsrc, tok0=tok0, rows=rows: e.dma_start(out=O.gg[0:rows, :],
                                                                            in_=xsrc[tok0:tok0 + rows, :]),
                [], [ot(O, "g")], dsem("d_g", O))
            DMA(lambda e, O=O, psrc=psrc, tok0=tok0, rows=rows: e.dma_start(out=O.pt32[0:rows, :],
                                                                            in_=psrc[tok0:tok0 + rows, :]),
                [], [ot(O, "pt32")], dsem("d_p", O))
            DMA(lambda e, O=O, ymcol=ymcol, rows=rows: e.dma_start(out=O.ymi[:, :, 0:rows],
                                                                   in_=ym_d[:, :, ymcol:ymcol + rows]),
                [], [ot(O, "ymi")], dsem("d_m", O))

    def o_compute(Os, specs, after=None):
        for O, sp in zip(Os, specs):
            rows = sp[4]
            GE(lambda e, O=O, rows=rows: e.tensor_copy(out=O.ptb[0:rows, :], in_=O.pt32[0:rows, :]),
               [ot(O, "pt32")], [ot(O, "ptb")])
        for hf in range(2):
            P = {}
            for O, sp in zip(Os, specs):
                rows = sp[4]
                po, tk_ = psalloc(O)
                t = tk_ + ("o",)
                P[O.g] = (po, t)
                for c in range(8):
                    TE(lambda e, hf=hf, c=c, po=po, O=O, rows=rows: e.matmul(po[0:rows, :], lhsT=O.ymi[:, c, 0:rows],
                                                                      rhs=wout[:, c, hf * 512:(hf + 1) * 512],
                                                                      start=(c == 0), stop=(c == 7)),
                       [ot(O, "ymi"), "wout"], [t], sig=(c == 7))
            for O, sp in zip(Os, specs):
                rows = sp[4]
                po, t = P[O.g]
                VE(lambda e, hf=hf, po=po, O=O, rows=rows: e.tensor_tensor(out=O.h1[0:rows, hf * 512:(hf + 1) * 512],
                                                                    in0=po[0:rows, :],
                                                                    in1=O.gg[0:rows, hf * 512:(hf + 1) * 512],
                                                                    op=ALU.add), [t, ot(O, "g")], [ot(O, "h1", hf)])
        h1t = lambda O: [ot(O, "h1", 0), ot(O, "h1", 1)]
        for O, sp in zip(Os, specs):
            rows = sp[4]
            AE(lambda e, hf=hf, O=O, rows=rows: e.copy(out=O.h1b[0:rows, :], in_=O.h1[0:rows, :]), h1t(O), [ot(O, "h1b")])
        for O, sp in zip(Os, specs):
            rows = sp[4]
            pt, tk_ = ptalloc(O)
            t = tk_ + ("h",)
            for c in range(8):
                TE(lambda e, c=c, pt=pt, O=O, rows=rows: e.transpose(
                    pt[:, c * 128:c * 128 + rows], O.h1b[0:rows, c * 128:(c + 1) * 128], ident[0:rows, 0:rows]),
                   [ot(O, "h1b"), "ident"], [t], sig=(c == 7))
            AE(lambda e, pt=pt, O=O, rows=rows: e.copy(
                out=O.h1T[:, :, 0:rows],
                in_=pt[:, :].rearrange("p (c t) -> p c t", t=128)[:, :, 0:rows]), [t],
               [ot(O, "h1T", 0), ot(O, "h1T", 1)])
        for O, sp in zip(Os, specs):
            rows = sp[4]
            pt, tk_ = ptalloc(O)
            t = tk_ + ("p",)
            for c in range(2):
                TE(lambda e, c=c, pt=pt, O=O, rows=rows: e.transpose(pt[:, c * 128:c * 128 + rows],
                                                                     O.ptb[0:rows, c * 128:(c + 1) * 128],
                                                                     ident[0:rows, 0:rows]), [ot(O, "ptb"), "ident"], [t])
            VE(lambda e, pt=pt, O=O, rows=rows: e.tensor_copy(
                out=O.pT[:, :, 0:rows], in_=pt[:, 0:256].rearrange("p (c t) -> p c t", t=128)[:, :, 0:rows]),
               [t], [ot(O, "pT")])
        for hf in range(2):
            PG, PQ = {}, {}
            for O, sp in zip(Os, specs):
                rows = sp[4]
                pg, tk_ = psalloc(O)
                tg = tk_ + ("g",)
                PG[O.g] = (pg, tg)
                for c in range(8):
                    TE(lambda e, hf=hf, c=c, pg=pg, O=O, rows=rows: e.matmul(pg[0:rows, :], lhsT=O.h1T[:, c, 0:rows],
                                                                      rhs=wpg[:, c, hf * 512:(hf + 1) * 512],
                                                                      start=(c == 0), stop=(c == 7)),
                       [ot(O, "h1T", 0), ot(O, "h1T", 1), "wpg"], [tg], sig=(c == 7))
            for O, sp in zip(Os, specs):
                rows = sp[4]
                pg, tg = PG[O.g]
                AE(lambda e, hf=hf, pg=pg, O=O, rows=rows: e.activation(out=O.sigb[0:rows, hf * 512:(hf + 1) * 512],
                                                                 in_=pg[0:rows, :], func=AF.Sigmoid),
                   [tg], [ot(O, "sig", hf)])
            for O, sp in zip(Os, specs):
                rows = sp[4]
                pq, tk_ = psalloc(O)
                tq = tk_ + ("q",)
                PQ[O.g] = (pq, tq)
                for c in range(2):
                    TE(lambda e, hf=hf, c=c, pq=pq, O=O, rows=rows: e.matmul(pq[0:rows, :], lhsT=O.pT[:, c, 0:rows],
                                                                      rhs=wpp[:, c, hf * 512:(hf + 1) * 512],
                                                                      start=(c == 0), stop=(c == 1)),
                       [ot(O, "pT"), "wpp"], [tq], sig=(c == 1))
            for O, sp in zip(Os, specs):
                rows = sp[4]
                pq, tq = PQ[O.g]
                VE(lambda e, hf=hf, pq=pq, O=O, rows=rows: e.tensor_tensor(out=O.gg[0:rows, hf * 512:(hf + 1) * 512],
                                                                    in0=O.sigb[0:rows, hf * 512:(hf + 1) * 512],
                                                                    in1=pq[0:rows, :], op=ALU.mult),
                   [tq, ot(O, "sig", hf), ot(O, "g")], [ot(O, "g")])
        for O, sp in zip(Os, specs):
            rows = sp[4]
            GE(lambda e, hf=hf, O=O, rows=rows: e.tensor_tensor(out=O.h1[0:rows, :], in0=O.h1[0:rows, :], in1=O.gg[0:rows, :],
                                                         op=ALU.add), h1t(O) + [ot(O, "g")], h1t(O))
        if after is not None:
            after()
        for O, sp in zip(Os, specs):
            rows = sp[4]
            so = 8 * O.g
            AE(lambda e, hf=hf, O=O, rows=rows, so=so: e.activation(out=O.sigb[0:rows, :], in_=O.h1[0:rows, :], func=AF.Square,
                                                             accum_out=stat[0:rows, so + 4:so + 5]),
               h1t(O) + [ot(O, "sig", 0), ot(O, "sig", 1)], [ot(O, "sig", 0), ot(O, "sig", 1), ot(O, "st4")])
        for O, sp in zip(Os, specs):
            rows = sp[4]
            so = 8 * O.g
            AE(lambda e, hf=hf, rows=rows, so=so: e.activation(out=stat[0:rows, so + 5:so + 6], in_=stat[0:rows, so + 4:so + 5],
                                                        func=AF.Ln, scale=1.0 / D, bias=prm[0:rows, RME_:RME_ + 1]),
               [ot(O, "st4"), "prm"], [ot(O, "st5")])
        for O, sp in zip(Os, specs):
            rows = sp[4]
            so = 8 * O.g
            AE(lambda e, rows=rows, so=so: e.activation(out=stat[0:rows, so + 6:so + 7], in_=stat[0:rows, so + 5:so + 6],
                                                        func=AF.Exp, scale=-0.5), [ot(O, "st5")], [ot(O, "st6")])
        for O, sp in zip(Os, specs):
            rows = sp[4]
            so = 8 * O.g
            VE(lambda e, hf=hf, O=O, rows=rows, so=so: e.scalar_tensor_tensor(out=O.yo[0:rows, :], in0=O.h1[0:rows, :],
                                                                       scalar=stat[0:rows, so + 6:so + 7],
                                                                       in1=gbc[0:rows, :], op0=ALU.mult, op1=ALU.mult),
               h1t(O) + [ot(O, "st6"), "gbc"], [O.yot])
        for O, (xsrc, psrc, ydst, tok0, rows, ymcol) in zip(Os, specs):
            DMA(lambda e, hf=hf, O=O, ydst=ydst, tok0=tok0, rows=rows: e.dma_start(out=ydst[tok0:tok0 + rows, :],
                                                                            in_=O.yo[0:rows, :]),
                [O.yot], [], dsem("d_y", O))

    groups = []
    if do_sample:
        groups.append(([OT[0]], [(xs_d, psm_d, ys_d, 0, NS, SEQ)]))
    ntile = n_st * 2
    for tt in range(0, ntile, 4):
        k_ = min(4, ntile - tt)
        groups.append((OT[:k_], [(x_d, p_d, y_d, (tt + k) * 128, 128, (tt + k) * 128) for k in range(k_)]))
    if groups:
        o_loads(*groups[0])
    for gi, (Os_, sp_) in enumerate(groups):
        nxt = (lambda gi=gi: o_loads(*groups[gi + 1])) if gi + 1 < len(groups) else None
        o_compute(Os_, sp_, nxt)
    S.barrier()

    with nc.allow_low_precision("bf16 matmul operands, fp32 accumulation"):
        with nc.allow_non_contiguous_dma("small strided state DMAs"):
            with nc.Block() as block:
                @block.tensor
                def _(e): S.replay("tensor", e, sems)
                @block.vector
                def _(e): S.replay("vector", e, sems)
                @block.scalar
                def _(e): S.replay("scalar", e, sems)
                @block.gpsimd
                def _(e): S.replay("gpsimd", e, sems)
                @block.sync
                def _(e): S.replay("sync", e, sems)
    es.close()
    return nc


_NC_CACHE = {}


def make_in_maps(inputs):
    f = lambda a: np.ascontiguousarray(np.asarray(a, dtype=np.float32))
    g = {k: f(v) for k, v in inputs.items()}
    rows = [g["norm_g"][0].reshape(8, 128), g["mu_shift"][0].reshape(13, 128), g["conv_w"][0].reshape(12, 128),
            g["w0"][0].reshape(4, 128), g["a0"][0].reshape(4, 128), g["k_k"][0].reshape(4, 128),
            g["k_a"][0].reshape(4, 128), g["r_k"][0].reshape(4, 128), g["ln_w"][0].reshape(4, 128),
            g["ln_b"][0].reshape(4, 128)]
    prm = f(np.concatenate(rows, axis=0).T)
    shared = {"prm": prm, "w_in": g["w_in"][0], "w_up": g["w_up"][0], "a_up": g["a_up"][0], "w_out": g["w_out"][0],
              "w_pg": g["w_pg"][0], "w_pp": g["w_pp"][0], "final_g": g["final_g"].reshape(1, D)}
    maps = []
    for c in range(NCORES):
        sl = slice(NS * c, NS * (c + 1))
        sc = g["state_conv"][0, sl]
        scT = f(sc.reshape(NS, 2, 4, 128).transpose(3, 2, 1, 0))
        ss = g["state_shift"][0, sl]
        ssT = f(ss.reshape(NS, 13, 128).transpose(2, 1, 0))
        m = dict(shared)
        m.update({"x": g["x_prompt"][c], "p": g["p_prompt"][0, c], "xs": g["x_sample"][sl, 0],
                  "psm": g["p_sample"][0, sl, 0], "scT": scT, "ssT": ssT,
                  "swkv": f(g["state_wkv"][0, sl].reshape(NS * 8, 4096))})
        maps.append(m)
    return maps


def assemble(results):
    y = np.stack([r["y"] for r in results], 0).astype(np.float32)
    ys = np.concatenate([r["ys"] for r in results], 0).reshape(NCORES * NS, 1, D).astype(np.float32)
    convp = np.stack([r["convp"].transpose(2, 1, 0).reshape(2, 512) for r in results], 0)[None].astype(np.float32)
    shiftp = np.stack([r["shiftp"].T.reshape(1664) for r in results], 0)[None].astype(np.float32)
    wkvp = np.stack([r["wkvp"].transpose(0, 2, 1) for r in results], 0)[None].astype(np.float32)
    convs = np.concatenate([r["convs"].transpose(3, 2, 1, 0).reshape(NS, 2, 512) for r in results], 0)[None]
    shifts = np.concatenate([r["shifts"].transpose(2, 1, 0).reshape(NS, 1664) for r in results], 0)[None]
    wkvs = np.concatenate([r["wkvs"].reshape(NS, 8, 64, 64) for r in results], 0)[None]
    return (y, ys, np.ascontiguousarray(convp), np.ascontiguousarray(shiftp), np.ascontiguousarray(wkvp),
            np.ascontiguousarray(convs.astype(np.float32)), np.ascontiguousarray(shifts.astype(np.float32)),
            np.ascontiguousarray(wkvs.astype(np.float32)))


def kernel(**inputs):
    if "nc" not in _NC_CACHE:
        _NC_CACHE["nc"] = build_nc()
    nc = _NC_CACHE["nc"]
    maps = make_in_maps(inputs)
    res = run_bass_kernel_spmd(nc, maps, core_ids=list(range(NCORES)))
    return assemble(res.results)
```
